# Optimizing a Trainium2 kernel written in Bass

```python
import math
import jax, jax.numpy as jnp
from jax import lax
import numpy as np

D_MODEL = 1024
BATCH = 8
SEQ = 4096
DEPTH = 2

CTX_LEN = 256
GRID_W = 64
EPS = 1e-6

NA_HEADS = 8
NA_HEAD_DIM = 64
NA_W = NA_HEADS * NA_HEAD_DIM
NA_WIN_ROWS = 8
NA_WIN_COLS = 16
NA_QBLK_COLS = 16
NA_KBLK_COLS = 32

SSM_HEADS = 16
SSM_HEAD_DIM = 64
SSM_D_INNER = SSM_HEADS * SSM_HEAD_DIM
SSM_GROUPS = 2
SSM_D_STATE = 64
SSM_BC = SSM_GROUPS * SSM_D_STATE
SSM_XBC = SSM_D_INNER + 2 * SSM_BC
SSM_CONV = 3
SSM_CHUNK = 128

GQA_HEADS = 8
GQA_KV_HEADS = 2
GQA_HEAD_DIM = 64
GQA_Q = GQA_HEADS * GQA_HEAD_DIM
GQA_KV = GQA_KV_HEADS * GQA_HEAD_DIM
ROPE_THETA = 10000.0
Q_BLOCK = 128

FFN_HIDDEN = 2816
FFN_CONV = 3

N_BRANCHES = 3
IN_SIZES = (3 * NA_W, SSM_D_INNER, SSM_XBC, 2 * SSM_HEADS, GQA_Q, GQA_KV, GQA_KV, N_BRANCHES * D_MODEL)
IN_SPLIT_IDX = tuple(int(i) for i in np.cumsum(IN_SIZES)[:-1])
D_IN_PROJ = sum(IN_SIZES)

kernel_name = 'hybrid_na_ssd_gqa_prefix_dit'

F32 = jnp.float32


def rms_norm(x, w):
    xf = x.astype(F32)
    y = xf * lax.rsqrt(jnp.mean(xf * xf, axis=-1, keepdims=True) + EPS)
    return (y * w.astype(F32)).astype(x.dtype)


def modulate(h, shift, scale):
    return h * (1 + scale) + shift


def heads(t, n_heads):
    return t.reshape(t.shape[0], t.shape[1], n_heads, -1)


def depthwise_conv(x, w, b):
    k = w.shape[0]
    pad = k // 2
    y = lax.conv_general_dilated(x, w[:, None, :], window_strides=(1,), padding=[(pad, pad)],
                                 dimension_numbers=('NWC', 'WIO', 'NWC'), feature_group_count=x.shape[-1])
    return y + b


def axial_rope_tables(n_tok):
    n_freq = GQA_HEAD_DIM // 4
    inv_freq = ROPE_THETA ** (-jnp.arange(n_freq, dtype=F32) / n_freq)
    t = jnp.arange(n_tok, dtype=jnp.int32)
    row = (t // GRID_W).astype(F32)
    col = (t % GRID_W).astype(F32)
    ang = jnp.concatenate([row[:, None] * inv_freq, col[:, None] * inv_freq], axis=-1)
    return jnp.cos(ang), jnp.sin(ang)


def apply_rope(x, cos, sin):
    xf = x.astype(F32)
    x1, x2 = xf[..., 0::2], xf[..., 1::2]
    cs, sn = cos[:, None, :], sin[:, None, :]
    out = jnp.stack([x1 * cs - x2 * sn, x1 * sn + x2 * cs], axis=-1).reshape(x.shape)
    return out.astype(x.dtype)


def context_attention(q, k, v):
    bsz, n, hq, hd = q.shape
    hkv = k.shape[2]
    qg = (q * hd ** -0.5).reshape(bsz, n, hkv, hq // hkv, hd)
    s = jnp.einsum('bqkgd,bskd->bkgqs', qg, k).astype(F32)
    p = jax.nn.softmax(s, axis=-1).astype(v.dtype)
    return jnp.einsum('bkgqs,bskd->bqkgd', p, v).reshape(bsz, n, hq * hd)


def neighbourhood_attention(q, k, v, k_ctx, v_ctx, rel_bias, n_rows):
    bsz, n_tok, nh, hd = q.shape
    kr = min(NA_WIN_ROWS, n_rows)
    n_cb = GRID_W // NA_QBLK_COLS
    qcol = np.arange(GRID_W).reshape(n_cb, NA_QBLK_COLS)
    band0 = np.clip(qcol[:, 0] - NA_WIN_COLS // 2, 0, GRID_W - NA_KBLK_COLS)
    kcol = band0[:, None] + np.arange(NA_KBLK_COLS)
    win0 = np.clip(qcol - NA_WIN_COLS // 2, 0, GRID_W - NA_WIN_COLS)
    kc3 = kcol[:, None, :]
    col_ok = jnp.asarray((kc3 >= win0[..., None]) & (kc3 < win0[..., None] + NA_WIN_COLS))
    dcol = np.clip(kc3 - qcol[..., None] + NA_WIN_COLS - 1, 0, 2 * NA_WIN_COLS - 2)
    qg = (q * hd ** -0.5).reshape(bsz, n_rows, n_cb, NA_QBLK_COLS, nh, hd)
    kg = k.reshape(bsz, n_rows, GRID_W, nh, hd)
    vg = v.reshape(bsz, n_rows, GRID_W, nh, hd)
    n_win = kr * NA_KBLK_COLS

    def row_block(r):
        rs = jnp.clip(r - kr // 2, 0, n_rows - kr)
        kb = lax.dynamic_slice_in_dim(kg, rs, kr, axis=1)[:, :, kcol]
        vb = lax.dynamic_slice_in_dim(vg, rs, kr, axis=1)[:, :, kcol]
        qr = lax.dynamic_index_in_dim(qg, r, axis=1, keepdims=False)
        s_win = jnp.einsum('bjqhd,bijkhd->bhjqik', qr, kb).astype(F32)
        drow = rs + jnp.arange(kr) - r + NA_WIN_ROWS - 1
        bias = rel_bias[:, drow][:, :, dcol].transpose(0, 2, 3, 1, 4).astype(F32)
        s_win = jnp.where(col_ok[None, None, :, :, None, :], s_win + bias[None], -jnp.inf)
        s_ctx = jnp.einsum('bjqhd,bshd->bhjqs', qr, k_ctx).astype(F32)
        s_all = jnp.concatenate([s_win.reshape(bsz, nh, n_cb, NA_QBLK_COLS, n_win), s_ctx], axis=-1)
        p = jax.nn.softmax(s_all, axis=-1).astype(v.dtype)
        p_win = p[..., :n_win].reshape(bsz, nh, n_cb, NA_QBLK_COLS, kr, NA_KBLK_COLS)
        return (jnp.einsum('bhjqik,bijkhd->bjqhd', p_win, vb)
                + jnp.einsum('bhjqs,bshd->bjqhd', p[..., n_win:], v_ctx))

    o = lax.map(row_block, jnp.arange(n_rows))
    return o.transpose(1, 0, 2, 3, 4, 5).reshape(bsz, n_tok, nh * hd)


def gqa_latent(q, k, v, k_ctx, v_ctx):
    bsz, n_tok, hq, hd = q.shape
    hkv = k.shape[2]
    k_all = jnp.concatenate([k_ctx, k], axis=1)
    v_all = jnp.concatenate([v_ctx, v], axis=1)
    qb = (q * hd ** -0.5).reshape(bsz, n_tok // Q_BLOCK, Q_BLOCK, hkv, hq // hkv, hd).transpose(1, 0, 2, 3, 4, 5)

    def block(qi):
        s = jnp.einsum('bqkgd,bskd->bkgqs', qi, k_all).astype(F32)
        p = jax.nn.softmax(s, axis=-1).astype(v_all.dtype)
        return jnp.einsum('bkgqs,bskd->bqkgd', p, v_all)

    o = lax.map(block, qb)
    return o.transpose(1, 0, 2, 3, 4, 5).reshape(bsz, n_tok, hq * hd)


def ssd_scan(x, dt, a, b, c, h0):
    bsz, n, nh, hd = x.shape
    ng, ds = b.shape[2], b.shape[3]
    rep = nh // ng
    tl = SSM_CHUNK
    nc = n // tl
    xs = (x.astype(F32) * dt[..., None]).reshape(bsz, nc, tl, ng, rep, hd)
    bs = b.astype(F32).reshape(bsz, nc, tl, ng, ds)
    cs = c.astype(F32).reshape(bsz, nc, tl, ng, ds)
    da = (dt * a).reshape(bsz, nc, tl, ng, rep).transpose(0, 3, 4, 1, 2)
    da_cs = jnp.cumsum(da, axis=-1)
    tri = jnp.tril(jnp.ones((tl, tl), dtype=bool))
    decay_in = jnp.exp(jnp.where(tri, da_cs[..., :, None] - da_cs[..., None, :], -jnp.inf))
    cb = jnp.einsum('bctgn,bcsgn->bgcts', cs, bs)
    y_diag = jnp.einsum('bgrcts,bcsgrp->bctgrp', cb[:, :, None] * decay_in, xs)
    decay_to_end = jnp.exp(da_cs[..., -1:] - da_cs)
    chunk_states = jnp.einsum('bcsgn,bgrcs,bcsgrp->bcgrpn', bs, decay_to_end, xs)
    states = jnp.concatenate([h0.reshape(bsz, 1, ng, rep, hd, ds), chunk_states], axis=1)
    chunk_cs = jnp.cumsum(jnp.pad(da_cs[..., -1], ((0, 0), (0, 0), (0, 0), (1, 0))), axis=-1)
    tri_c = jnp.tril(jnp.ones((nc + 1, nc + 1), dtype=bool))
    decay_chunk = jnp.exp(jnp.where(tri_c, chunk_cs[..., :, None] - chunk_cs[..., None, :], -jnp.inf))
    states = jnp.einsum('bgrzc,bcgrpn->bzgrpn', decay_chunk, states)
    y_off = jnp.einsum('bctgn,bcgrpn,bgrct->bctgrp', cs, states[:, :-1], jnp.exp(da_cs))
    y = (y_diag + y_off).reshape(bsz, n, nh, hd)
    return y, states[:, -1].reshape(bsz, nh, hd, ds)


def ssd_mixer(z_lat, xbc_lat, dt_lat, z_ctx, xbc_ctx, dt_ctx, conv_w, conv_b, a_log, dt_bias, d_skip, norm_w, need_ctx):
    a = -jnp.exp(a_log.astype(F32))

    def prep(xbc, dt_raw):
        bsz, n = xbc.shape[:2]
        xbc = jax.nn.silu(depthwise_conv(xbc, conv_w, conv_b))
        xs, bs, cs = jnp.split(xbc, [SSM_D_INNER, SSM_D_INNER + SSM_BC], axis=-1)
        dt = jax.nn.softplus(dt_raw.astype(F32).reshape(bsz, n, 2, SSM_HEADS) + dt_bias.astype(F32))
        return (xs.reshape(bsz, n, SSM_HEADS, SSM_HEAD_DIM), bs.reshape(bsz, n, SSM_GROUPS, SSM_D_STATE),
                cs.reshape(bsz, n, SSM_GROUPS, SSM_D_STATE), dt)

    x_l, b_l, c_l, dt_l = prep(xbc_lat, dt_lat)
    x_c, b_c, c_c, dt_c = prep(xbc_ctx, dt_ctx)
    h0 = jnp.zeros((x_c.shape[0], SSM_HEADS, SSM_HEAD_DIM, SSM_D_STATE), F32)
    rev = lambda t: jnp.flip(t, axis=1)
    yf_c, hf_c = ssd_scan(x_c, dt_c[:, :, 0], a[0], b_c, c_c, h0)
    yf_l, _ = ssd_scan(x_l, dt_l[:, :, 0], a[0], b_l, c_l, hf_c)
    yb_c, hb_c = ssd_scan(rev(x_c), rev(dt_c[:, :, 1]), a[1], rev(b_c), rev(c_c), h0)
    yb_l, _ = ssd_scan(rev(x_l), rev(dt_l[:, :, 1]), a[1], rev(b_l), rev(c_l), hb_c)

    def finish(yf, yb, xs, z):
        y = (yf + rev(yb) + d_skip.astype(F32)[:, None] * xs.astype(F32)).reshape(z.shape)
        return rms_norm(y * jax.nn.silu(z.astype(F32)), norm_w).astype(z.dtype)

    out_l = finish(yf_l, yb_l, x_l, z_lat)
    out_c = finish(yf_c, yb_c, x_c, z_ctx) if need_ctx else None
    return out_l, out_c


def token_mixing(p_lat, p_ctx, n_rows, cos, sin, na_rel_bias, ssm_conv_w, ssm_conv_b, ssm_a_log, ssm_dt_bias,
                 ssm_d, ssm_norm_w, q_norm_w, k_norm_w, w_out_na, w_out_ssm, w_out_gqa, w_o, need_ctx):
    na_l, z_l, xbc_l, dt_l, gq_l, gk_l, gv_l, gate_l = jnp.split(p_lat, IN_SPLIT_IDX, axis=-1)
    na_c, z_c, xbc_c, dt_c, gq_c, gk_c, gv_c, gate_c = jnp.split(p_ctx, IN_SPLIT_IDX, axis=-1)
    qa_l, ka_l, va_l = (heads(t, NA_HEADS) for t in jnp.split(na_l, 3, axis=-1))
    qa_c, ka_c, va_c = (heads(t, NA_HEADS) for t in jnp.split(na_c, 3, axis=-1))
    o_na_l = neighbourhood_attention(qa_l, ka_l, va_l, ka_c, va_c, na_rel_bias, n_rows)
    o_ssm_l, o_ssm_c = ssd_mixer(z_l, xbc_l, dt_l, z_c, xbc_c, dt_c, ssm_conv_w, ssm_conv_b, ssm_a_log,
                                 ssm_dt_bias, ssm_d, ssm_norm_w, need_ctx)
    q_l = apply_rope(rms_norm(heads(gq_l, GQA_HEADS), q_norm_w), cos, sin)
    k_l = apply_rope(rms_norm(heads(gk_l, GQA_KV_HEADS), k_norm_w), cos, sin)
    q_c = rms_norm(heads(gq_c, GQA_HEADS), q_norm_w)
    k_c = rms_norm(heads(gk_c, GQA_KV_HEADS), k_norm_w)
    v_l, v_c = heads(gv_l, GQA_KV_HEADS), heads(gv_c, GQA_KV_HEADS)
    o_gqa_l = gqa_latent(q_l, k_l, v_l, k_c, v_c)

    def merge(gate, o_na, o_ssm, o_gqa):
        g_na, g_ssm, g_gqa = jnp.split(jax.nn.sigmoid(gate), N_BRANCHES, axis=-1)
        return (g_na * (o_na @ w_out_na) + g_ssm * (o_ssm @ w_out_ssm) + g_gqa * (o_gqa @ w_out_gqa)) @ w_o

    mix_l = merge(gate_l, o_na_l, o_ssm_l, o_gqa_l)
    mix_c = None
    if need_ctx:
        mix_c = merge(gate_c, context_attention(qa_c, ka_c, va_c), o_ssm_c, context_attention(q_c, k_c, v_c))
    return mix_l, mix_c


def conv_ffn(h, w_up, conv_w, conv_b, w_down):
    u = depthwise_conv(h @ w_up, conv_w, conv_b)
    a, b = jnp.split(u, 2, axis=-1)
    return (jax.nn.silu(a) * b) @ w_down


def setup_inputs(seed: int = 0) -> dict:
    key = jax.random.key(seed)
    ks = jax.random.split(key, 32)
    L, D, F = DEPTH, D_MODEL, FFN_HIDDEN

    def normal(k, shape, scale):
        return jax.random.normal(k, shape, F32) * scale

    dt0 = jnp.exp(jax.random.uniform(ks[13], (L, 2, SSM_HEADS), F32, math.log(1e-3), math.log(1e-1)))
    return {
        'x': normal(ks[0], (BATCH, SEQ, D), 1.0),
        'c': normal(ks[1], (BATCH, D), 1.0),
        'ctx': normal(ks[2], (BATCH, CTX_LEN, D), 1.0),
        'c_ctx': normal(ks[3], (D,), 1.0),
        'w_mod': normal(ks[4], (L, D, 6 * D), 0.5 * D ** -0.5),
        'b_mod': normal(ks[5], (L, 6 * D), 0.02),
        'norm1_w': 1.0 + normal(ks[6], (L, D), 0.02),
        'norm2_w': 1.0 + normal(ks[7], (L, D), 0.02),
        'w_in': normal(ks[8], (L, D, D_IN_PROJ), D ** -0.5),
        'na_rel_bias': normal(ks[9], (L, NA_HEADS, 2 * NA_WIN_ROWS - 1, 2 * NA_WIN_COLS - 1), 0.1),
        'ssm_conv_w': normal(ks[10], (L, SSM_CONV, SSM_XBC), SSM_CONV ** -0.5),
        'ssm_conv_b': normal(ks[11], (L, SSM_XBC), 0.02),
        'ssm_a_log': jnp.log(jax.random.uniform(ks[12], (L, 2, SSM_HEADS), F32, 1.0, 16.0)),
        'ssm_dt_bias': dt0 + jnp.log(-jnp.expm1(-dt0)),
        'ssm_d': 1.0 + normal(ks[14], (L, SSM_HEADS), 0.1),
        'ssm_norm_w': 1.0 + normal(ks[15], (L, SSM_D_INNER), 0.02),
        'q_norm_w': 1.0 + normal(ks[16], (L, GQA_HEAD_DIM), 0.02),
        'k_norm_w': 1.0 + normal(ks[17], (L, GQA_HEAD_DIM), 0.02),
        'w_out_na': normal(ks[18], (L, NA_W, D), NA_W ** -0.5),
        'w_out_ssm': normal(ks[19], (L, SSM_D_INNER, D), SSM_D_INNER ** -0.5),
        'w_out_gqa': normal(ks[20], (L, GQA_Q, D), GQA_Q ** -0.5),
        'w_o': normal(ks[21], (L, D, D), D ** -0.5),
        'ffn_w_up': normal(ks[22], (L, D, 2 * F), D ** -0.5),
        'ffn_conv_w': normal(ks[23], (L, FFN_CONV, 2 * F), FFN_CONV ** -0.5),
        'ffn_conv_b': normal(ks[24], (L, 2 * F), 0.02),
        'ffn_w_down': normal(ks[25], (L, F, D), F ** -0.5),
        'final_norm_w': 1.0 + normal(ks[26], (D,), 0.02),
    }


def reference(x, c, ctx, c_ctx, w_mod, b_mod, norm1_w, norm2_w, w_in, na_rel_bias, ssm_conv_w, ssm_conv_b,
              ssm_a_log, ssm_dt_bias, ssm_d, ssm_norm_w, q_norm_w, k_norm_w, w_out_na, w_out_ssm, w_out_gqa,
              w_o, ffn_w_up, ffn_conv_w, ffn_conv_b, ffn_w_down, final_norm_w):
    n_tok = x.shape[1]
    n_rows = n_tok // GRID_W
    cos, sin = axial_rope_tables(n_tok)
    h_lat, h_ctx = x, ctx
    for layer in range(DEPTH):
        need_ctx = layer < DEPTH - 1
        mod_lat = (jax.nn.silu(c) @ w_mod[layer] + b_mod[layer])[:, None, :]
        mod_ctx = (jax.nn.silu(c_ctx) @ w_mod[layer] + b_mod[layer])[None, None, :]
        sh1_l, sc1_l, g1_l, sh2_l, sc2_l, g2_l = jnp.split(mod_lat, 6, axis=-1)
        sh1_c, sc1_c, g1_c, sh2_c, sc2_c, g2_c = jnp.split(mod_ctx, 6, axis=-1)
        p_lat = modulate(rms_norm(h_lat, norm1_w[layer]), sh1_l, sc1_l) @ w_in[layer]
        p_ctx = modulate(rms_norm(h_ctx, norm1_w[layer]), sh1_c, sc1_c) @ w_in[layer]
        mix_l, mix_c = token_mixing(p_lat, p_ctx, n_rows, cos, sin, na_rel_bias[layer], ssm_conv_w[layer],
                                    ssm_conv_b[layer], ssm_a_log[layer], ssm_dt_bias[layer], ssm_d[layer],
                                    ssm_norm_w[layer], q_norm_w[layer], k_norm_w[layer], w_out_na[layer],
                                    w_out_ssm[layer], w_out_gqa[layer], w_o[layer], need_ctx)
        h_lat = h_lat + g1_l * mix_l
        h_lat = h_lat + g2_l * conv_ffn(modulate(rms_norm(h_lat, norm2_w[layer]), sh2_l, sc2_l), ffn_w_up[layer],
                                        ffn_conv_w[layer], ffn_conv_b[layer], ffn_w_down[layer])
        if need_ctx:
            h_ctx = h_ctx + g1_c * mix_c
            h_ctx = h_ctx + g2_c * conv_ffn(modulate(rms_norm(h_ctx, norm2_w[layer]), sh2_c, sc2_c),
                                            ffn_w_up[layer], ffn_conv_w[layer], ffn_conv_b[layer], ffn_w_down[layer])
    return rms_norm(h_lat, final_norm_w)
```

```python
import numpy as np
import ml_dtypes
import concourse.bass as bass
import concourse.mybir as mybir
from concourse.bass_utils import run_bass_kernel_spmd

F32, BF16 = mybir.dt.float32, mybir.dt.bfloat16
AF = mybir.ActivationFunctionType
ALU = mybir.AluOpType
AX = mybir.AxisListType

D = 1024
SEQ = 4096
LC = 256
NT = SEQ + LC
NTILE = NT // 128
DEPTH = 2
GW = 64
EPS = 1e-6
DIN = 7712
FH = 2816
C_NAQ, C_NAK, C_NAV, C_Z, C_XBC, C_DT, C_GQ, C_GK, C_GV, C_GATE = 0, 512, 1024, 1536, 2560, 3840, 3872, 4384, 4512, 4640


class Op:
    __slots__ = ("eng", "fn", "deps", "dma", "sem", "val", "flag", "semi")


class Sched:
    COMPUTE = ("pe", "act", "dve", "pool")

    def __init__(self, nc):
        self.nc = nc
        self.csem = {e: nc.alloc_semaphore(name="cs_" + e) for e in self.COMPUTE}
        self.ccnt = {e: 0 for e in self.COMPUTE}
        self.dpool = {"sp": [nc.alloc_semaphore(name="dsp%d" % i) for i in range(4)],
                      "pool": [nc.alloc_semaphore(name="dpl%d" % i) for i in range(2)]}
        self.duse = {q: [0] * len(v) for q, v in self.dpool.items()}
        self.drr = {q: 0 for q in self.dpool}
        self.waited = {e: {} for e in ("pe", "act", "dve", "pool", "sp")}
        self.nops = 0
        self.psum_names = set()
        self.begin()

    def begin(self):
        self.ops = []
        self.trk = {}
        self.dlast = {}

    def _entries(self, tok, create):
        name, slot = tok
        d = self.trk.setdefault(name, {})
        if slot is None:
            if create and None not in d:
                d[None] = [None, []]
            return list(d.values())
        out = []
        if None in d:
            out.append(d[None])
        if slot not in d and create:
            d[slot] = [None, []]
        if slot in d:
            out.append(d[slot])
        return out

    def add(self, eng, fn, r=(), w=(), dma=False):
        op = Op()
        op.eng, op.fn, op.dma, op.flag, op.sem, op.val = eng, fn, dma, False, None, 0
        idx = len(self.ops)
        deps = set()

        def want(d, kind):
            if d is None:
                return
            o = self.ops[d]
            if not o.dma and not dma and o.eng == eng:
                if eng == "pe" or kind != "raw":
                    return
            deps.add(d)

        for tok in r:
            tok = tok if isinstance(tok, tuple) else (tok, None)
            for e in self._entries(tok, True):
                want(e[0], "raw")
                if tok[0] in self.psum_names:
                    for rd in e[1]:
                        want(rd, "rar")
        for tok in w:
            tok = tok if isinstance(tok, tuple) else (tok, None)
            for e in self._entries(tok, True):
                want(e[0], "waw")
                for rd in e[1]:
                    want(rd, "war")
        for tok in r:
            tok = tok if isinstance(tok, tuple) else (tok, None)
            name, slot = tok
            self._entries(tok, True)
            tgt = self.trk[name][slot]
            if dma:
                tgt[1].append(idx)
            else:
                tgt[1][:] = [x for x in tgt[1] if self.ops[x].dma or self.ops[x].eng != eng]
                tgt[1].append(idx)
        for tok in w:
            tok = tok if isinstance(tok, tuple) else (tok, None)
            name, slot = tok
            d = self.trk[name]
            if slot is None:
                d.clear()
                d[None] = [idx, []]
            else:
                d[slot] = [idx, []]
        if dma:
            q = eng
            j = self.drr[q]
            self.drr[q] = (j + 1) % len(self.dpool[q])
            op.semi = j
            prev = self.dlast.get((q, j))
            if prev is not None:
                deps.add(prev)
            self.dlast[(q, j)] = idx
        op.deps = deps
        self.ops.append(op)
        return idx

    def emit(self):
        nc = self.nc
        ops = self.ops
        for op in ops:
            for d in op.deps:
                ops[d].flag = True
        for op in ops:
            if op.dma:
                q = op.eng
                self.duse[q][op.semi] += 1
                op.sem = self.dpool[q][op.semi]
                op.val = 16 * self.duse[q][op.semi]
            elif op.flag:
                self.ccnt[op.eng] += 1
                op.sem = self.csem[op.eng]
                op.val = self.ccnt[op.eng]
        self.nops += len(ops)
        sched = self

        def run(engname, eng):
            wd = sched.waited[engname]
            for op in ops:
                if op.eng != engname:
                    continue
                need = {}
                for d in op.deps:
                    o = ops[d]
                    key = id(o.sem)
                    if key not in need or need[key][1] < o.val:
                        need[key] = (o.sem, o.val)
                lst = []
                for key, (s, v) in need.items():
                    if wd.get(key, 0) >= v:
                        continue
                    wd[key] = v
                    lst.append((s, v))
                if op.dma or len(lst) > 1:
                    extra = lst if op.dma else lst[:-1]
                    for s, v in extra:
                        eng.wait_ge(s, v)
                    lst = [] if op.dma else lst[-1:]
                ins = op.fn(eng)
                if lst:
                    ins._wait_ge(lst[0][0], lst[0][1])
                if op.dma:
                    ins.then_inc(op.sem, 16)
                elif op.flag:
                    ins.then_inc(op.sem, 1)
            if engname in sched.dpool:
                for j, s in enumerate(sched.dpool[engname]):
                    v = 16 * sched.duse[engname][j]
                    if v and wd.get(id(s), 0) < v:
                        wd[id(s)] = v
                        eng.wait_ge(s, v)

        used = set(op.eng for op in ops)
        with nc.Block() as block:
            if "pe" in used:
                @block.tensor
                def _(e):
                    run("pe", e)
            if "act" in used:
                @block.scalar
                def _(e):
                    run("act", e)
            if "dve" in used:
                @block.vector
                def _(e):
                    run("dve", e)
            if "pool" in used:
                @block.gpsimd
                def _(e):
                    run("pool", e)
            if "sp" in used:
                @block.sync
                def _(e):
                    run("sp", e)
        self.begin()


class Buf:
    def __init__(self, name, h):
        self.name, self.h = name, h

    def t(self, slot=None):
        return (self.name, slot)

    def __getitem__(self, k):
        return self.h[k]


class Sl:
    def __init__(self, buf, n):
        self.b, self.n, self.name = buf, n, buf.name

    def t(self, slot=None):
        return self.b.t(slot)

    def __getitem__(self, k):
        if not isinstance(k, tuple):
            k = (k,)
        nd = len(self.b.h.shape)
        k = tuple(k) + (slice(None),) * (nd - len(k))
        last = k[-1]
        if isinstance(last, slice) and last == slice(None):
            k = k[:-1] + (slice(0, self.n),)
        return self.b.h[k]


def rap(buf, row, p0, npart, off, dims):
    return bass.AP(buf.h, p0 * row + off, [[row, npart]] + [list(d) for d in dims])


import contextlib

NA_NOPOOL = True
PENG = "dve"
TOKBLKS = [(0, 256, 1)] + [(256 + 512 * i, 512, 0) for i in range(8)]


class KB:
    def __init__(self, dbg=()):
        self.dbg = set(dbg)
        nc = self.nc = bass.Bass("TRN2", target_bir_lowering=False)
        self.S = Sched(nc)
        self.ges = contextlib.ExitStack()
        self.es = None
        self.din = {}
        self._uid = 0

    def inp(self, name, shape, dt=F32):
        t = self.nc.dram_tensor(name, list(shape), dt, kind="ExternalInput")
        self.din[name] = t
        return t

    def dram(self, name, shape, dt, out=False):
        kind = "ExternalOutput" if (out or name in self.dbg) else "Internal"
        return Buf(name, self.nc.dram_tensor(name, list(shape), dt, kind=kind))

    def sb(self, name, shape, dt, glob=False, es=None):
        es = es if es is not None else (self.ges if glob else self.es)
        self._uid += 1
        name = "%s_u%d" % (name, self._uid)
        return Buf(name, es.enter_context(self.nc.sbuf_tensor(name, list(shape), dt)))

    def ps(self, name, shape, dt=F32):
        self._uid += 1
        name = "%s_u%d" % (name, self._uid)
        self.S.psum_names.add(name)
        return Buf(name, self.es.enter_context(self.nc.psum_tensor(name, list(shape), dt)))

    @contextlib.contextmanager
    def stage(self):
        self.es = contextlib.ExitStack()
        with self.es:
            yield
            with self.nc.allow_non_contiguous_dma(reason="small strided parameter loads"):
                self.S.emit()
        self.es = None

    def dma(self, q, out, in_, r, w):
        self.S.add(q, lambda e: e.dma_start(out=out, in_=in_), r=r, w=w, dma=True)

    def mm(self, out, lhsT, rhs, start, stop, r, w):
        self.S.add("pe", lambda e: e.matmul(out, lhsT, rhs, start=start, stop=stop), r=r, w=w)

    def tr(self, out, in_, ident, r, w):
        self.S.add("pe", lambda e: e.transpose(out, in_, ident), r=r, w=w)

    def act(self, out, in_, func, r, w, bias=0.0, scale=1.0, accum_out=None):
        if accum_out is None:
            self.S.add("act", lambda e: e.activation(out=out, in_=in_, func=func, bias=bias, scale=scale), r=r, w=w)
        else:
            self.S.add("act", lambda e: e.activation(out=out, in_=in_, func=func, bias=bias, scale=scale,
                                                     accum_out=accum_out), r=r, w=w)

    def tt(self, eng, out, in0, in1, op, r, w):
        self.S.add(eng, lambda e: e.tensor_tensor(out=out, in0=in0, in1=in1, op=op), r=r, w=w)

    def ts(self, eng, out, in0, s1, s2, op0, op1, r, w):
        if s2 is None:
            self.S.add(eng, lambda e: e.tensor_scalar(out=out, in0=in0, scalar1=s1, scalar2=None, op0=op0), r=r, w=w)
        else:
            self.S.add(eng, lambda e: e.tensor_scalar(out=out, in0=in0, scalar1=s1, scalar2=s2, op0=op0, op1=op1),
                       r=r, w=w)

    def stt(self, out, in0, scalar, in1, op0, op1, r, w):
        self.S.add("dve", lambda e: e.scalar_tensor_tensor(out=out, in0=in0, scalar=scalar, in1=in1, op0=op0, op1=op1),
                   r=r, w=w)

    def cp(self, eng, out, in_, r, w):
        if eng == "act":
            self.S.add("act", lambda e: e.copy(out=out, in_=in_), r=r, w=w)
        else:
            self.S.add(eng, lambda e: e.tensor_copy(out=out, in_=in_), r=r, w=w)

    def memset(self, eng, ap, val, w):
        self.S.add(eng, lambda e: e.memset(ap, val), r=(), w=w)

    def declare_io(self):
        L = DEPTH
        self.x = self.inp("x", [SEQ, D])
        self.c = self.inp("c", [D])
        self.ctx = self.inp("ctx", [LC, D])
        self.c_ctx = self.inp("c_ctx", [D])
        self.w_mod = self.inp("w_mod", [L, D, 6 * D])
        self.b_mod = self.inp("b_mod", [L, 6 * D])
        self.norm1_w = self.inp("norm1_w", [L, D])
        self.norm2_w = self.inp("norm2_w", [L, D])
        self.w_in = self.inp("w_in", [L, D, DIN])
        self.na_bias_g = self.inp("na_bias_g", [L, 16, 128, 512])
        self.na_mask = self.inp("na_mask", [16, 128, 512])
        self.rope = self.inp("rope", [SEQ, 2, 32])
        self.ssm_conv_w = self.inp("ssm_conv_w", [L, 3, 1280])
        self.ssm_conv_b = self.inp("ssm_conv_b", [L, 1280])
        self.ssm_a_log = self.inp("ssm_a_log", [L, 2, 16])
        self.ssm_dt_bias = self.inp("ssm_dt_bias", [L, 2, 16])
        self.ssm_d = self.inp("ssm_d", [L, 16])
        self.ssm_norm_w = self.inp("ssm_norm_w", [L, 1024])
        self.q_norm_w = self.inp("q_norm_w", [L, 64])
        self.k_norm_w = self.inp("k_norm_w", [L, 64])
        self.w_out_na = self.inp("w_out_na", [L, 512, D])
        self.w_out_ssm = self.inp("w_out_ssm", [L, 1024, D])
        self.w_out_gqa = self.inp("w_out_gqa", [L, 512, D])
        self.w_o = self.inp("w_o", [L, D, D])
        self.ffn_w_up = self.inp("ffn_w_up", [L, D, 2 * FH])
        self.ffn_conv_w = self.inp("ffn_conv_w", [L, 3, 2 * FH])
        self.ffn_conv_b = self.inp("ffn_conv_b", [L, 2 * FH])
        self.ffn_w_down = self.inp("ffn_w_down", [L, FH, D])
        self.final_norm_w = self.inp("final_norm_w", [D])
        self.out = self.dram("out", [SEQ, D], F32, out=True)
        self.hT = self.dram("hT", [D, NT], F32)
        self.hT2 = self.dram("hT2", [D, NT], F32)
        self.hcur = self.hT
        self.naqT = self.dram("naqT", [512, NT], BF16)
        self.nakT = self.dram("nakT", [512, NT], BF16)
        self.nav = self.dram("nav", [NT, 512], BF16)
        self.zs = self.dram("zs", [NT, 1024], BF16)
        self.xbcT = self.dram("xbcT", [1280, NT], BF16)
        self.dtr = self.dram("dtr", [NT, 32], F32)
        self.gqT = self.dram("gqT", [512, NT], BF16)
        self.gkT2 = self.dram("gkT2", [2, 128, NT], BF16)
        self.gv = self.dram("gv", [NT, 128], BF16)
        self.gatesT = self.dram("gatesT", [3072, NT], BF16)
        self.onaT = self.dram("onaT", [8, 64, NT], BF16)
        self.ogT = self.dram("ogT", [8, 64, NT], BF16)
        self.ossmT = self.dram("ossmT", [1024, NT], BF16)
        self.yf = self.dram("yf", [NT, 1024], F32)

    def consts(self):
        nc = self.nc
        self.ident_f = self.sb("ident_f", [128, 128], F32, glob=True)
        self.ident_b = self.sb("ident_b", [128, 128], BF16, glob=True)
        self.ones_b = self.sb("ones_b", [128, 128], BF16, glob=True)
        self.ones_f = self.sb("ones_f", [128, 128], F32, glob=True)
        self.modv = [self.sb("modv%d" % l, [128, 48, 2], F32, glob=True) for l in range(DEPTH)]
        self.A1 = [self.sb("A1_%d" % l, [128, 8, 2], F32, glob=True) for l in range(DEPTH)]
        self.A2 = [self.sb("A2_%d" % l, [128, 8, 2], F32, glob=True) for l in range(DEPTH)]
        self.eps_t = self.sb("eps_t", [128, 1], F32, glob=True)
        self.zero_t = self.sb("zero_t", [128, 1], F32, glob=True)
        with self.stage():
            idf, idb, ob, of = self.ident_f, self.ident_b, self.ones_b, self.ones_f
            self.memset("dve", self.eps_t[:], EPS, w=[self.eps_t.t()])
            self.memset("dve", self.zero_t[:], 0.0, w=[self.zero_t.t()])
            self.memset("pool", idf[:], 0.0, w=[idf.t()])
            self.S.add("pool", lambda e: e.affine_select(out=idf[:], in_=idf[:], pattern=[[-1, 128]],
                                                         compare_op=ALU.not_equal, fill=1.0, base=0,
                                                         channel_multiplier=1), r=[idf.t()], w=[idf.t()])
            self.cp("dve", idb[:], idf[:], r=[idf.t()], w=[idb.t()])
            self.memset("dve", ob[:], 1.0, w=[ob.t()])
            self.memset("dve", of[:], 1.0, w=[of.t()])

    def prologue(self):
        with self.stage():
            xin = [self.sb("xin%d" % i, [128, D], F32) for i in range(3)]
            pt = [self.ps("ptr%d" % i, [128, 1024], F32) for i in range(2)]
            stg = [self.sb("stg%d" % i, [128, 8, 512], F32) for i in range(2)]
            hTv = self.hT.h.ap().rearrange("(k p) t -> p k t", p=128)
            groups = [(0, 2)] + [(2 + 4 * g, 4) for g in range(8)]
            n = 0
            for gi, (ti0, cnt) in enumerate(groups):
                sg = stg[gi % 2]
                for j in range(cnt):
                    ti = ti0 + j
                    xb = xin[n % 3]
                    p = pt[n % 2]
                    n += 1
                    src = self.ctx.ap()[ti * 128:(ti + 1) * 128, :] if ti < 2 else \
                        self.x.ap()[(ti - 2) * 128:(ti - 1) * 128, :]
                    self.dma("sp", xb[:], src, r=[], w=[xb.t()])
                    for k in range(8):
                        self.tr(p[:, k * 128:(k + 1) * 128], xb[:, k * 128:(k + 1) * 128], self.ident_f[:],
                                r=[xb.t()], w=[p.t()])
                    dst = sg[:, :, j * 128:(j + 1) * 128]
                    src_ps = p[:].rearrange("p (k t) -> p k t", k=8)
                    self.cp("act" if n % 2 else "dve", dst, src_ps, r=[p.t()], w=[sg.t(j)])
                t0 = ti0 * 128
                self.dma("sp", hTv[:, :, t0:t0 + cnt * 128], sg[:, :, 0:cnt * 128], r=[sg.t()], w=[self.hT.t()])

    def modstage(self):
        with self.stage():
            craw = self.sb("craw", [128, 8, 2], F32)
            csil = self.sb("csil", [128, 8, 2], F32)
            self.dma("sp", craw[:, :, 0], self.c.ap().rearrange("(k p) -> p k", p=128), r=[], w=[craw.t(0)])
            self.dma("sp", craw[:, :, 1], self.c_ctx.ap().rearrange("(k p) -> p k", p=128), r=[], w=[craw.t(1)])
            self.act(csil[:], craw[:], AF.Silu, r=[craw.t()], w=[csil.t()])
            wm = [self.sb("wm%d" % i, [128, 8, 512], F32) for i in range(2)]
            bm = self.sb("bm", [128, 48], F32)
            nw = self.sb("nw", [128, 8], F32)
            pm = self.ps("pmod", [128, 96], F32)
            n = 0
            for l in range(DEPTH):
                wv = self.w_mod.ap()[l].rearrange("(k p) n -> p k n", p=128)
                for og in range(12):
                    w = wm[n % 2]
                    n += 1
                    self.dma("sp", w[:], wv[:, :, og * 512:(og + 1) * 512], r=[], w=[w.t()])
                    for oc in range(4):
                        o = og * 4 + oc
                        for k in range(8):
                            self.mm(pm[:, 2 * o:2 * o + 2], w[:, k, oc * 128:(oc + 1) * 128], csil[:, k, :],
                                    k == 0, k == 7, r=[w.t(), csil.t()], w=[pm.t()])
                self.dma("sp", bm[:], self.b_mod.ap()[l].rearrange("(o p) -> p o", p=128), r=[], w=[bm.t()])
                mv = self.modv[l]
                self.tt("dve", mv[:], pm[:].rearrange("p (o j) -> p o j", j=2),
                        bm[:].unsqueeze(2).broadcast_to([128, 48, 2]), ALU.add, r=[pm.t(), bm.t()], w=[mv.t()])
                for (A, nwin, sc0) in ((self.A1[l], self.norm1_w, 8), (self.A2[l], self.norm2_w, 32)):
                    self.dma("sp", nw[:], nwin.ap()[l].rearrange("(k p) -> p k", p=128), r=[], w=[nw.t()])
                    self.ts("dve", A[:], mv[:, sc0:sc0 + 8, :], 1.0, None, ALU.add, None, r=[mv.t()], w=[A.t()])
                    self.tt("dve", A[:], A[:], nw[:].unsqueeze(2).broadcast_to([128, 8, 2]), ALU.mult,
                            r=[A.t(), nw.t()], w=[A.t()])

    def norm_block(self, hb, N, A, mv, b0, j, out_fn, out_w, sq, ssp, rs, tmp):
        self.act(sq[:, :, 0:N], hb[:, :, 0:N], AF.Square, r=[hb.t()], w=[sq.t()])
        for k in range(8):
            self.mm(ssp[:, 0:N], self.ones_b[:], sq[:, k, 0:N], k == 0, k == 7, r=[sq.t()], w=[ssp.t()])
        self.act(rs[:, 0:N], ssp[:, 0:N], AF.Sqrt, bias=self.eps_t[:, 0:1], scale=1.0 / D, r=[ssp.t()], w=[rs.t()])
        self.S.add("dve", lambda e: e.reciprocal(out=rs[:, 0:N], in_=rs[:, 0:N]), r=[rs.t()], w=[rs.t()])
        if isinstance(tmp, list):
            for k in range(8):
                tk = tmp[k % len(tmp)]
                self.tt("dve", tk[:, 0:N], hb[:, k, 0:N], rs[:, 0:N], ALU.mult, r=[hb.t(), rs.t()], w=[tk.t()])
                self.act(out_fn(k), tk[:, 0:N], AF.Identity, scale=A[:, k, j:j + 1], bias=mv[:, b0 + k, j:j + 1],
                         r=[tk.t()], w=out_w)
            return
        self.tt("dve", tmp[:, :, 0:N], hb[:, :, 0:N], rs[:, 0:N].unsqueeze(1).broadcast_to([128, 8, N]), ALU.mult,
                r=[hb.t(), rs.t()], w=[tmp.t()])
        for k in range(8):
            self.act(out_fn(k), tmp[:, k, 0:N], AF.Identity, scale=A[:, k, j:j + 1], bias=mv[:, b0 + k, j:j + 1],
                     r=[tmp.t()], w=out_w)

    def instage(self, l, outer):
        xnT = self.sb("xnT", [128, 8, NT], BF16, es=outer)
        hTv = self.hcur.h.ap().rearrange("(k p) t -> p k t", p=128)
        with self.stage():
            hbs = [self.sb("hb%d" % i, [128, 8, 512], F32) for i in range(2)]
            sqs = [self.sb("sq%d" % i, [128, 8, 512], BF16) for i in range(2)]
            tmp = self.sb("ntmp", [128, 8, 512], F32)
            rs = self.sb("nrs", [128, 512], F32)
            ssp = self.ps("nssp", [128, 512], F32)
            for bi, (t0, N, isctx) in enumerate(TOKBLKS):
                hb, sq = hbs[bi % 2], sqs[bi % 2]
                self.dma("sp", hb[:, :, 0:N], hTv[:, :, t0:t0 + N], r=[self.hcur.t()], w=[hb.t()])
                self.norm_block(hb, N, self.A1[l], self.modv[l], 0, isctx,
                                lambda k, t0=t0, N=N: xnT[:, k, t0:t0 + N], [xnT.t(bi)], sq, ssp, rs, tmp)
        wv = self.w_in.ap()[l].rearrange("(k p) n -> p k n", p=128)
        with self.stage():
            Ws = [self.sb("Wc%d" % i, [128, 8, 512], BF16) for i in range(2)]
            pss = [self.ps("pin%d" % i, [128, 512], F32) for i in range(4)]
            stg = [self.sb("stgi%d" % i, [128, 4, 512], BF16) for i in range(3)]
            stgf = [self.sb("stgf%d" % i, [128, 4, 32], F32) for i in range(2)]
            cnt = {"w": 0, "p": 0, "s": 0, "e": 0}

            def nxt(lst, key):
                b = lst[cnt[key] % len(lst)]
                cnt[key] += 1
                return b

            def evac(func, out, in_, r, w):
                if func == "copy":
                    cnt["e"] += 1
                    self.cp("act" if cnt["e"] % 2 else "dve", out, in_, r=r, w=w)
                elif func == "silu":
                    self.act(out, in_, AF.Silu, r=r, w=w)
                elif func == "sigmoid":
                    self.act(out, in_, AF.Sigmoid, r=r, w=w)
                elif func == "q8":
                    self.ts("dve", out, in_, 0.125, None, ALU.mult, None, r=r, w=w)

            def blk_of_tile(ti):
                return 0 if ti < 2 else 1 + (ti - 2) // 4

            def fm_group(c0, n, func, dstT, drow0):
                W = nxt(Ws, "w")
                self.dma("pool", W[:, :, 0:n], wv[:, :, c0:c0 + n], r=[], w=[W.t()])
                for bi, (t0, N, isctx) in enumerate(TOKBLKS):
                    sg = nxt(stg, "s")
                    for oc in range(n // 128):
                        ps = nxt(pss, "p")
                        for k in range(8):
                            self.mm(ps[:, 0:N], W[:, k, oc * 128:(oc + 1) * 128], xnT[:, k, t0:t0 + N], k == 0, k == 7,
                                    r=[W.t(), xnT.t(bi)], w=[ps.t()])
                        evac(func, sg[:, oc, 0:N], ps[:, 0:N], r=[ps.t()], w=[sg.t(oc)])
                    dst = dstT.h.ap()[drow0:drow0 + n, t0:t0 + N].rearrange("(o p) t -> p o t", p=128)
                    self.dma("sp", dst, sg[:, 0:n // 128, 0:N], r=[sg.t()], w=[dstT.t()])

            def tm_group(c0, n, func, dst, dcol0, fp32=False):
                W = nxt(Ws, "w")
                self.dma("pool", W[:, :, 0:n], wv[:, :, c0:c0 + n], r=[], w=[W.t()])
                for (ti0, tc) in [(0, 2)] + [(2 + 4 * g, 4) for g in range(8)]:
                    sg = nxt(stgf, "s") if fp32 else nxt(stg, "s")
                    for j in range(tc):
                        ti = ti0 + j
                        ps = nxt(pss, "p")
                        for k in range(8):
                            self.mm(ps[:, 0:n], xnT[:, k, ti * 128:(ti + 1) * 128], W[:, k, 0:n], k == 0, k == 7,
                                    r=[W.t(), xnT.t(blk_of_tile(ti))], w=[ps.t()])
                        evac(func, sg[:, j, 0:n], ps[:, 0:n], r=[ps.t()], w=[sg.t(j)])
                    d = dst.h.ap()[ti0 * 128:(ti0 + tc) * 128, dcol0:dcol0 + n].rearrange("(j p) n -> p j n", p=128)
                    self.dma("sp", d, sg[:, 0:tc, 0:n], r=[sg.t()], w=[dst.t()])

            fm_group(C_NAQ, 512, "q8", self.naqT, 0)
            fm_group(C_NAK, 512, "copy", self.nakT, 0)
            tm_group(C_NAV, 512, "copy", self.nav, 0)
            tm_group(C_Z, 512, "silu", self.zs, 0)
            tm_group(C_Z + 512, 512, "silu", self.zs, 512)
            fm_group(C_XBC, 512, "copy", self.xbcT, 0)
            fm_group(C_XBC + 512, 512, "copy", self.xbcT, 512)
            fm_group(C_XBC + 1024, 256, "copy", self.xbcT, 1024)
            tm_group(C_DT, 32, "copy", self.dtr, 0, fp32=True)
            tm_group(C_GV, 128, "copy", self.gv, 0)
            for g in range(6):
                fm_group(C_GATE + 512 * g, 512, "sigmoid", self.gatesT, 512 * g)
            self.gqa_group(l, xnT, wv, blk_of_tile)

    def gqa_group(self, l, xnT, wv, blk_of_tile):
        wg = self.sb("wg", [128, 8, 640], BF16)
        self.dma("pool", wg[:], wv[:, :, C_GQ:C_GQ + 640], r=[], w=[wg.t()])
        ropeT = self.sb("ropeT", [128, 32, 64], F32)
        self.dma("sp", ropeT[:], self.rope.ap().rearrange("(i p) a b -> p i (a b)", p=128), r=[], w=[ropeT.t()])
        wqk = self.sb("wqk", [128, 640], F32)
        self.dma("sp", wqk[:, 0:64], self.q_norm_w.ap()[l:l + 1, :].broadcast_to([128, 64]), r=[], w=[wqk.t()])
        self.dma("sp", wqk[:, 512:576], self.k_norm_w.ap()[l:l + 1, :].broadcast_to([128, 64]), r=[], w=[wqk.t()])
        self.ts("dve", wqk[:, 0:64], wqk[:, 0:64], 0.125, None, ALU.mult, None, r=[wqk.t()], w=[wqk.t()])
        for h in range(1, 8):
            self.cp("dve", wqk[:, h * 64:(h + 1) * 64], wqk[:, 0:64], r=[wqk.t()], w=[wqk.t()])
        self.cp("dve", wqk[:, 576:640], wqk[:, 512:576], r=[wqk.t()], w=[wqk.t()])
        ps2 = self.ps("pg2", [128, 1024], F32)
        pT = self.ps("pgT", [128, 6, 128], BF16)
        sqv = self.sb("gsqv", [128, 640], F32)
        ss10 = self.sb("gss", [128, 10], F32)
        xn = self.sb("gxn", [128, 640], F32)
        t1 = self.sb("gt1", [128, 320], F32)
        t2 = self.sb("gt2", [128, 320], F32)
        t3 = self.sb("gt3", [128, 320], F32)
        t4 = self.sb("gt4", [128, 320], F32)
        qkb = self.sb("gqkb", [128, 640], BF16)
        kd = self.sb("gkd", [128, 256], BF16)
        stq = [self.sb("gstq%d" % i, [128, 6, 512], BF16) for i in range(2)]
        v3 = lambda ap, a: ap.rearrange("p (a b) -> p a b", a=a)
        gi = 0
        for (ti0, tc) in [(0, 2)] + [(2 + 4 * g, 4) for g in range(8)]:
            sg = stq[gi % 2]
            gi += 1
            for j in range(tc):
                ti = ti0 + j
                xs = [wg.t(), xnT.t(blk_of_tile(ti))]
                for k in range(8):
                    self.mm(ps2[:, 0:512], xnT[:, k, ti * 128:(ti + 1) * 128], wg[:, k, 0:512], k == 0, k == 7,
                            r=xs, w=[ps2.t(0)])
                for k in range(8):
                    self.mm(ps2[:, 512:640], xnT[:, k, ti * 128:(ti + 1) * 128], wg[:, k, 512:640], k == 0, k == 7,
                            r=xs, w=[ps2.t(1)])
                self.act(sqv[:], ps2[:, 0:640], AF.Square, r=[ps2.t()], w=[sqv.t()])
                self.S.add("dve", lambda e: e.tensor_reduce(out=ss10[:], in_=v3(sqv[:], 10), axis=AX.X, op=ALU.add),
                           r=[sqv.t()], w=[ss10.t()])
                self.act(ss10[:], ss10[:], AF.Sqrt, bias=self.eps_t[:, 0:1], scale=1.0 / 64, r=[ss10.t()], w=[ss10.t()])
                self.S.add("dve", lambda e: e.reciprocal(out=ss10[:], in_=ss10[:]), r=[ss10.t()], w=[ss10.t()])
                self.tt("dve", v3(xn[:], 10), v3(ps2[:, 0:640], 10), ss10[:].unsqueeze(2).broadcast_to([128, 10, 64]),
                        ALU.mult, r=[ps2.t(), ss10.t()], w=[xn.t()])
                self.tt("dve", xn[:], xn[:], wqk[:], ALU.mult, r=[xn.t(), wqk.t()], w=[xn.t()])
                if ti >= 2:
                    x4 = xn[:].rearrange("p (h i two) -> p h i two", h=10, two=2)
                    o4 = qkb[:].rearrange("p (h i two) -> p h i two", h=10, two=2)
                    xe, xo, oe, oo = x4[:, :, :, 0], x4[:, :, :, 1], o4[:, :, :, 0], o4[:, :, :, 1]
                    cosb = ropeT[:, ti - 2, 0:32].unsqueeze(1).broadcast_to([128, 10, 32])
                    sinb = ropeT[:, ti - 2, 32:64].unsqueeze(1).broadcast_to([128, 10, 32])
                    a1, a2, a3, a4 = v3(t1[:], 10), v3(t2[:], 10), v3(t3[:], 10), v3(t4[:], 10)
                    self.tt("dve", a1, xe, cosb, ALU.mult, r=[xn.t(), ropeT.t()], w=[t1.t()])
                    self.tt("dve", a2, xo, sinb, ALU.mult, r=[xn.t(), ropeT.t()], w=[t2.t()])
                    self.tt("dve", oe, a1, a2, ALU.subtract, r=[t1.t(), t2.t()], w=[qkb.t(0)])
                    self.tt(PENG, a3, xe, sinb, ALU.mult, r=[xn.t(), ropeT.t()], w=[t3.t()])
                    self.tt(PENG, a4, xo, cosb, ALU.mult, r=[xn.t(), ropeT.t()], w=[t4.t()])
                    self.tt(PENG, oo, a3, a4, ALU.add, r=[t3.t(), t4.t()], w=[qkb.t(1)])
                else:
                    self.cp("dve", qkb[:], xn[:], r=[xn.t()], w=[qkb.t()])
                kd4 = kd[:].rearrange("p (g c d) -> p g c d", g=2, c=2)
                ksrc = v3(qkb[:, 512:640], 2).unsqueeze(2).broadcast_to([128, 2, 2, 64])
                self.cp("act", kd4, ksrc, r=[qkb.t()], w=[kd.t()])
                for c in range(4):
                    self.tr(pT[:, c, :], qkb[:, c * 128:(c + 1) * 128], self.ident_b[:], r=[qkb.t()], w=[pT.t()])
                for g in range(2):
                    self.tr(pT[:, 4 + g, :], kd[:, g * 128:(g + 1) * 128], self.ident_b[:], r=[kd.t()], w=[pT.t()])
                self.cp("act", sg[:, :, j * 128:(j + 1) * 128], pT[:], r=[pT.t()], w=[sg.t(j)])
            t0, n = ti0 * 128, tc * 128
            self.dma("sp", self.gqT.h.ap()[:, t0:t0 + n].rearrange("(c p) t -> p c t", p=128), sg[:, 0:4, 0:n],
                     r=[sg.t()], w=[self.gqT.t()])
            self.dma("sp", self.gkT2.h.ap()[:, :, t0:t0 + n].rearrange("g p t -> p g t"), sg[:, 4:6, 0:n],
                     r=[sg.t()], w=[self.gkT2.t()])

    def softmax_finish(self, po, N, rrow, pb, pbs, out_ap, out_w):
        self.S.add("dve", lambda e: e.reciprocal(out=rrow[64:65, 0:N], in_=po[64:65, 0:N]), r=[po.t()], w=[rrow.t()])
        self.mm(pb[0:64, 0:N], self.ones_f[64:65, 0:64], rrow[64:65, 0:N], True, True, r=[rrow.t()], w=[pb.t()])
        self.cp("act", pbs[0:64, 0:N], pb[0:64, 0:N], r=[pb.t()], w=[pbs.t()])
        self.tt("dve", out_ap, po[0:64, 0:N], pbs[0:64, 0:N], ALU.mult, r=[po.t(), pbs.t()], w=out_w)

    def nastage(self, l, need_ctx, lim=None):
        with self.stage():
            kT = self.sb("nkT", [128, 4, NT], BF16)
            qT = self.sb("nqT", [128, 4, NT], BF16)
            V = self.sb("nV", [128, NTILE, 8, 65], BF16)
            TB = self.sb("nTB", [128, 16, 512], BF16)
            self.dma("sp", kT[:], self.nakT.h.ap().rearrange("(c p) t -> p c t", p=128), r=[self.nakT.t()], w=[kT.t()])
            self.dma("sp", qT[:], self.naqT.h.ap().rearrange("(c p) t -> p c t", p=128), r=[self.naqT.t()], w=[qT.t()])
            self.memset("dve", V[:, :, :, 64:65], 1.0, w=[V.t("ones")])
            nv = self.nav.h.ap().rearrange("(i p) (h d) -> p i h d", p=128, h=8)
            for i0 in range(NTILE):
                self.dma("sp", V[:, i0, :, 0:64], nv[:, i0], r=[self.nav.t()], w=[V.t("d%d" % i0)])
            bst = [self.sb("nbst%d" % i, [128, 2, 512], F32) for i in range(2)]
            mst = [self.sb("nmst%d" % i, [128, 2, 512], F32) for i in range(2)]
            for ch in range(8):
                b, m = bst[ch % 2], mst[ch % 2]
                self.dma("sp", b[:], self.na_bias_g.ap()[l, 2 * ch:2 * ch + 2].rearrange("t p n -> p t n"), r=[], w=[b.t()])
                self.dma("sp", m[:], self.na_mask.ap()[2 * ch:2 * ch + 2].rearrange("t p n -> p t n"), r=[], w=[m.t()])
                self.act(b[:], b[:], AF.Exp, r=[b.t()], w=[b.t()])
                self.tt("dve", TB[:, 2 * ch:2 * ch + 2, :], b[:], m[:], ALU.mult, r=[b.t(), m.t()], w=[TB.t(ch)])
            TB5 = TB[:].rearrange("p t (c e q) -> p t c e q", c=4, e=2)
            pS = [self.ps("nps%d" % i, [128, 512], F32) for i in range(4)]
            pO = [self.ps("npo%d" % i, [128, 512], F32) for i in range(2)]
            pb = self.ps("npb", [128, 512], F32)
            P = [[[self.sb("nP%d_%d_%d" % (u, a, e), [128, 512], BF16) for e in range(2)] for a in range(4)]
                 for u in range(2)]
            rrow = self.sb("nrrow", [128, 512], F32)
            pbs = self.sb("npbs", [64, 512], F32)
            osb = [self.sb("nosb%d" % i, [64, 8, 256], BF16) for i in range(2)]
            units = []
            if need_ctx:
                for u in range(4):
                    units.append((u * 64, [(0, 0, None), (1, 128, None)]))
            for r_ in range(64):
                rs = min(max(r_ - 4, 0), 56)
                kts = [(0, 0, None), (1, 128, None)]
                if rs % 2 == 0:
                    for j in range(4):
                        a = rs + 2 * j
                        kts.append((2 + a // 2, 256 + a * 64, a - r_ + 7))
                else:
                    for j in range(5):
                        a = rs - 1 + 2 * j
                        dr = a - r_ + 7
                        tb = 14 if j == 0 else (15 if j == 4 else dr)
                        kts.append((2 + a // 2, 256 + a * 64, tb))
                units.append((256 + r_ * 64, kts))
            psn = 0
            mtog = 0
            npend = []
            if lim:
                units = units[:lim]
            def phaseA(ui, hook=None):
                nonlocal psn, mtog
                tq0, kts = units[ui]
                ub = ui % 2
                nk = len(kts)
                npair = (nk + 1) // 2
                for a in range(npair):
                    pair = kts[2 * a:2 * a + 2]
                    banks = [pS[psn % 4], pS[(psn + 1) % 4]]
                    psn += 2
                    for s_, (vt, kc0, tb) in enumerate(pair):
                        for h in range(8):
                            e, c = h % 2, h // 2
                            self.mm(banks[e][:, s_ * 256 + c * 64: s_ * 256 + c * 64 + 64],
                                    kT[e * 64:(e + 1) * 64, c, kc0:kc0 + 128], qT[e * 64:(e + 1) * 64, c, tq0:tq0 + 64],
                                    True, True, r=[kT.t(), qT.t()], w=[banks[e].t()])
                    ncol = 256 * len(pair)
                    for e in range(2):
                        pt = P[ub][a][e]
                        self.act(pt[:, 0:ncol], banks[e][:, 0:ncol], AF.Exp, r=[banks[e].t()], w=[pt.t()])
                        for s_, (vt, kc0, tb) in enumerate(pair):
                            if tb is None:
                                continue
                            mtog += 1
                            sl = pt[:, s_ * 256:(s_ + 1) * 256].rearrange("p (c q) -> p c q", c=4)
                            self.tt("dve" if (mtog % 2 or NA_NOPOOL) else "pool", sl, sl, TB5[:, tb, :, e, :], ALU.mult,
                                    r=[pt.t(), TB.t()], w=[pt.t()])
                    if hook:
                        hook(a, npair)

            def pv_heads(ui, heads):
                tq0, kts = units[ui]
                ub = ui % 2
                nk = len(kts)
                po = pO[ui % 2]
                for h in heads:
                    e, c = h % 2, h // 2
                    for ki, (vt, kc0, tb) in enumerate(kts):
                        pt = P[ub][ki // 2][e]
                        s_ = ki % 2
                        self.mm(po[0:65, h * 64:(h + 1) * 64], V[:, vt, h, :],
                                pt[:, s_ * 256 + c * 64: s_ * 256 + c * 64 + 64], ki == 0, ki == nk - 1,
                                r=[V.t(), pt.t()], w=[po.t()])

            def finish(ui):
                tq0, kts = units[ui]
                po = pO[ui % 2]
                ob = osb[(ui // 4) % 2]
                j = ui % 4
                self.act(rrow[64:65, 0:512], po[64:65, 0:512], AF.Ln, r=[po.t()], w=[rrow.t()])
                self.act(rrow[64:65, 0:512], rrow[64:65, 0:512], AF.Exp, scale=-1.0, r=[rrow.t()], w=[rrow.t()])
                self.mm(pb[0:64, 0:512], self.ones_f[64:65, 0:64], rrow[64:65, 0:512], True, True, r=[rrow.t()], w=[pb.t()])
                self.cp("act", pbs[0:64, 0:512], pb[0:64, 0:512], r=[pb.t()], w=[pbs.t()])
                self.tt("dve", ob[:, :, j * 64:(j + 1) * 64], po[0:64, 0:512], pbs[0:64, 0:512], ALU.mult,
                        r=[po.t(), pbs.t()], w=[ob.t(j)])
                if j == 3:
                    t0 = tq0 - 192
                    self.dma("sp", self.onaT.h.ap()[:, :, t0:t0 + 256].rearrange("h d t -> d h t"), ob[:],
                             r=[ob.t()], w=[self.onaT.t()])

            phaseA(0)
            nu = len(units)
            for ui in range(nu):
                if ui + 1 < nu:
                    def hook(a, npair, ui=ui):
                        lo, hi = (8 * a) // npair, (8 * (a + 1)) // npair
                        pv_heads(ui, range(lo, hi))
                        if a == 0 and ui > 0:
                            finish(ui - 1)
                    phaseA(ui + 1, hook)
                else:
                    pv_heads(ui, range(8))
                    if ui > 0:
                        finish(ui - 1)
            finish(nu - 1)

    def gqastage(self, l, need_ctx, lim=None):
        with self.stage():
            kT2 = self.sb("gkT", [128, 2, NT], BF16)
            qT = self.sb("gqTz", [128, 8, NT], BF16)
            V = self.sb("gV", [128, NTILE, 2, 65], BF16)
            self.dma("sp", kT2[:], self.gkT2.h.ap().rearrange("g p t -> p g t"), r=[self.gkT2.t()], w=[kT2.t()])
            for h in range(8):
                e_, c_ = h % 2, h // 2
                self.memset("dve", qT[(1 - e_) * 64:(2 - e_) * 64, h, :], 0.0, w=[qT.t((h, 0))])
                self.dma("sp", qT[e_ * 64:(e_ + 1) * 64, h, :], self.gqT.h.ap()[c_ * 128 + e_ * 64:c_ * 128 + (e_ + 1) * 64, :],
                         r=[self.gqT.t()], w=[qT.t((h, 1))])
            self.memset("dve", V[:, :, :, 64:65], 1.0, w=[V.t("ones")])
            for g in range(2):
                self.dma("sp", V[:, :, g, 0:64], self.gv.h.ap()[:, g * 64:(g + 1) * 64].rearrange("(i p) d -> p i d", p=128),
                         r=[self.gv.t()], w=[V.t("d%d" % g)])
            pS = [self.ps("gps%d" % i, [128, 1024], F32) for i in range(2)]
            pO = [self.ps("gpo%d" % i, [128, 512], F32) for i in range(2)]
            pb = self.ps("gpb", [128, 512], F32)
            Pb = [self.sb("gP%d" % i, [128, 1024], BF16) for i in range(3)]
            rrow = self.sb("grrow", [128, 512], F32)
            pbs = self.sb("gpbs", [64, 512], F32)
            osb = [self.sb("gosb%d" % i, [64, 512], BF16) for i in range(2)]
            blocks = []
            if need_ctx:
                blocks.append((0, 256, [0, 1]))
            for qb in range(8):
                blocks.append((256 + qb * 512, 512, list(range(NTILE))))
            n = 0
            ui = 0
            pend = []
            if lim:
                blocks = blocks[:lim]
            for h in range(8 if not lim else 2):
                g, e, c = h // 4, h % 2, h // 2
                for (tq0, N, kts) in blocks:
                    po = pO[ui % 2]
                    ob = osb[ui % 2]
                    ui += 1
                    nk = len(kts)
                    npair = nk // 2
                    LA = 1
                    bufs = {}

                    def qk(i):
                        nonlocal n
                        ps, pt = pS[n % 2], Pb[n % 3]
                        n += 1
                        for a_ in range(2):
                            kt = kts[2 * i + a_]
                            self.mm(ps[:, a_ * 512:a_ * 512 + N], kT2[:, g, kt * 128:(kt + 1) * 128],
                                    qT[:, h, tq0:tq0 + N], True, True, r=[kT2.t(), qT.t()], w=[ps.t(a_)])
                        v2 = lambda ap: ap.rearrange("p (a n) -> p a n", a=2)[:, :, 0:N]
                        self.act(v2(pt[:]), v2(ps[:]), AF.Exp, r=[ps.t()], w=[pt.t()])
                        bufs[i] = pt

                    for i in range(min(LA, npair)):
                        qk(i)
                    for i in range(npair):
                        if i + LA < npair:
                            qk(i + LA)
                        if i == 3 and pend:
                            pend.pop()()
                        pt = bufs.pop(i)
                        for a_ in range(2):
                            self.mm(po[0:65, 0:N], V[:, kts[2 * i + a_], g, :], pt[:, a_ * 512:a_ * 512 + N],
                                    i == 0 and a_ == 0, i == npair - 1 and a_ == 1, r=[V.t(), pt.t()], w=[po.t()])
                    if pend:
                        pend.pop()()

                    def fin(po=po, N=N, ob=ob, h=h, tq0=tq0):
                        self.softmax_finish(po, N, rrow, pb, pbs, ob[:, 0:N], [ob.t()])
                        self.dma("sp", self.ogT.h.ap()[h, :, tq0:tq0 + N], ob[:, 0:N], r=[ob.t()], w=[self.ogT.t()])
                    pend.append(fin)
            if pend:
                pend.pop()()

    def ssdstage(self, l, need_ctx):
        with contextlib.ExitStack() as outer:
            sbo = lambda name, shape, dt: self.sb(name, shape, dt, es=outer)
            xT = sbo("sxT", [128, NTILE, 1024], BF16)
            BT = sbo("sBT", [128, NT], BF16)
            CT0 = sbo("sCT0", [128, NT], BF16)
            CT1 = sbo("sCT1", [128, NT], BF16)
            Btm = sbo("sBtm", [128, NTILE, 128], BF16)
            dt = sbo("sdt", [128, NTILE, 32], F32)
            da = sbo("sda", [128, NTILE, 32], F32)
            with self.stage():
                cw = self.sb("scw", [128, 10, 3], F32)
                cbias = self.sb("scb", [128, 10], F32)
                for k in range(3):
                    self.dma("sp", cw[:, :, k], self.ssm_conv_w.ap()[l, k].rearrange("(c p) -> p c", p=128),
                             r=[], w=[cw.t(k)])
                self.dma("sp", cbias[:], self.ssm_conv_b.ap()[l].rearrange("(c p) -> p c", p=128), r=[], w=[cbias.t()])
                raws = [self.sb("sraw%d" % i, [128, NT], BF16) for i in range(2)]
                acc = self.sb("sacc", [128, NT], F32)
                xF = self.sb("sxF", [128, NT], BF16)
                pT = [self.ps("spT%d" % i, [128, 8, 128], BF16) for i in range(2)]
                xv = self.xbcT.h.ap().rearrange("(c p) t -> p c t", p=128)
                ntr = 0
                for c in range(10):
                    raw = raws[c % 2]
                    self.dma("sp", raw[:], xv[:, c, :], r=[self.xbcT.t()], w=[raw.t()])
                    self.ts("dve", acc[:], raw[:], cw[:, c, 1:2], cbias[:, c:c + 1], ALU.mult, ALU.add,
                            r=[raw.t(), cw.t(), cbias.t()], w=[acc.t()])
                    for (a, b) in ((0, LC), (LC, NT)):
                        self.stt(acc[:, a + 1:b], raw[:, a:b - 1], cw[:, c, 0:1], acc[:, a + 1:b], ALU.mult, ALU.add,
                                 r=[raw.t(), acc.t()], w=[acc.t()])
                        self.stt(acc[:, a:b - 1], raw[:, a + 1:b], cw[:, c, 2:3], acc[:, a:b - 1], ALU.mult, ALU.add,
                                 r=[raw.t(), acc.t()], w=[acc.t()])
                    if c < 9:
                        dstF = xF if c < 8 else BT
                        self.act(dstF[:], acc[:], AF.Silu, r=[acc.t()], w=[dstF.t()])
                        for (ti0, tc) in [(0, 2)] + [(2 + 4 * g, 4) for g in range(8)]:
                            p = pT[ntr % 2]
                            ntr += 1
                            for j in range(tc):
                                ti = ti0 + j
                                self.tr(p[:, j, :], dstF[:, ti * 128:(ti + 1) * 128], self.ident_b[:], r=[dstF.t()], w=[p.t()])
                            if c < 8:
                                self.cp("act" if ntr % 2 else "dve", xT[:, ti0:ti0 + tc, c * 128:(c + 1) * 128], p[:, 0:tc, :],
                                        r=[p.t()], w=[xT.t((c, ti0))])
                            else:
                                self.cp("dve", Btm[:, ti0:ti0 + tc, :], p[:, 0:tc, :], r=[p.t()], w=[Btm.t(ti0)])
                    else:
                        self.act(CT0[:], acc[:], AF.Silu, r=[acc.t()], w=[CT0.t()])
                        self.cp("dve", CT1[:], CT0[:], r=[CT0.t()], w=[CT1.t()])
                        self.memset("dve", CT0[64:128, :], 0.0, w=[CT0.t()])
                        self.memset("dve", CT1[0:64, :], 0.0, w=[CT1.t()])
                dtb = self.sb("sdtb", [128, 32], F32)
                ab = self.sb("sab", [128, 32], F32)
                self.dma("sp", dtb[:], self.ssm_dt_bias.ap()[l:l + 1].rearrange("o a b -> o (a b)").broadcast_to([128, 32]),
                         r=[], w=[dtb.t()])
                self.dma("sp", ab[:], self.ssm_a_log.ap()[l:l + 1].rearrange("o a b -> o (a b)").broadcast_to([128, 32]),
                         r=[], w=[ab.t()])
                self.act(ab[:], ab[:], AF.Exp, r=[ab.t()], w=[ab.t()])
                self.ts("dve", ab[:], ab[:], -1.0, None, ALU.mult, None, r=[ab.t()], w=[ab.t()])
                self.dma("sp", dt[:], self.dtr.h.ap().rearrange("(i p) n -> p i n", p=128), r=[self.dtr.t()], w=[dt.t()])
                self.tt("dve", dt[:], dt[:], dtb[:].unsqueeze(1).broadcast_to([128, NTILE, 32]), ALU.add,
                        r=[dt.t(), dtb.t()], w=[dt.t()])
                self.act(dt[:], dt[:], AF.Exp, r=[dt.t()], w=[dt.t()])
                self.act(dt[:], dt[:], AF.Ln, bias=self.ones_f[:, 0:1], r=[dt.t()], w=[dt.t()])
                self.tt("dve", da[:], dt[:], ab[:].unsqueeze(1).broadcast_to([128, NTILE, 32]), ALU.mult,
                        r=[dt.t(), ab.t()], w=[da.t()])
            with self.stage():
                U = self.sb("sU", [128, 128], F32)
                Lo = self.sb("sLo", [128, 128], F32)
                Ls = self.sb("sLs", [128, 128], F32)
                Us = self.sb("sUs", [128, 128], F32)
                for (mtx, op, sg) in ((U, ALU.is_ge, -1), (Lo, ALU.is_ge, 1), (Ls, ALU.is_gt, 1), (Us, ALU.is_gt, -1)):
                    self.memset("pool", mtx[:], 1.0, w=[mtx.t()])
                    self.S.add("pool", lambda e, mtx=mtx, op=op, sg=sg: e.affine_select(
                        out=mtx[:], in_=mtx[:], pattern=[[-sg, 128]], compare_op=op, fill=0.0, base=0,
                        channel_multiplier=sg), r=[mtx.t()], w=[mtx.t()])
                dsk = self.sb("sdsk", [128, 16], F32)
                nwT = self.sb("snwT", [128, 8], F32)
                self.dma("sp", dsk[:], self.ssm_d.ap()[l:l + 1, :].broadcast_to([128, 16]), r=[], w=[dsk.t()])
                self.dma("sp", nwT[:], self.ssm_norm_w.ap()[l].rearrange("(k p) -> p k", p=128), r=[], w=[nwT.t()])
                LA = self.sb("sLA", [128, 16, 128], F32)
                expD = self.sb("sexpD", [128, 16, 128], F32)
                mcb = self.sb("smcb", [128, 2, 128], F32)
                Mt = [self.sb("sM%d" % i, [128, 16, 128], BF16) for i in range(2)]
                xs = [self.sb("sxs%d" % i, [128, 16, 64], BF16) for i in range(2)]
                xsw = [self.sb("sxsw%d" % i, [128, 16, 64], BF16) for i in range(2)]
                E = self.sb("sE", [128, 16], F32)
                dec = self.sb("sdec", [128, 8], F32)
                H = self.sb("sH", [128, 512], F32)
                Hb = self.sb("sHb", [128, 512], BF16)
                tmpH = self.sb("stmpH", [128, 512], F32)
                yt = self.sb("syt", [128, 1024], F32)
                ys = [self.sb("sy%d" % i, [128, 1024], F32) for i in range(2)]
                yfb = self.sb("syfb", [128, 1024], F32)
                zt = self.sb("szt", [128, 1024], BF16)
                gg = self.sb("sgg", [128, 1024], F32)
                gjunk = self.sb("sgjunk", [128, 1024], BF16)
                ssq = self.sb("sssq", [128, 1], F32)
                ob = self.sb("sob", [128, 1024], BF16)
                ost = self.sb("sost", [128, 8, 512], BF16)
                pD = self.ps("spD", [128, 2048], F32)
                pm = self.ps("spm", [128, 512], F32)
                pm2 = self.ps("spm2", [128, 512], F32)
                pS = self.ps("spS", [128, 512], F32)
                pTo = self.ps("spTo", [128, 8, 128], BF16)
                oT = self.ossmT.h.ap().rearrange("(k p) t -> p k t", p=128)
                decs = [dec, self.sb("sdec2", [128, 8], F32)]
                steps = []
                for d in range(2):
                    order = [0, 1] + list(range(2, NTILE)) if d == 0 else [1, 0] + list(range(NTILE - 1, 1, -1))
                    for oi, ti in enumerate(order):
                        steps.append((d, ti, oi == 0))

                def par(i):
                    d, ti, first = steps[i]
                    A_, Bm_, mask_, wcol = (Ls, U, U, 127) if d == 0 else (Us, Lo, Lo, 0)
                    return dict(d=d, ti=ti, first=first, A_=A_, Bm_=Bm_, mask_=mask_, wcol=wcol,
                                do_y=(need_ctx or ti >= 2), cols=slice(ti * 128, (ti + 1) * 128),
                                dav=da[:, ti, d * 16:(d + 1) * 16], dtv=dt[:, ti, d * 16:(d + 1) * 16],
                                M=Mt[i % 2], xs_=xs[i % 2], xsw_=xsw[i % 2], y=ys[i % 2], dec=decs[i % 2],
                                x3=xT[:, ti, :].rearrange("p (h q) -> p h q", h=16))

                def stepA(i):
                    p = par(i)
                    A_, Bm_, dav, dtv, xs_, xsw_, dec_ = p["A_"], p["Bm_"], p["dav"], p["dtv"], p["xs_"], p["xsw_"], p["dec"]
                    self.tt(PENG, LA[:], A_[:].unsqueeze(1).broadcast_to([128, 16, 128]),
                            dav.unsqueeze(2).broadcast_to([128, 16, 128]), ALU.mult, r=[A_.t(), da.t()], w=[LA.t()])
                    for h in range(16):
                        self.mm(pD[:, h * 128:(h + 1) * 128], LA[:, h, :], Bm_[:], True, True,
                                r=[LA.t(), Bm_.t()], w=[pD.t(h // 4)])
                    self.mm(pm2[:, 0:16], Bm_[:], dav, True, True, r=[Bm_.t(), da.t()], w=[pm2.t()])
                    self.mm(pm2[:, 16:32], self.ones_f[:], dav, True, True, r=[da.t()], w=[pm2.t()])
                    for q in range(4):
                        self.act(expD[:, 4 * q:4 * q + 4, :], pD[:, q * 512:(q + 1) * 512].rearrange("p (h t) -> p h t", h=4),
                                 AF.Exp, r=[pD.t(q)], w=[expD.t(q)])
                    self.act(E[:], pm2[:, 0:16], AF.Exp, r=[pm2.t()], w=[E.t()])
                    self.act(dec_[0:64, :], pm2[0:64, 16:24], AF.Exp, r=[pm2.t()], w=[dec_.t(0)])
                    self.act(dec_[64:128, :], pm2[64:128, 24:32], AF.Exp, r=[pm2.t()], w=[dec_.t(1)])

                def stepB(i):
                    p = par(i)
                    if p["first"]:
                        self.memset("dve", H[:], 0.0, w=[H.t()])
                        self.memset("dve", Hb[:], 0.0, w=[Hb.t()])
                    self.tt(PENG, p["xs_"][:], p["x3"], p["dtv"].unsqueeze(2).broadcast_to([128, 16, 64]), ALU.mult,
                            r=[xT.t(), dt.t()], w=[p["xs_"].t()])
                    if p["do_y"]:
                        stepB_y(p)
                    self.tt(PENG, p["xsw_"][:], p["xs_"][:], expD[:, :, p["wcol"]].unsqueeze(2).broadcast_to([128, 16, 64]),
                            ALU.mult, r=[p["xs_"].t(), expD.t()], w=[p["xsw_"].t()])

                def stepB_y(p):
                    cols, mask_, M, xs_, y = p["cols"], p["mask_"], p["M"], p["xs_"], p["y"]
                    for g, CTg in enumerate((CT0, CT1)):
                        self.mm(pm[:, g * 128:(g + 1) * 128], BT[:, cols], CTg[:, cols], True, True,
                                r=[BT.t(), CTg.t()], w=[pm.t()])
                    self.tt("dve", mcb[:], pm[:, 0:256].rearrange("p (g t) -> p g t", g=2),
                            mask_[:].unsqueeze(1).broadcast_to([128, 2, 128]), ALU.mult,
                            r=[pm.t(), mask_.t()], w=[mcb.t()])
                    for g in range(2):
                        self.tt("dve", M[:, 8 * g:8 * g + 8, :], expD[:, 8 * g:8 * g + 8, :],
                                mcb[:, g:g + 1, :].broadcast_to([128, 8, 128]), ALU.mult,
                                r=[expD.t(), mcb.t()], w=[M.t(g)])
                    for h in range(16):
                        self.mm(pD[:, h * 64:(h + 1) * 64], M[:, h, :], xs_[:, h, :], True, True,
                                r=[M.t(), xs_.t()], w=[pD.t(h // 8)])
                    for g, CTg in enumerate((CT0, CT1)):
                        self.mm(pD[:, 1024 + g * 512:1024 + (g + 1) * 512], CTg[:, cols], Hb[:], True, True,
                                r=[CTg.t(), Hb.t()], w=[pD.t(2 + g)])
                    for g in range(2):
                        self.tt("dve", yt[:, g * 512:(g + 1) * 512].rearrange("p (h q) -> p h q", h=8),
                                pD[:, 1024 + g * 512:1024 + (g + 1) * 512].rearrange("p (h q) -> p h q", h=8),
                                E[:, 8 * g:8 * g + 8].unsqueeze(2).broadcast_to([128, 8, 64]), ALU.mult,
                                r=[pD.t(2 + g), E.t()], w=[yt.t(g)])
                    self.tt("dve", y[:], yt[:], pD[:, 0:1024], ALU.add, r=[yt.t(), pD.t(0), pD.t(1)], w=[y.t()])

                def stepC(i):
                    p = par(i)
                    ti, xsw_, dec_ = p["ti"], p["xsw_"], p["dec"]
                    for g in range(2):
                        self.mm(pS[g * 64:(g + 1) * 64, :], Btm[:, ti, g * 64:(g + 1) * 64],
                                xsw_[:, 8 * g:8 * g + 8, :].rearrange("p h q -> p (h q)"), True, True,
                                r=[Btm.t(), xsw_.t()], w=[pS.t(g)])
                    self.tt("dve", tmpH[:].rearrange("p (h q) -> p h q", h=8), H[:].rearrange("p (h q) -> p h q", h=8),
                            dec_[:].unsqueeze(2).broadcast_to([128, 8, 64]), ALU.mult, r=[H.t(), dec_.t()], w=[tmpH.t()])
                    self.tt("dve", H[:], tmpH[:], pS[:], ALU.add, r=[tmpH.t(), pS.t()], w=[H.t()])
                    self.cp("act", Hb[:], H[:], r=[H.t()], w=[Hb.t()])

                def stepD(i):
                    p = par(i)
                    if not p["do_y"]:
                        return
                    d, ti, y, x3 = p["d"], p["ti"], p["y"], p["x3"]
                    rows = self.yf.h.ap()[ti * 128:(ti + 1) * 128, :]
                    if d == 0:
                        self.dma("sp", rows, y[:], r=[y.t()], w=[self.yf.t(ti)])
                        return
                    self.dma("sp", yfb[:], rows, r=[self.yf.t(ti)], w=[yfb.t()])
                    self.dma("sp", zt[:], self.zs.h.ap()[ti * 128:(ti + 1) * 128, :], r=[self.zs.t()], w=[zt.t()])
                    self.tt("dve", y[:], y[:], yfb[:], ALU.add, r=[y.t(), yfb.t()], w=[y.t()])
                    self.tt(PENG, gg[:].rearrange("p (h q) -> p h q", h=16), x3,
                            dsk[:].unsqueeze(2).broadcast_to([128, 16, 64]), ALU.mult, r=[xT.t(), dsk.t()], w=[gg.t()])
                    self.tt("dve", y[:], y[:], gg[:], ALU.add, r=[y.t(), gg.t()], w=[y.t()])
                    self.tt("dve", gg[:], y[:], zt[:], ALU.mult, r=[y.t(), zt.t()], w=[gg.t()])
                    self.act(gjunk[:], gg[:], AF.Square, r=[gg.t()], w=[gjunk.t(), ssq.t()], accum_out=ssq[:])
                    self.act(ssq[:], ssq[:], AF.Sqrt, bias=self.eps_t[:, 0:1], scale=1.0 / 1024, r=[ssq.t()], w=[ssq.t()])
                    self.S.add("dve", lambda e: e.reciprocal(out=ssq[:], in_=ssq[:]), r=[ssq.t()], w=[ssq.t()])
                    self.ts("dve", ob[:], gg[:], ssq[:, 0:1], None, ALU.mult, None, r=[gg.t(), ssq.t()], w=[ob.t()])
                    for k in range(8):
                        self.tr(pTo[:, k, :], ob[:, k * 128:(k + 1) * 128], self.ident_b[:], r=[ob.t()], w=[pTo.t()])
                    j = ti if ti < 2 else (ti - 2) % 4
                    self.tt("dve", ost[:, :, j * 128:(j + 1) * 128], pTo[:], nwT[:].unsqueeze(2).broadcast_to([128, 8, 128]),
                            ALU.mult, r=[pTo.t(), nwT.t()], w=[ost.t(j)])
                    if j == 0:
                        t0, n = (0, 256) if ti < 2 else (ti * 128, 512)
                        self.dma("sp", oT[:, :, t0:t0 + n], ost[:, :, 0:n], r=[ost.t()], w=[self.ossmT.t()])

                stepA(0)
                for i in range(len(steps)):
                    stepB(i)
                    if i + 1 < len(steps):
                        stepA(i + 1)
                    stepC(i)
                    stepD(i)

    def mergestage(self, l, need_ctx):
        with self.stage():
            wna = self.sb("mwna", [64, 8, 1024], BF16)
            wgq = self.sb("mwgq", [64, 8, 1024], BF16)
            wss = self.sb("mwss", [128, 8, 1024], BF16)
            wo = self.sb("mwo", [128, 8, 1024], BF16)
            self.dma("pool", wna[:], self.w_out_na.ap()[l].rearrange("(h d) n -> d h n", d=64), r=[], w=[wna.t()])
            self.dma("pool", wgq[:], self.w_out_gqa.ap()[l].rearrange("(h d) n -> d h n", d=64), r=[], w=[wgq.t()])
            self.dma("pool", wss[:], self.w_out_ssm.ap()[l].rearrange("(k p) n -> p k n", p=128), r=[], w=[wss.t()])
            self.dma("pool", wo[:], self.w_o.ap()[l].rearrange("(k p) n -> p k n", p=128), r=[], w=[wo.t()])
            NB = 512
            ona = [self.sb("mona%d" % i, [64, 8, NB], BF16) for i in range(2)]
            ogq = [self.sb("mogq%d" % i, [64, 8, NB], BF16) for i in range(2)]
            oss = [self.sb("moss%d" % i, [128, 8, NB], BF16) for i in range(2)]
            gts = [self.sb("mgt%d" % i, [128, 24, NB], BF16) for i in range(2)]
            hbs = [self.sb("mhb%d" % i, [128, 8, NB], F32) for i in range(1)] * 2
            yTs = [self.sb("myT%d" % i, [128, 8, NB], BF16) for i in range(1)] * 2
            t1 = [self.sb("mt1_%d" % i, [128, NB], F32) for i in range(1)] * 2
            t2 = [self.sb("mt2_%d" % i, [128, NB], F32) for i in range(1)] * 2
            t3 = [self.sb("mt3_%d" % i, [128, NB], F32) for i in range(1)] * 2
            pA = [self.ps("mpA%d" % i, [128, 512], F32) for i in range(2)]
            pB = [self.ps("mpB%d" % i, [128, 512], F32) for i in range(2)]
            pC = [self.ps("mpC%d" % i, [128, 512], F32) for i in range(2)]
            pM = [self.ps("mpM%d" % i, [128, 512], F32) for i in range(2)]
            hv = self.hcur.h.ap().rearrange("(k p) t -> p k t", p=128)
            blocks = ([(0, LC, 1)] if need_ctx else []) + [(LC + NB * i, NB, 0) for i in range(SEQ // NB)]
            mv = self.modv[l]
            n = 0
            for bi, (t0, N, j) in enumerate(blocks):
                a, g_, s_, gt, hb, yT = ona[bi % 2], ogq[bi % 2], oss[bi % 2], gts[bi % 2], hbs[bi % 2], yTs[bi % 2]
                a, g_, s_, gt, hb, yT = (Sl(a, N), Sl(g_, N), Sl(s_, N), Sl(gt, N), Sl(hb, N), Sl(yT, N))
                self.dma("sp", a[:], self.onaT.h.ap()[:, :, t0:t0 + N].rearrange("h d t -> d h t"), r=[self.onaT.t()], w=[a.t()])
                self.dma("sp", g_[:], self.ogT.h.ap()[:, :, t0:t0 + N].rearrange("h d t -> d h t"), r=[self.ogT.t()], w=[g_.t()])
                self.dma("sp", s_[:], self.ossmT.h.ap()[:, t0:t0 + N].rearrange("(k p) t -> p k t", p=128),
                         r=[self.ossmT.t()], w=[s_.t()])
                self.dma("sp", gt[:], self.gatesT.h.ap()[:, t0:t0 + N].rearrange("(c p) t -> p c t", p=128),
                         r=[self.gatesT.t()], w=[gt.t()])
                self.dma("sp", hb[:], hv[:, :, t0:t0 + N], r=[self.hcur.t()], w=[hb.t()])
                for oc in range(8):
                    n += 1
                    A_, B_, C_ = pA[n % 2], pB[n % 2], pC[n % 2]
                    u1, u2, u3 = Sl(t1[n % 2], N), Sl(t2[n % 2], N), Sl(t3[n % 2], N)
                    cs = slice(oc * 128, (oc + 1) * 128)
                    for h in range(8):
                        self.mm(A_[:, 0:N], wna[:, h, cs], a[:, h, :], h == 0, h == 7, r=[wna.t(), a.t()], w=[A_.t()])
                    for k in range(8):
                        self.mm(B_[:, 0:N], wss[:, k, cs], s_[:, k, :], k == 0, k == 7, r=[wss.t(), s_.t()], w=[B_.t()])
                    for h in range(8):
                        self.mm(C_[:, 0:N], wgq[:, h, cs], g_[:, h, :], h == 0, h == 7, r=[wgq.t(), g_.t()], w=[C_.t()])
                    self.tt("dve", u1[:], A_[:, 0:N], gt[:, oc, :], ALU.mult, r=[A_.t(), gt.t()], w=[u1.t()])
                    self.tt("dve", u2[:], B_[:, 0:N], gt[:, 8 + oc, :], ALU.mult, r=[B_.t(), gt.t()], w=[u2.t()])
                    self.tt("dve", u3[:], C_[:, 0:N], gt[:, 16 + oc, :], ALU.mult, r=[C_.t(), gt.t()], w=[u3.t()])
                    self.tt("dve", u1[:], u1[:], u2[:], ALU.add, r=[u1.t(), u2.t()], w=[u1.t()])
                    self.tt("dve", yT[:, oc, :], u1[:], u3[:], ALU.add, r=[u1.t(), u3.t()], w=[yT.t(oc)])
                for oc in range(8):
                    n += 1
                    M_ = pM[n % 2]
                    cs = slice(oc * 128, (oc + 1) * 128)
                    for k in range(8):
                        self.mm(M_[:, 0:N], wo[:, k, cs], yT[:, k, :], k == 0, k == 7, r=[wo.t(), yT.t()], w=[M_.t()])
                    self.stt(hb[:, oc, :], M_[:, 0:N], mv[:, 16 + oc, j:j + 1], hb[:, oc, :], ALU.mult, ALU.add,
                             r=[M_.t(), hb.t(oc)], w=[hb.t(oc)])
                self.dma("sp", hv[:, :, t0:t0 + N], hb[:], r=[hb.t()], w=[self.hcur.t()])

    def ffnstage(self, l, need_ctx):
        src, dst = self.hcur, (self.hT2 if self.hcur is self.hT else self.hT)
        with self.stage():
            wup = self.sb("fwup", [128, 8, 2 * FH], BF16)
            wdn = self.sb("fwdn", [128, 22, 1024], BF16)
            uv = self.ffn_w_up.ap()[l].rearrange("(k p) n -> p k n", p=128)
            for c in (0, 5, 6, 1, 7, 2, 8, 3, 9, 4, 10):
                self.dma("pool", wup[:, :, c * 512:(c + 1) * 512], uv[:, :, c * 512:(c + 1) * 512], r=[], w=[wup.t(c)])
            dv = self.ffn_w_down.ap()[l].rearrange("(c p) n -> p c n", p=128)
            for c in range(2):
                self.dma("pool", wdn[:, c * 11:(c + 1) * 11, :], dv[:, c * 11:(c + 1) * 11, :], r=[], w=[wdn.t(c)])
            fcw = self.sb("ffcw", [128, 44, 3], F32)
            fcb = self.sb("ffcb", [128, 44], F32)
            for k in range(3):
                self.dma("sp", fcw[:, :, k], self.ffn_conv_w.ap()[l, k].rearrange("(c p) -> p c", p=128), r=[], w=[fcw.t(k)])
            self.dma("sp", fcb[:], self.ffn_conv_b.ap()[l].rearrange("(c p) -> p c", p=128), r=[], w=[fcb.t()])
            NB = 384
            NH = NB + 2
            hbs = [self.sb("fhb%d" % i, [128, 8, NH], F32) for i in range(2)]
            tmp = [self.sb("ftmp%d" % i, [128, NH], F32) for i in range(2)]
            rs = self.sb("frs", [128, NH], F32)
            xns = [self.sb("fxn%d" % i, [128, 8, NH], BF16) for i in range(2)]
            ua = [self.sb("fua%d" % i, [128, NB], F32) for i in range(2)]
            ub = [self.sb("fub%d" % i, [128, NB], F32) for i in range(2)]
            sa = [self.sb("fsa%d" % i, [128, NB], BF16) for i in range(2)]
            t0s = [self.sb("ft0_%d" % i, [128, NB], F32) for i in range(4)]
            actb = self.sb("factb", [128, 22, NB], BF16)

            class SqView:
                name = actb.name

                def t(self_, slot=None):
                    return actb.t()

                def __getitem__(self_, k):
                    v = actb[:].rearrange("p c n -> p (c n)")[:, 0:8 * NH].rearrange("p (k n) -> p k n", k=8)
                    return v[k]
            sq = SqView()
            ssp = self.ps("fssp", [128, 512], F32)
            pa = [self.ps("fpa%d" % i, [128, 512], F32) for i in range(2)]
            pb_ = [self.ps("fpb%d" % i, [128, 512], F32) for i in range(2)]
            pd = [self.ps("fpd%d" % i, [128, 512], F32) for i in range(2)]
            sv = src.h.ap().rearrange("(k p) t -> p k t", p=128)
            dvw = dst.h.ap().rearrange("(k p) t -> p k t", p=128)
            mv = self.modv[l]
            blocks = [(0, LC, 1, 0, LC)] if need_ctx else []
            t_ = LC
            while t_ < NT:
                nb_ = min(NB, NT - t_)
                blocks.append((t_, nb_, 0, LC, NT))
                t_ += nb_
            n = 0

            def prep(bi):
                t0, nb, j, s0, s1 = blocks[bi]
                nh = nb + 2
                hb, xn = hbs[bi % 2], xns[bi % 2]
                lo, hi = max(t0 - 1, s0), min(t0 + nb + 1, s1)
                a0 = lo - (t0 - 1)
                if lo > t0 - 1:
                    self.memset("dve", hb[:, :, 0:1], 0.0, w=[hb.t()])
                if hi < t0 + nb + 1:
                    self.memset("dve", hb[:, :, nh - 1:nh], 0.0, w=[hb.t()])
                self.dma("sp", hb[:, :, a0:a0 + hi - lo], sv[:, :, lo:hi], r=[src.t()], w=[hb.t()])
                self.norm_block(hb, nh, self.A2[l], mv, 24, j, lambda k, xn=xn, nh=nh: xn[:, k, 0:nh], [xn.t()], xn, ssp, rs, tmp)
                if lo > t0 - 1:
                    self.memset("dve", xn[:, :, 0:1], 0.0, w=[xn.t()])
                if hi < t0 + nb + 1:
                    self.memset("dve", xn[:, :, nh - 1:nh], 0.0, w=[xn.t()])

            prep(0)
            for bi, (t0, nb, j, s0, s1) in enumerate(blocks):
                nh = nb + 2
                hb, xn = hbs[bi % 2], xns[bi % 2]
                for c in range(22):
                    n += 1
                    A_, B_ = pa[n % 2], pb_[n % 2]
                    va, vb, vs = ua[n % 2], ub[n % 2], sa[n % 2]
                    for (P_, cc, v) in ((A_, c, va), (B_, 22 + c, vb)):
                        for k in range(8):
                            self.mm(P_[:, 0:nh], wup[:, k, cc * 128:(cc + 1) * 128], xn[:, k, 0:nh], k == 0, k == 7,
                                    r=[wup.t(cc // 4), xn.t()], w=[P_.t()])
                        self.act(v[:, 0:nb], P_[:, 1:nb + 1], AF.Identity, scale=fcw[:, cc, 1:2], bias=fcb[:, cc:cc + 1],
                                 r=[P_.t(), fcw.t(), fcb.t()], w=[v.t()])
                        t0_ = t0s[(2 * n + (cc >= 22)) % 4]
                        self.act(t0_[:, 0:nb], P_[:, 0:nb], AF.Identity, scale=fcw[:, cc, 0:1], bias=self.zero_t[:, 0:1], r=[P_.t(), fcw.t()], w=[t0_.t()])
                        self.tt("dve", v[:, 0:nb], v[:, 0:nb], t0_[:, 0:nb], ALU.add, r=[v.t(), t0_.t()], w=[v.t()])
                        self.stt(v[:, 0:nb], P_[:, 2:nb + 2], fcw[:, cc, 2:3], v[:, 0:nb], ALU.mult, ALU.add, r=[P_.t(), v.t()], w=[v.t()])
                    self.act(vs[:, 0:nb], va[:, 0:nb], AF.Silu, r=[va.t()], w=[vs.t()])
                    self.tt("dve", actb[:, c, 0:nb], vs[:, 0:nb], vb[:, 0:nb], ALU.mult, r=[vs.t(), vb.t()], w=[actb.t()])
                if bi + 1 < len(blocks):
                    prep(bi + 1)
                for oc in range(8):
                    n += 1
                    D_ = pd[n % 2]
                    for c in range(22):
                        self.mm(D_[:, 0:nb], wdn[:, c, oc * 128:(oc + 1) * 128], actb[:, c, 0:nb], c == 0, c == 21,
                                r=[wdn.t(c // 11), actb.t()], w=[D_.t()])
                    self.stt(hb[:, oc, 1:nb + 1], D_[:, 0:nb], mv[:, 40 + oc, j:j + 1], hb[:, oc, 1:nb + 1], ALU.mult, ALU.add,
                             r=[D_.t(), hb.t()], w=[hb.t()])
                self.dma("sp", dvw[:, :, t0:t0 + nb], hb[:, :, 1:nb + 1], r=[hb.t()], w=[dst.t()])
        self.hcur = dst

    def finalstage(self):
        with self.stage():
            fw = self.sb("zfw", [128, 8], F32)
            self.dma("sp", fw[:], self.final_norm_w.ap().rearrange("(k p) -> p k", p=128), r=[], w=[fw.t()])
            NB = 256
            hbs = [self.sb("zhb%d" % i, [128, 8, NB], F32) for i in range(2)]
            sq = self.sb("zsq", [128, 8, NB], BF16)
            rs = self.sb("zrs", [128, NB], F32)
            tmp = self.sb("ztmp", [128, 8, NB], F32)
            osb = [self.sb("zosb%d" % i, [128, 1024], F32) for i in range(2)]
            ssp = self.ps("zssp", [128, 512], F32)
            pT = [self.ps("zpT%d" % i, [128, 1024], F32) for i in range(2)]
            sv = self.hcur.h.ap().rearrange("(k p) t -> p k t", p=128)
            n = 0
            for bi in range(SEQ // NB):
                t0 = LC + bi * NB
                hb = hbs[bi % 2]
                self.dma("sp", hb[:], sv[:, :, t0:t0 + NB], r=[self.hcur.t()], w=[hb.t()])
                self.act(sq[:], hb[:], AF.Square, r=[hb.t()], w=[sq.t()])
                for k in range(8):
                    self.mm(ssp[:, 0:NB], self.ones_b[:], sq[:, k, :], k == 0, k == 7, r=[sq.t()], w=[ssp.t()])
                self.act(rs[:], ssp[:, 0:NB], AF.Sqrt, bias=self.eps_t[:, 0:1], scale=1.0 / D, r=[ssp.t()], w=[rs.t()])
                self.S.add("dve", lambda e: e.reciprocal(out=rs[:], in_=rs[:]), r=[rs.t()], w=[rs.t()])
                self.tt("dve", tmp[:], hb[:], rs[:].unsqueeze(1).broadcast_to([128, 8, NB]), ALU.mult, r=[hb.t(), rs.t()], w=[tmp.t()])
                self.tt("dve", tmp[:], tmp[:], fw[:].unsqueeze(2).broadcast_to([128, 8, NB]), ALU.mult, r=[tmp.t(), fw.t()], w=[tmp.t()])
                for tt_ in range(NB // 128):
                    n += 1
                    p, ob = pT[n % 2], osb[n % 2]
                    for k in range(8):
                        self.tr(p[:, k * 128:(k + 1) * 128], tmp[:, k, tt_ * 128:(tt_ + 1) * 128], self.ident_f[:], r=[tmp.t()], w=[p.t()])
                    self.cp("act" if n % 2 else "dve", ob[:], p[:], r=[p.t()], w=[ob.t()])
                    r0 = bi * NB + tt_ * 128
                    self.dma("sp", self.out.h.ap()[r0:r0 + 128, :], ob[:], r=[ob.t()], w=[self.out.t()])

    def build_all(self):
        self.declare_io()
        self.consts()
        self.prologue()
        self.modstage()
        for l in range(DEPTH):
            need_ctx = l < DEPTH - 1
            with contextlib.ExitStack() as outer:
                self.instage(l, outer)
            self.nastage(l, need_ctx)
            self.gqastage(l, need_ctx)
            self.ssdstage(l, need_ctx)
            self.mergestage(l, need_ctx)
            self.ffnstage(l, need_ctx)
        self.finalstage()

    def dump(self, name, src_ap, shape, dt, r):
        d = self.dram(name, shape, dt, out=True)
        self.dma("sp", d.h.ap(), src_ap, r=r, w=[d.t()])
        return d


NA_TABLES = [(dr, True, True) for dr in range(14)] + [(2, False, True), (10, True, False)]


def host_tables(na_rel_bias):
    L = na_rel_bias.shape[0]
    p = np.arange(128)
    half, kc = p // 64, p % 64
    qc = np.arange(64)
    dcol = np.clip(kc[:, None] - qc[None, :] + 15, 0, 30)
    win0 = np.clip(qc - 8, 0, 48)
    ok = (kc[:, None] >= win0[None, :]) & (kc[:, None] < win0[None, :] + 16)
    bias_g = np.zeros((L, 16, 128, 8, 64), np.float32)
    mask = np.zeros((16, 128, 8, 64), np.float32)
    for tb, (dr, v0, v1) in enumerate(NA_TABLES):
        drow = dr + half
        g = na_rel_bias[:, :, drow[:, None], dcol]
        bias_g[:, tb] = np.transpose(g, (0, 2, 1, 3))
        rv = np.where(half == 0, v0, v1)
        mask[tb] = (ok & rv[:, None])[:, None, :].astype(np.float32)
    n_freq = 16
    inv_freq = (10000.0 ** (-np.arange(n_freq, dtype=np.float32) / n_freq)).astype(np.float32)
    t = np.arange(SEQ)
    row = (t // GW).astype(np.float32)
    col = (t % GW).astype(np.float32)
    ang = np.concatenate([row[:, None] * inv_freq, col[:, None] * inv_freq], axis=-1).astype(np.float32)
    rope = np.stack([np.cos(ang), np.sin(ang)], axis=1).astype(np.float32)
    return bias_g.reshape(L, 16, 128, 512), mask.reshape(16, 128, 512), rope


def make_in_maps(inputs, n_cores=8):
    f = lambda a: np.ascontiguousarray(np.asarray(a, dtype=np.float32))
    bias_g, mask, rope = host_tables(f(inputs["na_rel_bias"]))
    shared = {k: f(v) for k, v in inputs.items() if k not in ("x", "c", "ctx", "na_rel_bias")}
    shared["na_bias_g"] = bias_g
    shared["na_mask"] = mask
    shared["rope"] = rope
    x, c, ctx = f(inputs["x"]), f(inputs["c"]), f(inputs["ctx"])
    maps = []
    for b in range(n_cores):
        m = dict(shared)
        m["x"], m["c"], m["ctx"] = x[b], c[b], ctx[b]
        maps.append(m)
    return maps


_CACHE = {}


def kernel(**inputs):
    n_cores = 8
    if "kb" not in _CACHE:
        kb = KB()
        kb.build_all()
        _CACHE["kb"] = kb
    kb = _CACHE["kb"]
    maps = make_in_maps(inputs, n_cores)
    used = set(kb.din.keys())
    maps = [{k: v for k, v in m.items() if k in used} for m in maps]
    res = run_bass_kernel_spmd(kb.nc, maps, core_ids=list(range(n_cores)))
    return np.stack([np.asarray(r["out"], dtype=np.float32) for r in res.results], axis=0)
```

```python
import numpy as np
import ml_dtypes
import concourse.bass as bass
import concourse.mybir as mybir
from concourse.bass_utils import run_bass_kernel_spmd

F32, BF16 = mybir.dt.float32, mybir.dt.bfloat16
AF = mybir.ActivationFunctionType
ALU = mybir.AluOpType
AX = mybir.AxisListType

D = 1024
SEQ = 4096
LC = 256
NT = SEQ + LC
NTILE = NT // 128
DEPTH = 2
GW = 64
EPS = 1e-6
DIN = 7712
FH = 2816
C_NAQ, C_NAK, C_NAV, C_Z, C_XBC, C_DT, C_GQ, C_GK, C_GV, C_GATE = 0, 512, 1024, 1536, 2560, 3840, 3872, 4384, 4512, 4640


class Op:
    __slots__ = ("eng", "fn", "deps", "dma", "sem", "val", "flag", "semi")


class Sched:
    COMPUTE = ("pe", "act", "dve", "pool")

    def __init__(self, nc):
        self.nc = nc
        self.csem = {e: nc.alloc_semaphore(name="cs_" + e) for e in self.COMPUTE}
        self.ccnt = {e: 0 for e in self.COMPUTE}
        self.dpool = {"sp": [nc.alloc_semaphore(name="dsp%d" % i) for i in range(4)],
                      "pool": [nc.alloc_semaphore(name="dpl%d" % i) for i in range(2)]}
        self.duse = {q: [0] * len(v) for q, v in self.dpool.items()}
        self.drr = {q: 0 for q in self.dpool}
        self.waited = {e: {} for e in ("pe", "act", "dve", "pool", "sp")}
        self.nops = 0
        self.psum_names = set()
        self.begin()

    def begin(self):
        self.ops = []
        self.trk = {}
        self.dlast = {}

    def _entries(self, tok, create):
        name, slot = tok
        d = self.trk.setdefault(name, {})
        if slot is None:
            if create and None not in d:
                d[None] = [None, []]
            return list(d.values())
        out = []
        if None in d:
            out.append(d[None])
        if slot not in d and create:
            d[slot] = [None, []]
        if slot in d:
            out.append(d[slot])
        return out

    def add(self, eng, fn, r=(), w=(), dma=False):
        op = Op()
        op.eng, op.fn, op.dma, op.flag, op.sem, op.val = eng, fn, dma, False, None, 0
        idx = len(self.ops)
        deps = set()

        def want(d, kind):
            if d is None:
                return
            o = self.ops[d]
            if not o.dma and not dma and o.eng == eng:
                if eng == "pe" or kind != "raw":
                    return
            deps.add(d)

        for tok in r:
            tok = tok if isinstance(tok, tuple) else (tok, None)
            for e in self._entries(tok, True):
                want(e[0], "raw")
                if tok[0] in self.psum_names:
                    for rd in e[1]:
                        want(rd, "rar")
        for tok in w:
            tok = tok if isinstance(tok, tuple) else (tok, None)
            for e in self._entries(tok, True):
                want(e[0], "waw")
                for rd in e[1]:
                    want(rd, "war")
        for tok in r:
            tok = tok if isinstance(tok, tuple) else (tok, None)
            name, slot = tok
            self._entries(tok, True)
            tgt = self.trk[name][slot]
            if dma:
                tgt[1].append(idx)
            else:
                tgt[1][:] = [x for x in tgt[1] if self.ops[x].dma or self.ops[x].eng != eng]
                tgt[1].append(idx)
        for tok in w:
            tok = tok if isinstance(tok, tuple) else (tok, None)
            name, slot = tok
            d = self.trk[name]
            if slot is None:
                d.clear()
                d[None] = [idx, []]
            else:
                d[slot] = [idx, []]
        if dma:
            q = eng
            j = self.drr[q]
            self.drr[q] = (j + 1) % len(self.dpool[q])
            op.semi = j
            prev = self.dlast.get((q, j))
            if prev is not None:
                deps.add(prev)
            self.dlast[(q, j)] = idx
        op.deps = deps
        self.ops.append(op)
        return idx

    def emit(self):
        nc = self.nc
        ops = self.ops
        for op in ops:
            for d in op.deps:
                ops[d].flag = True
        for op in ops:
            if op.dma:
                q = op.eng
                self.duse[q][op.semi] += 1
                op.sem = self.dpool[q][op.semi]
                op.val = 16 * self.duse[q][op.semi]
            elif op.flag:
                self.ccnt[op.eng] += 1
                op.sem = self.csem[op.eng]
                op.val = self.ccnt[op.eng]
        self.nops += len(ops)
        sched = self

        def run(engname, eng):
            wd = sched.waited[engname]
            for op in ops:
                if op.eng != engname:
                    continue
                need = {}
                for d in op.deps:
                    o = ops[d]
                    key = id(o.sem)
                    if key not in need or need[key][1] < o.val:
                        need[key] = (o.sem, o.val)
                lst = []
                for key, (s, v) in need.items():
                    if wd.get(key, 0) >= v:
                        continue
                    wd[key] = v
                    lst.append((s, v))
                if op.dma or len(lst) > 1:
                    extra = lst if op.dma else lst[:-1]
                    for s, v in extra:
                        eng.wait_ge(s, v)
                    lst = [] if op.dma else lst[-1:]
                ins = op.fn(eng)
                if lst:
                    ins._wait_ge(lst[0][0], lst[0][1])
                if op.dma:
                    ins.then_inc(op.sem, 16)
                elif op.flag:
                    ins.then_inc(op.sem, 1)
            if engname in sched.dpool:
                for j, s in enumerate(sched.dpool[engname]):
                    v = 16 * sched.duse[engname][j]
                    if v and wd.get(id(s), 0) < v:
                        wd[id(s)] = v
                        eng.wait_ge(s, v)

        used = set(op.eng for op in ops)
        with nc.Block() as block:
            if "pe" in used:
                @block.tensor
                def _(e):
                    run("pe", e)
            if "act" in used:
                @block.scalar
                def _(e):
                    run("act", e)
            if "dve" in used:
                @block.vector
                def _(e):
                    run("dve", e)
            if "pool" in used:
                @block.gpsimd
                def _(e):
                    run("pool", e)
            if "sp" in used:
                @block.sync
                def _(e):
                    run("sp", e)
        self.begin()


class Buf:
    def __init__(self, name, h):
        self.name, self.h = name, h

    def t(self, slot=None):
        return (self.name, slot)

    def __getitem__(self, k):
        return self.h[k]


class Sl:
    def __init__(self, buf, n):
        self.b, self.n, self.name = buf, n, buf.name

    def t(self, slot=None):
        return self.b.t(slot)

    def __getitem__(self, k):
        if not isinstance(k, tuple):
            k = (k,)
        nd = len(self.b.h.shape)
        k = tuple(k) + (slice(None),) * (nd - len(k))
        last = k[-1]
        if isinstance(last, slice) and last == slice(None):
            k = k[:-1] + (slice(0, self.n),)
        return self.b.h[k]


def rap(buf, row, p0, npart, off, dims):
    return bass.AP(buf.h, p0 * row + off, [[row, npart]] + [list(d) for d in dims])


import contextlib

NA_NOPOOL = True
PENG = "dve"
TOKBLKS = [(0, 256, 1)] + [(256 + 512 * i, 512, 0) for i in range(8)]


class KB:
    def __init__(self, dbg=()):
        self.dbg = set(dbg)
        nc = self.nc = bass.Bass("TRN2", target_bir_lowering=False)
        self.S = Sched(nc)
        self.ges = contextlib.ExitStack()
        self.es = None
        self.din = {}
        self._uid = 0

    def inp(self, name, shape, dt=F32):
        t = self.nc.dram_tensor(name, list(shape), dt, kind="ExternalInput")
        self.din[name] = t
        return t

    def dram(self, name, shape, dt, out=False):
        kind = "ExternalOutput" if (out or name in self.dbg) else "Internal"
        return Buf(name, self.nc.dram_tensor(name, list(shape), dt, kind=kind))

    def sb(self, name, shape, dt, glob=False, es=None):
        es = es if es is not None else (self.ges if glob else self.es)
        self._uid += 1
        name = "%s_u%d" % (name, self._uid)
        return Buf(name, es.enter_context(self.nc.sbuf_tensor(name, list(shape), dt)))

    def ps(self, name, shape, dt=F32):
        self._uid += 1
        name = "%s_u%d" % (name, self._uid)
        self.S.psum_names.add(name)
        return Buf(name, self.es.enter_context(self.nc.psum_tensor(name, list(shape), dt)))

    @contextlib.contextmanager
    def stage(self):
        self.es = contextlib.ExitStack()
        with self.es:
            yield
            with self.nc.allow_non_contiguous_dma(reason="small strided parameter loads"):
                self.S.emit()
        self.es = None

    def dma(self, q, out, in_, r, w):
        self.S.add(q, lambda e: e.dma_start(out=out, in_=in_), r=r, w=w, dma=True)

    def mm(self, out, lhsT, rhs, start, stop, r, w):
        self.S.add("pe", lambda e: e.matmul(out, lhsT, rhs, start=start, stop=stop), r=r, w=w)

    def tr(self, out, in_, ident, r, w):
        self.S.add("pe", lambda e: e.transpose(out, in_, ident), r=r, w=w)

    def act(self, out, in_, func, r, w, bias=0.0, scale=1.0, accum_out=None):
        if accum_out is None:
            self.S.add("act", lambda e: e.activation(out=out, in_=in_, func=func, bias=bias, scale=scale), r=r, w=w)
        else:
            self.S.add("act", lambda e: e.activation(out=out, in_=in_, func=func, bias=bias, scale=scale,
                                                     accum_out=accum_out), r=r, w=w)

    def tt(self, eng, out, in0, in1, op, r, w):
        self.S.add(eng, lambda e: e.tensor_tensor(out=out, in0=in0, in1=in1, op=op), r=r, w=w)

    def ts(self, eng, out, in0, s1, s2, op0, op1, r, w):
        if s2 is None:
            self.S.add(eng, lambda e: e.tensor_scalar(out=out, in0=in0, scalar1=s1, scalar2=None, op0=op0), r=r, w=w)
        else:
            self.S.add(eng, lambda e: e.tensor_scalar(out=out, in0=in0, scalar1=s1, scalar2=s2, op0=op0, op1=op1),
                       r=r, w=w)

    def stt(self, out, in0, scalar, in1, op0, op1, r, w):
        self.S.add("dve", lambda e: e.scalar_tensor_tensor(out=out, in0=in0, scalar=scalar, in1=in1, op0=op0, op1=op1),
                   r=r, w=w)

    def cp(self, eng, out, in_, r, w):
        if eng == "act":
            self.S.add("act", lambda e: e.copy(out=out, in_=in_), r=r, w=w)
        else:
            self.S.add(eng, lambda e: e.tensor_copy(out=out, in_=in_), r=r, w=w)

    def memset(self, eng, ap, val, w):
        self.S.add(eng, lambda e: e.memset(ap, val), r=(), w=w)

    def declare_io(self):
        L = DEPTH
        self.x = self.inp("x", [SEQ, D])
        self.c = self.inp("c", [D])
        self.ctx = self.inp("ctx", [LC, D])
        self.c_ctx = self.inp("c_ctx", [D])
        self.w_mod = self.inp("w_mod", [L, D, 6 * D])
        self.b_mod = self.inp("b_mod", [L, 6 * D])
        self.norm1_w = self.inp("norm1_w", [L, D])
        self.norm2_w = self.inp("norm2_w", [L, D])
        self.w_in = self.inp("w_in", [L, D, DIN])
        self.na_bias_g = self.inp("na_bias_g", [L, 16, 128, 512])
        self.na_mask = self.inp("na_mask", [16, 128, 512])
        self.rope = self.inp("rope", [SEQ, 2, 32])
        self.ssm_conv_w = self.inp("ssm_conv_w", [L, 3, 1280])
        self.ssm_conv_b = self.inp("ssm_conv_b", [L, 1280])
        self.ssm_a_log = self.inp("ssm_a_log", [L, 2, 16])
        self.ssm_dt_bias = self.inp("ssm_dt_bias", [L, 2, 16])
        self.ssm_d = self.inp("ssm_d", [L, 16])
        self.ssm_norm_w = self.inp("ssm_norm_w", [L, 1024])
        self.q_norm_w = self.inp("q_norm_w", [L, 64])
        self.k_norm_w = self.inp("k_norm_w", [L, 64])
        self.w_out_na = self.inp("w_out_na", [L, 512, D])
        self.w_out_ssm = self.inp("w_out_ssm", [L, 1024, D])
        self.w_out_gqa = self.inp("w_out_gqa", [L, 512, D])
        self.w_o = self.inp("w_o", [L, D, D])
        self.ffn_w_up = self.inp("ffn_w_up", [L, D, 2 * FH])
        self.ffn_conv_w = self.inp("ffn_conv_w", [L, 3, 2 * FH])
        self.ffn_conv_b = self.inp("ffn_conv_b", [L, 2 * FH])
        self.ffn_w_down = self.inp("ffn_w_down", [L, FH, D])
        self.final_norm_w = self.inp("final_norm_w", [D])
        self.out = self.dram("out", [SEQ, D], F32, out=True)
        self.hT = self.dram("hT", [D, NT], F32)
        self.hT2 = self.dram("hT2", [D, NT], F32)
        self.hcur = self.hT
        self.naqT = self.dram("naqT", [512, NT], BF16)
        self.nakT = self.dram("nakT", [512, NT], BF16)
        self.nav = self.dram("nav", [NT, 512], BF16)
        self.zs = self.dram("zs", [NT, 1024], BF16)
        self.xbcT = self.dram("xbcT", [1280, NT], BF16)
        self.dtr = self.dram("dtr", [NT, 32], F32)
        self.gqT = self.dram("gqT", [512, NT], BF16)
        self.gkT2 = self.dram("gkT2", [2, 128, NT], BF16)
        self.gv = self.dram("gv", [NT, 128], BF16)
        self.gatesT = self.dram("gatesT", [3072, NT], BF16)
        self.onaT = self.dram("onaT", [8, 64, NT], BF16)
        self.ogT = self.dram("ogT", [8, 64, NT], BF16)
        self.ossmT = self.dram("ossmT", [1024, NT], BF16)
        self.yf = self.dram("yf", [NT, 1024], F32)

    def consts(self):
        nc = self.nc
        self.ident_f = self.sb("ident_f", [128, 128], F32, glob=True)
        self.ident_b = self.sb("ident_b", [128, 128], BF16, glob=True)
        self.ones_b = self.sb("ones_b", [128, 128], BF16, glob=True)
        self.ones_f = self.sb("ones_f", [128, 128], F32, glob=True)
        self.modv = [self.sb("modv%d" % l, [128, 48, 2], F32, glob=True) for l in range(DEPTH)]
        self.A1 = [self.sb("A1_%d" % l, [128, 8, 2], F32, glob=True) for l in range(DEPTH)]
        self.A2 = [self.sb("A2_%d" % l, [128, 8, 2], F32, glob=True) for l in range(DEPTH)]
        self.eps_t = self.sb("eps_t", [128, 1], F32, glob=True)
        self.zero_t = self.sb("zero_t", [128, 1], F32, glob=True)
        with self.stage():
            idf, idb, ob, of = self.ident_f, self.ident_b, self.ones_b, self.ones_f
            self.memset("dve", self.eps_t[:], EPS, w=[self.eps_t.t()])
            self.memset("dve", self.zero_t[:], 0.0, w=[self.zero_t.t()])
            self.memset("pool", idf[:], 0.0, w=[idf.t()])
            self.S.add("pool", lambda e: e.affine_select(out=idf[:], in_=idf[:], pattern=[[-1, 128]],
                                                         compare_op=ALU.not_equal, fill=1.0, base=0,
                                                         channel_multiplier=1), r=[idf.t()], w=[idf.t()])
            self.cp("dve", idb[:], idf[:], r=[idf.t()], w=[idb.t()])
            self.memset("dve", ob[:], 1.0, w=[ob.t()])
            self.memset("dve", of[:], 1.0, w=[of.t()])

    def prologue(self):
        with self.stage():
            xin = [self.sb("xin%d" % i, [128, D], F32) for i in range(3)]
            pt = [self.ps("ptr%d" % i, [128, 1024], F32) for i in range(2)]
            stg = [self.sb("stg%d" % i, [128, 8, 512], F32) for i in range(2)]
            hTv = self.hT.h.ap().rearrange("(k p) t -> p k t", p=128)
            groups = [(0, 2)] + [(2 + 4 * g, 4) for g in range(8)]
            n = 0
            for gi, (ti0, cnt) in enumerate(groups):
                sg = stg[gi % 2]
                for j in range(cnt):
                    ti = ti0 + j
                    xb = xin[n % 3]
                    p = pt[n % 2]
                    n += 1
                    src = self.ctx.ap()[ti * 128:(ti + 1) * 128, :] if ti < 2 else \
                        self.x.ap()[(ti - 2) * 128:(ti - 1) * 128, :]
                    self.dma("sp", xb[:], src, r=[], w=[xb.t()])
                    for k in range(8):
                        self.tr(p[:, k * 128:(k + 1) * 128], xb[:, k * 128:(k + 1) * 128], self.ident_f[:],
                                r=[xb.t()], w=[p.t()])
                    dst = sg[:, :, j * 128:(j + 1) * 128]
                    src_ps = p[:].rearrange("p (k t) -> p k t", k=8)
                    self.cp("act" if n % 2 else "dve", dst, src_ps, r=[p.t()], w=[sg.t(j)])
                t0 = ti0 * 128
                self.dma("sp", hTv[:, :, t0:t0 + cnt * 128], sg[:, :, 0:cnt * 128], r=[sg.t()], w=[self.hT.t()])

    def modstage(self):
        with self.stage():
            craw = self.sb("craw", [128, 8, 2], F32)
            csil = self.sb("csil", [128, 8, 2], F32)
            self.dma("sp", craw[:, :, 0], self.c.ap().rearrange("(k p) -> p k", p=128), r=[], w=[craw.t(0)])
            self.dma("sp", craw[:, :, 1], self.c_ctx.ap().rearrange("(k p) -> p k", p=128), r=[], w=[craw.t(1)])
            self.act(csil[:], craw[:], AF.Silu, r=[craw.t()], w=[csil.t()])
            wm = [self.sb("wm%d" % i, [128, 8, 512], F32) for i in range(2)]
            bm = self.sb("bm", [128, 48], F32)
            nw = self.sb("nw", [128, 8], F32)
            pm = self.ps("pmod", [128, 96], F32)
            n = 0
            for l in range(DEPTH):
                wv = self.w_mod.ap()[l].rearrange("(k p) n -> p k n", p=128)
                for og in range(12):
                    w = wm[n % 2]
                    n += 1
                    self.dma("sp", w[:], wv[:, :, og * 512:(og + 1) * 512], r=[], w=[w.t()])
                    for oc in range(4):
                        o = og * 4 + oc
                        for k in range(8):
                            self.mm(pm[:, 2 * o:2 * o + 2], w[:, k, oc * 128:(oc + 1) * 128], csil[:, k, :],
                                    k == 0, k == 7, r=[w.t(), csil.t()], w=[pm.t()])
                self.dma("sp", bm[:], self.b_mod.ap()[l].rearrange("(o p) -> p o", p=128), r=[], w=[bm.t()])
                mv = self.modv[l]
                self.tt("dve", mv[:], pm[:].rearrange("p (o j) -> p o j", j=2),
                        bm[:].unsqueeze(2).broadcast_to([128, 48, 2]), ALU.add, r=[pm.t(), bm.t()], w=[mv.t()])
                for (A, nwin, sc0) in ((self.A1[l], self.norm1_w, 8), (self.A2[l], self.norm2_w, 32)):
                    self.dma("sp", nw[:], nwin.ap()[l].rearrange("(k p) -> p k", p=128), r=[], w=[nw.t()])
                    self.ts("dve", A[:], mv[:, sc0:sc0 + 8, :], 1.0, None, ALU.add, None, r=[mv.t()], w=[A.t()])
                    self.tt("dve", A[:], A[:], nw[:].unsqueeze(2).broadcast_to([128, 8, 2]), ALU.mult,
                            r=[A.t(), nw.t()], w=[A.t()])

    def norm_block(self, hb, N, A, mv, b0, j, out_fn, out_w, sq, ssp, rs, tmp):
        self.act(sq[:, :, 0:N], hb[:, :, 0:N], AF.Square, r=[hb.t()], w=[sq.t()])
        for k in range(8):
            self.mm(ssp[:, 0:N], self.ones_b[:], sq[:, k, 0:N], k == 0, k == 7, r=[sq.t()], w=[ssp.t()])
        self.act(rs[:, 0:N], ssp[:, 0:N], AF.Sqrt, bias=self.eps_t[:, 0:1], scale=1.0 / D, r=[ssp.t()], w=[rs.t()])
        self.S.add("dve", lambda e: e.reciprocal(out=rs[:, 0:N], in_=rs[:, 0:N]), r=[rs.t()], w=[rs.t()])
        if isinstance(tmp, list):
            for k in range(8):
                tk = tmp[k % len(tmp)]
                self.tt("dve", tk[:, 0:N], hb[:, k, 0:N], rs[:, 0:N], ALU.mult, r=[hb.t(), rs.t()], w=[tk.t()])
                self.act(out_fn(k), tk[:, 0:N], AF.Identity, scale=A[:, k, j:j + 1], bias=mv[:, b0 + k, j:j + 1],
                         r=[tk.t()], w=out_w)
            return
        self.tt("dve", tmp[:, :, 0:N], hb[:, :, 0:N], rs[:, 0:N].unsqueeze(1).broadcast_to([128, 8, N]), ALU.mult,
                r=[hb.t(), rs.t()], w=[tmp.t()])
        for k in range(8):
            self.act(out_fn(k), tmp[:, k, 0:N], AF.Identity, scale=A[:, k, j:j + 1], bias=mv[:, b0 + k, j:j + 1],
                     r=[tmp.t()], w=out_w)

    def instage(self, l, outer):
        xnT = self.sb("xnT", [128, 8, NT], BF16, es=outer)
        hTv = self.hcur.h.ap().rearrange("(k p) t -> p k t", p=128)
        with self.stage():
            hbs = [self.sb("hb%d" % i, [128, 8, 512], F32) for i in range(2)]
            sqs = [self.sb("sq%d" % i, [128, 8, 512], BF16) for i in range(2)]
            tmp = self.sb("ntmp", [128, 8, 512], F32)
            rs = self.sb("nrs", [128, 512], F32)
            ssp = self.ps("nssp", [128, 512], F32)
            for bi, (t0, N, isctx) in enumerate(TOKBLKS):
                hb, sq = hbs[bi % 2], sqs[bi % 2]
                self.dma("sp", hb[:, :, 0:N], hTv[:, :, t0:t0 + N], r=[self.hcur.t()], w=[hb.t()])
                self.norm_block(hb, N, self.A1[l], self.modv[l], 0, isctx,
                                lambda k, t0=t0, N=N: xnT[:, k, t0:t0 + N], [xnT.t(bi)], sq, ssp, rs, tmp)
        wv = self.w_in.ap()[l].rearrange("(k p) n -> p k n", p=128)
        with self.stage():
            Ws = [self.sb("Wc%d" % i, [128, 8, 512], BF16) for i in range(2)]
            pss = [self.ps("pin%d" % i, [128, 512], F32) for i in range(4)]
            stg = [self.sb("stgi%d" % i, [128, 4, 512], BF16) for i in range(3)]
            stgf = [self.sb("stgf%d" % i, [128, 4, 32], F32) for i in range(2)]
            cnt = {"w": 0, "p": 0, "s": 0, "e": 0}

            def nxt(lst, key):
                b = lst[cnt[key] % len(lst)]
                cnt[key] += 1
                return b

            def evac(func, out, in_, r, w):
                if func == "copy":
                    cnt["e"] += 1
                    self.cp("act" if cnt["e"] % 2 else "dve", out, in_, r=r, w=w)
                elif func == "silu":
                    self.act(out, in_, AF.Silu, r=r, w=w)
                elif func == "sigmoid":
                    self.act(out, in_, AF.Sigmoid, r=r, w=w)
                elif func == "q8":
                    self.ts("dve", out, in_, 0.125, None, ALU.mult, None, r=r, w=w)

            def blk_of_tile(ti):
                return 0 if ti < 2 else 1 + (ti - 2) // 4

            def fm_group(c0, n, func, dstT, drow0):
                W = nxt(Ws, "w")
                self.dma("pool", W[:, :, 0:n], wv[:, :, c0:c0 + n], r=[], w=[W.t()])
                for bi, (t0, N, isctx) in enumerate(TOKBLKS):
                    sg = nxt(stg, "s")
                    for oc in range(n // 128):
                        ps = nxt(pss, "p")
                        for k in range(8):
                            self.mm(ps[:, 0:N], W[:, k, oc * 128:(oc + 1) * 128], xnT[:, k, t0:t0 + N], k == 0, k == 7,
                                    r=[W.t(), xnT.t(bi)], w=[ps.t()])
                        evac(func, sg[:, oc, 0:N], ps[:, 0:N], r=[ps.t()], w=[sg.t(oc)])
                    dst = dstT.h.ap()[drow0:drow0 + n, t0:t0 + N].rearrange("(o p) t -> p o t", p=128)
                    self.dma("sp", dst, sg[:, 0:n // 128, 0:N], r=[sg.t()], w=[dstT.t()])

            def tm_group(c0, n, func, dst, dcol0, fp32=False):
                W = nxt(Ws, "w")
                self.dma("pool", W[:, :, 0:n], wv[:, :, c0:c0 + n], r=[], w=[W.t()])
                for (ti0, tc) in [(0, 2)] + [(2 + 4 * g, 4) for g in range(8)]:
                    sg = nxt(stgf, "s") if fp32 else nxt(stg, "s")
                    for j in range(tc):
                        ti = ti0 + j
                        ps = nxt(pss, "p")
                        for k in range(8):
                            self.mm(ps[:, 0:n], xnT[:, k, ti * 128:(ti + 1) * 128], W[:, k, 0:n], k == 0, k == 7,
                                    r=[W.t(), xnT.t(blk_of_tile(ti))], w=[ps.t()])
                        evac(func, sg[:, j, 0:n], ps[:, 0:n], r=[ps.t()], w=[sg.t(j)])
                    d = dst.h.ap()[ti0 * 128:(ti0 + tc) * 128, dcol0:dcol0 + n].rearrange("(j p) n -> p j n", p=128)
                    self.dma("sp", d, sg[:, 0:tc, 0:n], r=[sg.t()], w=[dst.t()])

            fm_group(C_NAQ, 512, "q8", self.naqT, 0)
            fm_group(C_NAK, 512, "copy", self.nakT, 0)
            tm_group(C_NAV, 512, "copy", self.nav, 0)
            tm_group(C_Z, 512, "silu", self.zs, 0)
            tm_group(C_Z + 512, 512, "silu", self.zs, 512)
            fm_group(C_XBC, 512, "copy", self.xbcT, 0)
            fm_group(C_XBC + 512, 512, "copy", self.xbcT, 512)
            fm_group(C_XBC + 1024, 256, "copy", self.xbcT, 1024)
            tm_group(C_DT, 32, "copy", self.dtr, 0, fp32=True)
            tm_group(C_GV, 128, "copy", self.gv, 0)
            for g in range(6):
                fm_group(C_GATE + 512 * g, 512, "sigmoid", self.gatesT, 512 * g)
            self.gqa_group(l, xnT, wv, blk_of_tile)

    def gqa_group(self, l, xnT, wv, blk_of_tile):
        wg = self.sb("wg", [128, 8, 640], BF16)
        self.dma("pool", wg[:], wv[:, :, C_GQ:C_GQ + 640], r=[], w=[wg.t()])
        ropeT = self.sb("ropeT", [128, 32, 64], F32)
        self.dma("sp", ropeT[:], self.rope.ap().rearrange("(i p) a b -> p i (a b)", p=128), r=[], w=[ropeT.t()])
        wqk = self.sb("wqk", [128, 640], F32)
        self.dma("sp", wqk[:, 0:64], self.q_norm_w.ap()[l:l + 1, :].broadcast_to([128, 64]), r=[], w=[wqk.t()])
        self.dma("sp", wqk[:, 512:576], self.k_norm_w.ap()[l:l + 1, :].broadcast_to([128, 64]), r=[], w=[wqk.t()])
        self.ts("dve", wqk[:, 0:64], wqk[:, 0:64], 0.125, None, ALU.mult, None, r=[wqk.t()], w=[wqk.t()])
        for h in range(1, 8):
            self.cp("dve", wqk[:, h * 64:(h + 1) * 64], wqk[:, 0:64], r=[wqk.t()], w=[wqk.t()])
        self.cp("dve", wqk[:, 576:640], wqk[:, 512:576], r=[wqk.t()], w=[wqk.t()])
        ps2 = self.ps("pg2", [128, 1024], F32)
        pT = self.ps("pgT", [128, 6, 128], BF16)
        sqv = self.sb("gsqv", [128, 640], F32)
        ss10 = self.sb("gss", [128, 10], F32)
        xn = self.sb("gxn", [128, 640], F32)
        t1 = self.sb("gt1", [128, 320], F32)
        t2 = self.sb("gt2", [128, 320], F32)
        t3 = self.sb("gt3", [128, 320], F32)
        t4 = self.sb("gt4", [128, 320], F32)
        qkb = self.sb("gqkb", [128, 640], BF16)
        kd = self.sb("gkd", [128, 256], BF16)
        stq = [self.sb("gstq%d" % i, [128, 6, 512], BF16) for i in range(2)]
        v3 = lambda ap, a: ap.rearrange("p (a b) -> p a b", a=a)
        gi = 0
        for (ti0, tc) in [(0, 2)] + [(2 + 4 * g, 4) for g in range(8)]:
            sg = stq[gi % 2]
            gi += 1
            for j in range(tc):
                ti = ti0 + j
                xs = [wg.t(), xnT.t(blk_of_tile(ti))]
                for k in range(8):
                    self.mm(ps2[:, 0:512], xnT[:, k, ti * 128:(ti + 1) * 128], wg[:, k, 0:512], k == 0, k == 7,
                            r=xs, w=[ps2.t(0)])
                for k in range(8):
                    self.mm(ps2[:, 512:640], xnT[:, k, ti * 128:(ti + 1) * 128], wg[:, k, 512:640], k == 0, k == 7,
                            r=xs, w=[ps2.t(1)])
                self.act(sqv[:], ps2[:, 0:640], AF.Square, r=[ps2.t()], w=[sqv.t()])
                self.S.add("dve", lambda e: e.tensor_reduce(out=ss10[:], in_=v3(sqv[:], 10), axis=AX.X, op=ALU.add),
                           r=[sqv.t()], w=[ss10.t()])
                self.act(ss10[:], ss10[:], AF.Sqrt, bias=self.eps_t[:, 0:1], scale=1.0 / 64, r=[ss10.t()], w=[ss10.t()])
                self.S.add("dve", lambda e: e.reciprocal(out=ss10[:], in_=ss10[:]), r=[ss10.t()], w=[ss10.t()])
                self.tt("dve", v3(xn[:], 10), v3(ps2[:, 0:640], 10), ss10[:].unsqueeze(2).broadcast_to([128, 10, 64]),
                        ALU.mult, r=[ps2.t(), ss10.t()], w=[xn.t()])
                self.tt("dve", xn[:], xn[:], wqk[:], ALU.mult, r=[xn.t(), wqk.t()], w=[xn.t()])
                if ti >= 2:
                    x4 = xn[:].rearrange("p (h i two) -> p h i two", h=10, two=2)
                    o4 = qkb[:].rearrange("p (h i two) -> p h i two", h=10, two=2)
                    xe, xo, oe, oo = x4[:, :, :, 0], x4[:, :, :, 1], o4[:, :, :, 0], o4[:, :, :, 1]
                    cosb = ropeT[:, ti - 2, 0:32].unsqueeze(1).broadcast_to([128, 10, 32])
                    sinb = ropeT[:, ti - 2, 32:64].unsqueeze(1).broadcast_to([128, 10, 32])
                    a1, a2, a3, a4 = v3(t1[:], 10), v3(t2[:], 10), v3(t3[:], 10), v3(t4[:], 10)
                    self.tt("dve", a1, xe, cosb, ALU.mult, r=[xn.t(), ropeT.t()], w=[t1.t()])
                    self.tt("dve", a2, xo, sinb, ALU.mult, r=[xn.t(), ropeT.t()], w=[t2.t()])
                    self.tt("dve", oe, a1, a2, ALU.subtract, r=[t1.t(), t2.t()], w=[qkb.t(0)])
                    self.tt(PENG, a3, xe, sinb, ALU.mult, r=[xn.t(), ropeT.t()], w=[t3.t()])
                    self.tt(PENG, a4, xo, cosb, ALU.mult, r=[xn.t(), ropeT.t()], w=[t4.t()])
                    self.tt(PENG, oo, a3, a4, ALU.add, r=[t3.t(), t4.t()], w=[qkb.t(1)])
                else:
                    self.cp("dve", qkb[:], xn[:], r=[xn.t()], w=[qkb.t()])
                kd4 = kd[:].rearrange("p (g c d) -> p g c d", g=2, c=2)
                ksrc = v3(qkb[:, 512:640], 2).unsqueeze(2).broadcast_to([128, 2, 2, 64])
                self.cp("act", kd4, ksrc, r=[qkb.t()], w=[kd.t()])
                for c in range(4):
                    self.tr(pT[:, c, :], qkb[:, c * 128:(c + 1) * 128], self.ident_b[:], r=[qkb.t()], w=[pT.t()])
                for g in range(2):
                    self.tr(pT[:, 4 + g, :], kd[:, g * 128:(g + 1) * 128], self.ident_b[:], r=[kd.t()], w=[pT.t()])
                self.cp("act", sg[:, :, j * 128:(j + 1) * 128], pT[:], r=[pT.t()], w=[sg.t(j)])
            t0, n = ti0 * 128, tc * 128
            self.dma("sp", self.gqT.h.ap()[:, t0:t0 + n].rearrange("(c p) t -> p c t", p=128), sg[:, 0:4, 0:n],
                     r=[sg.t()], w=[self.gqT.t()])
            self.dma("sp", self.gkT2.h.ap()[:, :, t0:t0 + n].rearrange("g p t -> p g t"), sg[:, 4:6, 0:n],
                     r=[sg.t()], w=[self.gkT2.t()])

    def softmax_finish(self, po, N, rrow, pb, pbs, out_ap, out_w):
        self.S.add("dve", lambda e: e.reciprocal(out=rrow[64:65, 0:N], in_=po[64:65, 0:N]), r=[po.t()], w=[rrow.t()])
        self.mm(pb[0:64, 0:N], self.ones_f[64:65, 0:64], rrow[64:65, 0:N], True, True, r=[rrow.t()], w=[pb.t()])
        self.cp("act", pbs[0:64, 0:N], pb[0:64, 0:N], r=[pb.t()], w=[pbs.t()])
        self.tt("dve", out_ap, po[0:64, 0:N], pbs[0:64, 0:N], ALU.mult, r=[po.t(), pbs.t()], w=out_w)

    def nastage(self, l, need_ctx, lim=None):
        with self.stage():
            kT = self.sb("nkT", [128, 4, NT], BF16)
            qT = self.sb("nqT", [128, 4, NT], BF16)
            V = self.sb("nV", [128, NTILE, 8, 65], BF16)
            TB = self.sb("nTB", [128, 16, 512], BF16)
            self.dma("sp", kT[:], self.nakT.h.ap().rearrange("(c p) t -> p c t", p=128), r=[self.nakT.t()], w=[kT.t()])
            self.dma("sp", qT[:], self.naqT.h.ap().rearrange("(c p) t -> p c t", p=128), r=[self.naqT.t()], w=[qT.t()])
            self.memset("dve", V[:, :, :, 64:65], 1.0, w=[V.t("ones")])
            nv = self.nav.h.ap().rearrange("(i p) (h d) -> p i h d", p=128, h=8)
            for i0 in range(NTILE):
                self.dma("sp", V[:, i0, :, 0:64], nv[:, i0], r=[self.nav.t()], w=[V.t("d%d" % i0)])
            bst = [self.sb("nbst%d" % i, [128, 2, 512], F32) for i in range(2)]
            mst = [self.sb("nmst%d" % i, [128, 2, 512], F32) for i in range(2)]
            for ch in range(8):
                b, m = bst[ch % 2], mst[ch % 2]
                self.dma("sp", b[:], self.na_bias_g.ap()[l, 2 * ch:2 * ch + 2].rearrange("t p n -> p t n"), r=[], w=[b.t()])
                self.dma("sp", m[:], self.na_mask.ap()[2 * ch:2 * ch + 2].rearrange("t p n -> p t n"), r=[], w=[m.t()])
                self.act(b[:], b[:], AF.Exp, r=[b.t()], w=[b.t()])
                self.tt("dve", TB[:, 2 * ch:2 * ch + 2, :], b[:], m[:], ALU.mult, r=[b.t(), m.t()], w=[TB.t(ch)])
            TB5 = TB[:].rearrange("p t (c e q) -> p t c e q", c=4, e=2)
            pS = [self.ps("nps%d" % i, [128, 512], F32) for i in range(4)]
            pO = [self.ps("npo%d" % i, [128, 512], F32) for i in range(2)]
            pb = self.ps("npb", [128, 512], F32)
            P = [[[self.sb("nP%d_%d_%d" % (u, a, e), [128, 512], BF16) for e in range(2)] for a in range(4)]
                 for u in range(2)]
            rrow = self.sb("nrrow", [128, 512], F32)
            pbs = self.sb("npbs", [64, 512], F32)
            osb = [self.sb("nosb%d" % i, [64, 8, 256], BF16) for i in range(2)]
            units = []
            if need_ctx:
                for u in range(4):
                    units.append((u * 64, [(0, 0, None), (1, 128, None)]))
            for r_ in range(64):
                rs = min(max(r_ - 4, 0), 56)
                kts = [(0, 0, None), (1, 128, None)]
                if rs % 2 == 0:
                    for j in range(4):
                        a = rs + 2 * j
                        kts.append((2 + a // 2, 256 + a * 64, a - r_ + 7))
                else:
                    for j in range(5):
                        a = rs - 1 + 2 * j
                        dr = a - r_ + 7
                        tb = 14 if j == 0 else (15 if j == 4 else dr)
                        kts.append((2 + a // 2, 256 + a * 64, tb))
                units.append((256 + r_ * 64, kts))
            psn = 0
            mtog = 0
            npend = []
            if lim:
                units = units[:lim]
            def phaseA(ui, hook=None):
                nonlocal psn, mtog
                tq0, kts = units[ui]
                ub = ui % 2
                nk = len(kts)
                npair = (nk + 1) // 2
                for a in range(npair):
                    pair = kts[2 * a:2 * a + 2]
                    banks = [pS[psn % 4], pS[(psn + 1) % 4]]
                    psn += 2
                    for s_, (vt, kc0, tb) in enumerate(pair):
                        for h in range(8):
                            e, c = h % 2, h // 2
                            self.mm(banks[e][:, s_ * 256 + c * 64: s_ * 256 + c * 64 + 64],
                                    kT[e * 64:(e + 1) * 64, c, kc0:kc0 + 128], qT[e * 64:(e + 1) * 64, c, tq0:tq0 + 64],
                                    True, True, r=[kT.t(), qT.t()], w=[banks[e].t()])
                    ncol = 256 * len(pair)
                    for e in range(2):
                        pt = P[ub][a][e]
                        self.act(pt[:, 0:ncol], banks[e][:, 0:ncol], AF.Exp, r=[banks[e].t()], w=[pt.t()])
                        for s_, (vt, kc0, tb) in enumerate(pair):
                            if tb is None:
                                continue
                            mtog += 1
                            sl = pt[:, s_ * 256:(s_ + 1) * 256].rearrange("p (c q) -> p c q", c=4)
                            self.tt("dve" if (mtog % 2 or NA_NOPOOL) else "pool", sl, sl, TB5[:, tb, :, e, :], ALU.mult,
                                    r=[pt.t(), TB.t()], w=[pt.t()])
                    if hook:
                        hook(a, npair)

            def pv_heads(ui, heads):
                tq0, kts = units[ui]
                ub = ui % 2
                nk = len(kts)
                po = pO[ui % 2]
                for h in heads:
                    e, c = h % 2, h // 2
                    for ki, (vt, kc0, tb) in enumerate(kts):
                        pt = P[ub][ki // 2][e]
                        s_ = ki % 2
                        self.mm(po[0:65, h * 64:(h + 1) * 64], V[:, vt, h, :],
                                pt[:, s_ * 256 + c * 64: s_ * 256 + c * 64 + 64], ki == 0, ki == nk - 1,
                                r=[V.t(), pt.t()], w=[po.t()])

            def finish(ui):
                tq0, kts = units[ui]
                po = pO[ui % 2]
                ob = osb[(ui // 4) % 2]
                j = ui % 4
                self.act(rrow[64:65, 0:512], po[64:65, 0:512], AF.Ln, r=[po.t()], w=[rrow.t()])
                self.act(rrow[64:65, 0:512], rrow[64:65, 0:512], AF.Exp, scale=-1.0, r=[rrow.t()], w=[rrow.t()])
                self.mm(pb[0:64, 0:512], self.ones_f[64:65, 0:64], rrow[64:65, 0:512], True, True, r=[rrow.t()], w=[pb.t()])
                self.cp("act", pbs[0:64, 0:512], pb[0:64, 0:512], r=[pb.t()], w=[pbs.t()])
                self.tt("dve", ob[:, :, j * 64:(j + 1) * 64], po[0:64, 0:512], pbs[0:64, 0:512], ALU.mult,
                        r=[po.t(), pbs.t()], w=[ob.t(j)])
                if j == 3:
                    t0 = tq0 - 192
                    self.dma("sp", self.onaT.h.ap()[:, :, t0:t0 + 256].rearrange("h d t -> d h t"), ob[:],
                             r=[ob.t()], w=[self.onaT.t()])

            phaseA(0)
            nu = len(units)
            for ui in range(nu):
                if ui + 1 < nu:
                    def hook(a, npair, ui=ui):
                        lo, hi = (8 * a) // npair, (8 * (a + 1)) // npair
                        pv_heads(ui, range(lo, hi))
                        if a == 0 and ui > 0:
                            finish(ui - 1)
                    phaseA(ui + 1, hook)
                else:
                    pv_heads(ui, range(8))
                    if ui > 0:
                        finish(ui - 1)
            finish(nu - 1)

    def gqastage(self, l, need_ctx, lim=None):
        with self.stage():
            kT2 = self.sb("gkT", [128, 2, NT], BF16)
            qT = self.sb("gqTz", [128, 8, NT], BF16)
            V = self.sb("gV", [128, NTILE, 2, 65], BF16)
            self.dma("sp", kT2[:], self.gkT2.h.ap().rearrange("g p t -> p g t"), r=[self.gkT2.t()], w=[kT2.t()])
            for h in range(8):
                e_, c_ = h % 2, h // 2
                self.memset("dve", qT[(1 - e_) * 64:(2 - e_) * 64, h, :], 0.0, w=[qT.t((h, 0))])
                self.dma("sp", qT[e_ * 64:(e_ + 1) * 64, h, :], self.gqT.h.ap()[c_ * 128 + e_ * 64:c_ * 128 + (e_ + 1) * 64, :],
                         r=[self.gqT.t()], w=[qT.t((h, 1))])
            self.memset("dve", V[:, :, :, 64:65], 1.0, w=[V.t("ones")])
            for g in range(2):
                self.dma("sp", V[:, :, g, 0:64], self.gv.h.ap()[:, g * 64:(g + 1) * 64].rearrange("(i p) d -> p i d", p=128),
                         r=[self.gv.t()], w=[V.t("d%d" % g)])
            pS = [self.ps("gps%d" % i, [128, 1024], F32) for i in range(2)]
            pO = [self.ps("gpo%d" % i, [128, 512], F32) for i in range(2)]
            pb = self.ps("gpb", [128, 512], F32)
            Pb = [self.sb("gP%d" % i, [128, 1024], BF16) for i in range(3)]
            rrow = self.sb("grrow", [128, 512], F32)
            pbs = self.sb("gpbs", [64, 512], F32)
            osb = [self.sb("gosb%d" % i, [64, 512], BF16) for i in range(2)]
            blocks = []
            if need_ctx:
                blocks.append((0, 256, [0, 1]))
            for qb in range(8):
                blocks.append((256 + qb * 512, 512, list(range(NTILE))))
            n = 0
            ui = 0
            pend = []
            if lim:
                blocks = blocks[:lim]
            for h in range(8 if not lim else 2):
                g, e, c = h // 4, h % 2, h // 2
                for (tq0, N, kts) in blocks:
                    po = pO[ui % 2]
                    ob = osb[ui % 2]
                    ui += 1
                    nk = len(kts)
                    npair = nk // 2
                    LA = 1
                    bufs = {}

                    def qk(i):
                        nonlocal n
                        ps, pt = pS[n % 2], Pb[n % 3]
                        n += 1
                        for a_ in range(2):
                            kt = kts[2 * i + a_]
                            self.mm(ps[:, a_ * 512:a_ * 512 + N], kT2[:, g, kt * 128:(kt + 1) * 128],
                                    qT[:, h, tq0:tq0 + N], True, True, r=[kT2.t(), qT.t()], w=[ps.t(a_)])
                        v2 = lambda ap: ap.rearrange("p (a n) -> p a n", a=2)[:, :, 0:N]
                        self.act(v2(pt[:]), v2(ps[:]), AF.Exp, r=[ps.t()], w=[pt.t()])
                        bufs[i] = pt

                    for i in range(min(LA, npair)):
                        qk(i)
                    for i in range(npair):
                        if i + LA < npair:
                            qk(i + LA)
                        if i == 3 and pend:
                            pend.pop()()
                        pt = bufs.pop(i)
                        for a_ in range(2):
                            self.mm(po[0:65, 0:N], V[:, kts[2 * i + a_], g, :], pt[:, a_ * 512:a_ * 512 + N],
                                    i == 0 and a_ == 0, i == npair - 1 and a_ == 1, r=[V.t(), pt.t()], w=[po.t()])
                    if pend:
                        pend.pop()()

                    def fin(po=po, N=N, ob=ob, h=h, tq0=tq0):
                        self.softmax_finish(po, N, rrow, pb, pbs, ob[:, 0:N], [ob.t()])
                        self.dma("sp", self.ogT.h.ap()[h, :, tq0:tq0 + N], ob[:, 0:N], r=[ob.t()], w=[self.ogT.t()])
                    pend.append(fin)
            if pend:
                pend.pop()()

    def ssdstage(self, l, need_ctx):
        with contextlib.ExitStack() as outer:
            sbo = lambda name, shape, dt: self.sb(name, shape, dt, es=outer)
            xT = sbo("sxT", [128, NTILE, 1024], BF16)
            BT = sbo("sBT", [128, NT], BF16)
            CT0 = sbo("sCT0", [128, NT], BF16)
            CT1 = sbo("sCT1", [128, NT], BF16)
            Btm = sbo("sBtm", [128, NTILE, 128], BF16)
            dt = sbo("sdt", [128, NTILE, 32], F32)
            da = sbo("sda", [128, NTILE, 32], F32)
            with self.stage():
                cw = self.sb("scw", [128, 10, 3], F32)
                cbias = self.sb("scb", [128, 10], F32)
                for k in range(3):
                    self.dma("sp", cw[:, :, k], self.ssm_conv_w.ap()[l, k].rearrange("(c p) -> p c", p=128),
                             r=[], w=[cw.t(k)])
                self.dma("sp", cbias[:], self.ssm_conv_b.ap()[l].rearrange("(c p) -> p c", p=128), r=[], w=[cbias.t()])
                raws = [self.sb("sraw%d" % i, [128, NT], BF16) for i in range(2)]
                acc = self.sb("sacc", [128, NT], F32)
                xF = self.sb("sxF", [128, NT], BF16)
                pT = [self.ps("spT%d" % i, [128, 8, 128], BF16) for i in range(2)]
                xv = self.xbcT.h.ap().rearrange("(c p) t -> p c t", p=128)
                ntr = 0
                for c in range(10):
                    raw = raws[c % 2]
                    self.dma("sp", raw[:], xv[:, c, :], r=[self.xbcT.t()], w=[raw.t()])
                    self.ts("dve", acc[:], raw[:], cw[:, c, 1:2], cbias[:, c:c + 1], ALU.mult, ALU.add,
                            r=[raw.t(), cw.t(), cbias.t()], w=[acc.t()])
                    for (a, b) in ((0, LC), (LC, NT)):
                        self.stt(acc[:, a + 1:b], raw[:, a:b - 1], cw[:, c, 0:1], acc[:, a + 1:b], ALU.mult, ALU.add,
                                 r=[raw.t(), acc.t()], w=[acc.t()])
                        self.stt(acc[:, a:b - 1], raw[:, a + 1:b], cw[:, c, 2:3], acc[:, a:b - 1], ALU.mult, ALU.add,
                                 r=[raw.t(), acc.t()], w=[acc.t()])
                    if c < 9:
                        dstF = xF if c < 8 else BT
                        self.act(dstF[:], acc[:], AF.Silu, r=[acc.t()], w=[dstF.t()])
                        for (ti0, tc) in [(0, 2)] + [(2 + 4 * g, 4) for g in range(8)]:
                            p = pT[ntr % 2]
                            ntr += 1
                            for j in range(tc):
                                ti = ti0 + j
                                self.tr(p[:, j, :], dstF[:, ti * 128:(ti + 1) * 128], self.ident_b[:], r=[dstF.t()], w=[p.t()])
                            if c < 8:
                                self.cp("act" if ntr % 2 else "dve", xT[:, ti0:ti0 + tc, c * 128:(c + 1) * 128], p[:, 0:tc, :],
                                        r=[p.t()], w=[xT.t((c, ti0))])
                            else:
                                self.cp("dve", Btm[:, ti0:ti0 + tc, :], p[:, 0:tc, :], r=[p.t()], w=[Btm.t(ti0)])
                    else:
                        self.act(CT0[:], acc[:], AF.Silu, r=[acc.t()], w=[CT0.t()])
                        self.cp("dve", CT1[:], CT0[:], r=[CT0.t()], w=[CT1.t()])
                        self.memset("dve", CT0[64:128, :], 0.0, w=[CT0.t()])
                        self.memset("dve", CT1[0:64, :], 0.0, w=[CT1.t()])
                dtb = self.sb("sdtb", [128, 32], F32)
                ab = self.sb("sab", [128, 32], F32)
                self.dma("sp", dtb[:], self.ssm_dt_bias.ap()[l:l + 1].rearrange("o a b -> o (a b)").broadcast_to([128, 32]),
                         r=[], w=[dtb.t()])
                self.dma("sp", ab[:], self.ssm_a_log.ap()[l:l + 1].rearrange("o a b -> o (a b)").broadcast_to([128, 32]),
                         r=[], w=[ab.t()])
                self.act(ab[:], ab[:], AF.Exp, r=[ab.t()], w=[ab.t()])
                self.ts("dve", ab[:], ab[:], -1.0, None, ALU.mult, None, r=[ab.t()], w=[ab.t()])
                self.dma("sp", dt[:], self.dtr.h.ap().rearrange("(i p) n -> p i n", p=128), r=[self.dtr.t()], w=[dt.t()])
                self.tt("dve", dt[:], dt[:], dtb[:].unsqueeze(1).broadcast_to([128, NTILE, 32]), ALU.add,
                        r=[dt.t(), dtb.t()], w=[dt.t()])
                self.act(dt[:], dt[:], AF.Exp, r=[dt.t()], w=[dt.t()])
                self.act(dt[:], dt[:], AF.Ln, bias=self.ones_f[:, 0:1], r=[dt.t()], w=[dt.t()])
                self.tt("dve", da[:], dt[:], ab[:].unsqueeze(1).broadcast_to([128, NTILE, 32]), ALU.mult,
                        r=[dt.t(), ab.t()], w=[da.t()])
            with self.stage():
                U = self.sb("sU", [128, 128], F32)
                Lo = self.sb("sLo", [128, 128], F32)
                Ls = self.sb("sLs", [128, 128], F32)
                Us = self.sb("sUs", [128, 128], F32)
                for (mtx, op, sg) in ((U, ALU.is_ge, -1), (Lo, ALU.is_ge, 1), (Ls, ALU.is_gt, 1), (Us, ALU.is_gt, -1)):
                    self.memset("pool", mtx[:], 1.0, w=[mtx.t()])
                    self.S.add("pool", lambda e, mtx=mtx, op=op, sg=sg: e.affine_select(
                        out=mtx[:], in_=mtx[:], pattern=[[-sg, 128]], compare_op=op, fill=0.0, base=0,
                        channel_multiplier=sg), r=[mtx.t()], w=[mtx.t()])
                dsk = self.sb("sdsk", [128, 16], F32)
                nwT = self.sb("snwT", [128, 8], F32)
                self.dma("sp", dsk[:], self.ssm_d.ap()[l:l + 1, :].broadcast_to([128, 16]), r=[], w=[dsk.t()])
                self.dma("sp", nwT[:], self.ssm_norm_w.ap()[l].rearrange("(k p) -> p k", p=128), r=[], w=[nwT.t()])
                LA = self.sb("sLA", [128, 16, 128], F32)
                expD = self.sb("sexpD", [128, 16, 128], BF16)
                mcb = self.sb("smcb", [128, 2, 128], BF16)
                Mt = [self.sb("sM%d" % i, [128, 16, 128], BF16) for i in range(2)]
                xs = [self.sb("sxs%d" % i, [128, 16, 64], BF16) for i in range(2)]
                xsw = [self.sb("sxsw%d" % i, [128, 16, 64], BF16) for i in range(2)]
                E = self.sb("sE", [128, 16], F32)
                dec = self.sb("sdec", [128, 8], F32)
                H = self.sb("sH", [128, 512], F32)
                Hb = self.sb("sHb", [128, 512], BF16)
                tmpH = self.sb("stmpH", [128, 512], F32)
                yt = self.sb("syt", [128, 1024], F32)
                ys = [self.sb("sy%d" % i, [128, 1024], F32) for i in range(2)]
                yfb = self.sb("syfb", [128, 1024], F32)
                zt = self.sb("szt", [128, 1024], BF16)
                gg = self.sb("sgg", [128, 1024], F32)
                gjunk = self.sb("sgjunk", [128, 1024], BF16)
                ssq = self.sb("sssq", [128, 1], F32)
                ob = self.sb("sob", [128, 1024], BF16)
                ost = self.sb("sost", [128, 8, 512], BF16)
                pD = self.ps("spD", [128, 2048], F32)
                pm = self.ps("spm", [128, 512], F32)
                pm2 = self.ps("spm2", [128, 512], F32)
                pS = self.ps("spS", [128, 512], F32)
                pTo = self.ps("spTo", [128, 8, 128], BF16)
                oT = self.ossmT.h.ap().rearrange("(k p) t -> p k t", p=128)
                decs = [dec, self.sb("sdec2", [128, 8], F32)]
                steps = []
                for d in range(2):
                    order = [0, 1] + list(range(2, NTILE)) if d == 0 else [1, 0] + list(range(NTILE - 1, 1, -1))
                    for oi, ti in enumerate(order):
                        steps.append((d, ti, oi == 0))

                def par(i):
                    d, ti, first = steps[i]
                    A_, Bm_, mask_, wcol = (Ls, U, U, 127) if d == 0 else (Us, Lo, Lo, 0)
                    return dict(d=d, ti=ti, first=first, A_=A_, Bm_=Bm_, mask_=mask_, wcol=wcol,
                                do_y=(need_ctx or ti >= 2), cols=slice(ti * 128, (ti + 1) * 128),
                                dav=da[:, ti, d * 16:(d + 1) * 16], dtv=dt[:, ti, d * 16:(d + 1) * 16],
                                M=Mt[i % 2], xs_=xs[i % 2], xsw_=xsw[i % 2], y=ys[i % 2], dec=decs[i % 2],
                                x3=xT[:, ti, :].rearrange("p (h q) -> p h q", h=16))

                def stepA(i):
                    p = par(i)
                    A_, Bm_, dav, dtv, xs_, xsw_, dec_ = p["A_"], p["Bm_"], p["dav"], p["dtv"], p["xs_"], p["xsw_"], p["dec"]
                    for h in range(16):
                        self.act(LA[:, h, :], A_[:], AF.Identity, scale=dav[:, h:h + 1], bias=self.zero_t[:, 0:1],
                                 r=[A_.t(), da.t()], w=[LA.t(h)])
                    for h in range(16):
                        self.mm(pD[:, h * 128:(h + 1) * 128], LA[:, h, :], Bm_[:], True, True,
                                r=[LA.t(h), Bm_.t()], w=[pD.t(h // 4)])
                    self.mm(pm2[:, 0:16], Bm_[:], dav, True, True, r=[Bm_.t(), da.t()], w=[pm2.t()])
                    self.mm(pm2[:, 16:32], self.ones_f[:], dav, True, True, r=[da.t()], w=[pm2.t()])
                    for q in range(4):
                        self.act(expD[:, 4 * q:4 * q + 4, :], pD[:, q * 512:(q + 1) * 512].rearrange("p (h t) -> p h t", h=4),
                                 AF.Exp, r=[pD.t(q)], w=[expD.t(q)])
                    self.act(E[:], pm2[:, 0:16], AF.Exp, r=[pm2.t()], w=[E.t()])
                    self.act(dec_[0:64, :], pm2[0:64, 16:24], AF.Exp, r=[pm2.t()], w=[dec_.t(0)])
                    self.act(dec_[64:128, :], pm2[64:128, 24:32], AF.Exp, r=[pm2.t()], w=[dec_.t(1)])

                def stepB(i):
                    p = par(i)
                    if p["first"]:
                        self.memset("dve", H[:], 0.0, w=[H.t()])
                        self.memset("dve", Hb[:], 0.0, w=[Hb.t()])
                    self.tt(PENG, p["xs_"][:], p["x3"], p["dtv"].unsqueeze(2).broadcast_to([128, 16, 64]), ALU.mult,
                            r=[xT.t(), dt.t()], w=[p["xs_"].t()])
                    if p["do_y"]:
                        stepB_y(p)
                    self.tt(PENG, p["xsw_"][:], p["xs_"][:], expD[:, :, p["wcol"]].unsqueeze(2).broadcast_to([128, 16, 64]),
                            ALU.mult, r=[p["xs_"].t(), expD.t()], w=[p["xsw_"].t()])

                def stepB_y(p):
                    cols, mask_, M, xs_, y = p["cols"], p["mask_"], p["M"], p["xs_"], p["y"]
                    for g, CTg in enumerate((CT0, CT1)):
                        self.mm(pm[:, g * 128:(g + 1) * 128], BT[:, cols], CTg[:, cols], True, True,
                                r=[BT.t(), CTg.t()], w=[pm.t()])
                    self.tt("dve", mcb[:], pm[:, 0:256].rearrange("p (g t) -> p g t", g=2),
                            mask_[:].unsqueeze(1).broadcast_to([128, 2, 128]), ALU.mult,
                            r=[pm.t(), mask_.t()], w=[mcb.t()])
                    for g in range(2):
                        self.tt("dve", M[:, 8 * g:8 * g + 8, :], expD[:, 8 * g:8 * g + 8, :],
                                mcb[:, g:g + 1, :].broadcast_to([128, 8, 128]), ALU.mult,
                                r=[expD.t(), mcb.t()], w=[M.t(g)])
                    for h in range(16):
                        self.mm(pD[:, h * 64:(h + 1) * 64], M[:, h, :], xs_[:, h, :], True, True,
                                r=[M.t(), xs_.t()], w=[pD.t(h // 8)])
                    for g, CTg in enumerate((CT0, CT1)):
                        self.mm(pD[:, 1024 + g * 512:1024 + (g + 1) * 512], CTg[:, cols], Hb[:], True, True,
                                r=[CTg.t(), Hb.t()], w=[pD.t(2 + g)])
                    for g in range(2):
                        self.tt("dve", yt[:, g * 512:(g + 1) * 512].rearrange("p (h q) -> p h q", h=8),
                                pD[:, 1024 + g * 512:1024 + (g + 1) * 512].rearrange("p (h q) -> p h q", h=8),
                                E[:, 8 * g:8 * g + 8].unsqueeze(2).broadcast_to([128, 8, 64]), ALU.mult,
                                r=[pD.t(2 + g), E.t()], w=[yt.t(g)])
                    self.tt("dve", y[:], yt[:], pD[:, 0:1024], ALU.add, r=[yt.t(), pD.t(0), pD.t(1)], w=[y.t()])

                def stepC(i):
                    p = par(i)
                    ti, xsw_, dec_ = p["ti"], p["xsw_"], p["dec"]
                    for g in range(2):
                        self.mm(pS[g * 64:(g + 1) * 64, :], Btm[:, ti, g * 64:(g + 1) * 64],
                                xsw_[:, 8 * g:8 * g + 8, :].rearrange("p h q -> p (h q)"), True, True,
                                r=[Btm.t(), xsw_.t()], w=[pS.t(g)])
                    self.tt("dve", tmpH[:].rearrange("p (h q) -> p h q", h=8), H[:].rearrange("p (h q) -> p h q", h=8),
                            dec_[:].unsqueeze(2).broadcast_to([128, 8, 64]), ALU.mult, r=[H.t(), dec_.t()], w=[tmpH.t()])
                    self.tt("dve", H[:], tmpH[:], pS[:], ALU.add, r=[tmpH.t(), pS.t()], w=[H.t()])
                    self.cp("act", Hb[:], H[:], r=[H.t()], w=[Hb.t()])

                def stepD(i):
                    p = par(i)
                    if not p["do_y"]:
                        return
                    d, ti, y, x3 = p["d"], p["ti"], p["y"], p["x3"]
                    rows = self.yf.h.ap()[ti * 128:(ti + 1) * 128, :]
                    if d == 0:
                        self.dma("sp", rows, y[:], r=[y.t()], w=[self.yf.t(ti)])
                        return
                    self.dma("sp", yfb[:], rows, r=[self.yf.t(ti)], w=[yfb.t()])
                    self.dma("sp", zt[:], self.zs.h.ap()[ti * 128:(ti + 1) * 128, :], r=[self.zs.t()], w=[zt.t()])
                    self.tt("dve", y[:], y[:], yfb[:], ALU.add, r=[y.t(), yfb.t()], w=[y.t()])
                    self.tt(PENG, gg[:].rearrange("p (h q) -> p h q", h=16), x3,
                            dsk[:].unsqueeze(2).broadcast_to([128, 16, 64]), ALU.mult, r=[xT.t(), dsk.t()], w=[gg.t()])
                    self.tt("dve", y[:], y[:], gg[:], ALU.add, r=[y.t(), gg.t()], w=[y.t()])
                    self.tt("dve", gg[:], y[:], zt[:], ALU.mult, r=[y.t(), zt.t()], w=[gg.t()])
                    self.act(gjunk[:], gg[:], AF.Square, r=[gg.t()], w=[gjunk.t(), ssq.t()], accum_out=ssq[:])
                    self.act(ssq[:], ssq[:], AF.Sqrt, bias=self.eps_t[:, 0:1], scale=1.0 / 1024, r=[ssq.t()], w=[ssq.t()])
                    self.S.add("dve", lambda e: e.reciprocal(out=ssq[:], in_=ssq[:]), r=[ssq.t()], w=[ssq.t()])
                    self.ts("dve", ob[:], gg[:], ssq[:, 0:1], None, ALU.mult, None, r=[gg.t(), ssq.t()], w=[ob.t()])
                    for k in range(8):
                        self.tr(pTo[:, k, :], ob[:, k * 128:(k + 1) * 128], self.ident_b[:], r=[ob.t()], w=[pTo.t()])
                    j = ti if ti < 2 else (ti - 2) % 4
                    self.tt("dve", ost[:, :, j * 128:(j + 1) * 128], pTo[:], nwT[:].unsqueeze(2).broadcast_to([128, 8, 128]),
                            ALU.mult, r=[pTo.t(), nwT.t()], w=[ost.t(j)])
                    if j == 0:
                        t0, n = (0, 256) if ti < 2 else (ti * 128, 512)
                        self.dma("sp", oT[:, :, t0:t0 + n], ost[:, :, 0:n], r=[ost.t()], w=[self.ossmT.t()])

                stepA(0)
                for i in range(len(steps)):
                    stepB(i)
                    if i + 1 < len(steps):
                        stepA(i + 1)
                    stepC(i)
                    stepD(i)

    def mergestage(self, l, need_ctx):
        with self.stage():
            wna = self.sb("mwna", [64, 8, 1024], BF16)
            wgq = self.sb("mwgq", [64, 8, 1024], BF16)
            wss = self.sb("mwss", [128, 8, 1024], BF16)
            wo = self.sb("mwo", [128, 8, 1024], BF16)
            self.dma("pool", wna[:], self.w_out_na.ap()[l].rearrange("(h d) n -> d h n", d=64), r=[], w=[wna.t()])
            self.dma("pool", wgq[:], self.w_out_gqa.ap()[l].rearrange("(h d) n -> d h n", d=64), r=[], w=[wgq.t()])
            self.dma("pool", wss[:], self.w_out_ssm.ap()[l].rearrange("(k p) n -> p k n", p=128), r=[], w=[wss.t()])
            self.dma("pool", wo[:], self.w_o.ap()[l].rearrange("(k p) n -> p k n", p=128), r=[], w=[wo.t()])
            NB = 512
            ona = [self.sb("mona%d" % i, [64, 8, NB], BF16) for i in range(2)]
            ogq = [self.sb("mogq%d" % i, [64, 8, NB], BF16) for i in range(2)]
            oss = [self.sb("moss%d" % i, [128, 8, NB], BF16) for i in range(2)]
            gts = [self.sb("mgt%d" % i, [128, 24, NB], BF16) for i in range(2)]
            hbs = [self.sb("mhb%d" % i, [128, 8, NB], F32) for i in range(1)] * 2
            yTs = [self.sb("myT%d" % i, [128, 8, NB], BF16) for i in range(1)] * 2
            t1 = [self.sb("mt1_%d" % i, [128, NB], F32) for i in range(1)] * 2
            t2 = [self.sb("mt2_%d" % i, [128, NB], F32) for i in range(1)] * 2
            t3 = [self.sb("mt3_%d" % i, [128, NB], F32) for i in range(1)] * 2
            pA = [self.ps("mpA%d" % i, [128, 512], F32) for i in range(2)]
            pB = [self.ps("mpB%d" % i, [128, 512], F32) for i in range(2)]
            pC = [self.ps("mpC%d" % i, [128, 512], F32) for i in range(2)]
            pM = [self.ps("mpM%d" % i, [128, 512], F32) for i in range(2)]
            hv = self.hcur.h.ap().rearrange("(k p) t -> p k t", p=128)
            blocks = ([(0, LC, 1)] if need_ctx else []) + [(LC + NB * i, NB, 0) for i in range(SEQ // NB)]
            mv = self.modv[l]
            n = 0
            for bi, (t0, N, j) in enumerate(blocks):
                a, g_, s_, gt, hb, yT = ona[bi % 2], ogq[bi % 2], oss[bi % 2], gts[bi % 2], hbs[bi % 2], yTs[bi % 2]
                a, g_, s_, gt, hb, yT = (Sl(a, N), Sl(g_, N), Sl(s_, N), Sl(gt, N), Sl(hb, N), Sl(yT, N))
                self.dma("sp", a[:], self.onaT.h.ap()[:, :, t0:t0 + N].rearrange("h d t -> d h t"), r=[self.onaT.t()], w=[a.t()])
                self.dma("sp", g_[:], self.ogT.h.ap()[:, :, t0:t0 + N].rearrange("h d t -> d h t"), r=[self.ogT.t()], w=[g_.t()])
                self.dma("sp", s_[:], self.ossmT.h.ap()[:, t0:t0 + N].rearrange("(k p) t -> p k t", p=128),
                         r=[self.ossmT.t()], w=[s_.t()])
                self.dma("sp", gt[:], self.gatesT.h.ap()[:, t0:t0 + N].rearrange("(c p) t -> p c t", p=128),
                         r=[self.gatesT.t()], w=[gt.t()])
                self.dma("sp", hb[:], hv[:, :, t0:t0 + N], r=[self.hcur.t()], w=[hb.t()])
                for oc in range(8):
                    n += 1
                    A_, B_, C_ = pA[n % 2], pB[n % 2], pC[n % 2]
                    u1, u2, u3 = Sl(t1[n % 2], N), Sl(t2[n % 2], N), Sl(t3[n % 2], N)
                    cs = slice(oc * 128, (oc + 1) * 128)
                    for h in range(8):
                        self.mm(A_[:, 0:N], wna[:, h, cs], a[:, h, :], h == 0, h == 7, r=[wna.t(), a.t()], w=[A_.t()])
                    for k in range(8):
                        self.mm(B_[:, 0:N], wss[:, k, cs], s_[:, k, :], k == 0, k == 7, r=[wss.t(), s_.t()], w=[B_.t()])
                    for h in range(8):
                        self.mm(C_[:, 0:N], wgq[:, h, cs], g_[:, h, :], h == 0, h == 7, r=[wgq.t(), g_.t()], w=[C_.t()])
                    self.tt("dve", u1[:], A_[:, 0:N], gt[:, oc, :], ALU.mult, r=[A_.t(), gt.t()], w=[u1.t()])
                    self.tt("dve", u2[:], B_[:, 0:N], gt[:, 8 + oc, :], ALU.mult, r=[B_.t(), gt.t()], w=[u2.t()])
                    self.tt("dve", u3[:], C_[:, 0:N], gt[:, 16 + oc, :], ALU.mult, r=[C_.t(), gt.t()], w=[u3.t()])
                    self.tt("dve", u1[:], u1[:], u2[:], ALU.add, r=[u1.t(), u2.t()], w=[u1.t()])
                    self.tt("dve", yT[:, oc, :], u1[:], u3[:], ALU.add, r=[u1.t(), u3.t()], w=[yT.t(oc)])
                for oc in range(8):
                    n += 1
                    M_ = pM[n % 2]
                    cs = slice(oc * 128, (oc + 1) * 128)
                    for k in range(8):
                        self.mm(M_[:, 0:N], wo[:, k, cs], yT[:, k, :], k == 0, k == 7, r=[wo.t(), yT.t()], w=[M_.t()])
                    self.stt(hb[:, oc, :], M_[:, 0:N], mv[:, 16 + oc, j:j + 1], hb[:, oc, :], ALU.mult, ALU.add,
                             r=[M_.t(), hb.t(oc)], w=[hb.t(oc)])
                self.dma("sp", hv[:, :, t0:t0 + N], hb[:], r=[hb.t()], w=[self.hcur.t()])

    def ffnstage(self, l, need_ctx):
        src, dst = self.hcur, (self.hT2 if self.hcur is self.hT else self.hT)
        with self.stage():
            wup = self.sb("fwup", [128, 8, 2 * FH], BF16)
            wdn = self.sb("fwdn", [128, 22, 1024], BF16)
            uv = self.ffn_w_up.ap()[l].rearrange("(k p) n -> p k n", p=128)
            for c in (0, 5, 6, 1, 7, 2, 8, 3, 9, 4, 10):
                self.dma("pool", wup[:, :, c * 512:(c + 1) * 512], uv[:, :, c * 512:(c + 1) * 512], r=[], w=[wup.t(c)])
            dv = self.ffn_w_down.ap()[l].rearrange("(c p) n -> p c n", p=128)
            for c in range(2):
                self.dma("pool", wdn[:, c * 11:(c + 1) * 11, :], dv[:, c * 11:(c + 1) * 11, :], r=[], w=[wdn.t(c)])
            fcw = self.sb("ffcw", [128, 44, 3], F32)
            fcb = self.sb("ffcb", [128, 44], F32)
            for k in range(3):
                self.dma("sp", fcw[:, :, k], self.ffn_conv_w.ap()[l, k].rearrange("(c p) -> p c", p=128), r=[], w=[fcw.t(k)])
            self.dma("sp", fcb[:], self.ffn_conv_b.ap()[l].rearrange("(c p) -> p c", p=128), r=[], w=[fcb.t()])
            NB = 384
            NH = NB + 2
            hbs = [self.sb("fhb%d" % i, [128, 8, NH], F32) for i in range(2)]
            tmp = [self.sb("ftmp%d" % i, [128, NH], F32) for i in range(2)]
            rs = self.sb("frs", [128, NH], F32)
            xns = [self.sb("fxn%d" % i, [128, 8, NH], BF16) for i in range(2)]
            ua = [self.sb("fua%d" % i, [128, NB], F32) for i in range(2)]
            ub = [self.sb("fub%d" % i, [128, NB], F32) for i in range(2)]
            sa = [self.sb("fsa%d" % i, [128, NB], BF16) for i in range(2)]
            t0s = [self.sb("ft0_%d" % i, [128, NB], F32) for i in range(4)]
            actb = self.sb("factb", [128, 22, NB], BF16)

            class SqView:
                name = actb.name

                def t(self_, slot=None):
                    return actb.t()

                def __getitem__(self_, k):
                    v = actb[:].rearrange("p c n -> p (c n)")[:, 0:8 * NH].rearrange("p (k n) -> p k n", k=8)
                    return v[k]
            sq = SqView()
            ssp = self.ps("fssp", [128, 512], F32)
            pa = [self.ps("fpa%d" % i, [128, 512], F32) for i in range(2)]
            pb_ = [self.ps("fpb%d" % i, [128, 512], F32) for i in range(2)]
            pd = [self.ps("fpd%d" % i, [128, 512], F32) for i in range(2)]
            sv = src.h.ap().rearrange("(k p) t -> p k t", p=128)
            dvw = dst.h.ap().rearrange("(k p) t -> p k t", p=128)
            mv = self.modv[l]
            blocks = [(0, LC, 1, 0, LC)] if need_ctx else []
            t_ = LC
            while t_ < NT:
                nb_ = min(NB, NT - t_)
                blocks.append((t_, nb_, 0, LC, NT))
                t_ += nb_
            n = 0

            def prep(bi):
                t0, nb, j, s0, s1 = blocks[bi]
                nh = nb + 2
                hb, xn = hbs[bi % 2], xns[bi % 2]
                lo, hi = max(t0 - 1, s0), min(t0 + nb + 1, s1)
                a0 = lo - (t0 - 1)
                if lo > t0 - 1:
                    self.memset("dve", hb[:, :, 0:1], 0.0, w=[hb.t()])
                if hi < t0 + nb + 1:
                    self.memset("dve", hb[:, :, nh - 1:nh], 0.0, w=[hb.t()])
                self.dma("sp", hb[:, :, a0:a0 + hi - lo], sv[:, :, lo:hi], r=[src.t()], w=[hb.t()])
                self.norm_block(hb, nh, self.A2[l], mv, 24, j, lambda k, xn=xn, nh=nh: xn[:, k, 0:nh], [xn.t()], xn, ssp, rs, tmp)
                if lo > t0 - 1:
                    self.memset("dve", xn[:, :, 0:1], 0.0, w=[xn.t()])
                if hi < t0 + nb + 1:
                    self.memset("dve", xn[:, :, nh - 1:nh], 0.0, w=[xn.t()])

            prep(0)
            for bi, (t0, nb, j, s0, s1) in enumerate(blocks):
                nh = nb + 2
                hb, xn = hbs[bi % 2], xns[bi % 2]
                for c in range(22):
                    n += 1
                    A_, B_ = pa[n % 2], pb_[n % 2]
                    va, vb, vs = ua[n % 2], ub[n % 2], sa[n % 2]
                    for (P_, cc, v) in ((A_, c, va), (B_, 22 + c, vb)):
                        for k in range(8):
                            self.mm(P_[:, 0:nh], wup[:, k, cc * 128:(cc + 1) * 128], xn[:, k, 0:nh], k == 0, k == 7,
                                    r=[wup.t(cc // 4), xn.t()], w=[P_.t()])
                        self.act(v[:, 0:nb], P_[:, 1:nb + 1], AF.Identity, scale=fcw[:, cc, 1:2], bias=fcb[:, cc:cc + 1],
                                 r=[P_.t(), fcw.t(), fcb.t()], w=[v.t()])
                        t0_ = t0s[(2 * n + (cc >= 22)) % 4]
                        self.act(t0_[:, 0:nb], P_[:, 0:nb], AF.Identity, scale=fcw[:, cc, 0:1], bias=self.zero_t[:, 0:1], r=[P_.t(), fcw.t()], w=[t0_.t()])
                        self.tt("dve", v[:, 0:nb], v[:, 0:nb], t0_[:, 0:nb], ALU.add, r=[v.t(), t0_.t()], w=[v.t()])
                        self.stt(v[:, 0:nb], P_[:, 2:nb + 2], fcw[:, cc, 2:3], v[:, 0:nb], ALU.mult, ALU.add, r=[P_.t(), v.t()], w=[v.t()])
                    self.act(vs[:, 0:nb], va[:, 0:nb], AF.Silu, r=[va.t()], w=[vs.t()])
                    self.tt("dve", actb[:, c, 0:nb], vs[:, 0:nb], vb[:, 0:nb], ALU.mult, r=[vs.t(), vb.t()], w=[actb.t()])
                if bi + 1 < len(blocks):
                    prep(bi + 1)
                for oc in range(8):
                    n += 1
                    D_ = pd[n % 2]
                    for c in range(22):
                        self.mm(D_[:, 0:nb], wdn[:, c, oc * 128:(oc + 1) * 128], actb[:, c, 0:nb], c == 0, c == 21,
                                r=[wdn.t(c // 11), actb.t()], w=[D_.t()])
                    self.stt(hb[:, oc, 1:nb + 1], D_[:, 0:nb], mv[:, 40 + oc, j:j + 1], hb[:, oc, 1:nb + 1], ALU.mult, ALU.add,
                             r=[D_.t(), hb.t()], w=[hb.t()])
                self.dma("sp", dvw[:, :, t0:t0 + nb], hb[:, :, 1:nb + 1], r=[hb.t()], w=[dst.t()])
        self.hcur = dst

    def finalstage(self):
        with self.stage():
            fw = self.sb("zfw", [128, 8], F32)
            self.dma("sp", fw[:], self.final_norm_w.ap().rearrange("(k p) -> p k", p=128), r=[], w=[fw.t()])
            NB = 256
            hbs = [self.sb("zhb%d" % i, [128, 8, NB], F32) for i in range(2)]
            sq = self.sb("zsq", [128, 8, NB], BF16)
            rs = self.sb("zrs", [128, NB], F32)
            tmp = self.sb("ztmp", [128, 8, NB], F32)
            osb = [self.sb("zosb%d" % i, [128, 1024], F32) for i in range(2)]
            ssp = self.ps("zssp", [128, 512], F32)
            pT = [self.ps("zpT%d" % i, [128, 1024], F32) for i in range(2)]
            sv = self.hcur.h.ap().rearrange("(k p) t -> p k t", p=128)
            n = 0
            for bi in range(SEQ // NB):
                t0 = LC + bi * NB
                hb = hbs[bi % 2]
                self.dma("sp", hb[:], sv[:, :, t0:t0 + NB], r=[self.hcur.t()], w=[hb.t()])
                self.act(sq[:], hb[:], AF.Square, r=[hb.t()], w=[sq.t()])
                for k in range(8):
                    self.mm(ssp[:, 0:NB], self.ones_b[:], sq[:, k, :], k == 0, k == 7, r=[sq.t()], w=[ssp.t()])
                self.act(rs[:], ssp[:, 0:NB], AF.Sqrt, bias=self.eps_t[:, 0:1], scale=1.0 / D, r=[ssp.t()], w=[rs.t()])
                self.S.add("dve", lambda e: e.reciprocal(out=rs[:], in_=rs[:]), r=[rs.t()], w=[rs.t()])
                self.tt("dve", tmp[:], hb[:], rs[:].unsqueeze(1).broadcast_to([128, 8, NB]), ALU.mult, r=[hb.t(), rs.t()], w=[tmp.t()])
                self.tt("dve", tmp[:], tmp[:], fw[:].unsqueeze(2).broadcast_to([128, 8, NB]), ALU.mult, r=[tmp.t(), fw.t()], w=[tmp.t()])
                for tt_ in range(NB // 128):
                    n += 1
                    p, ob = pT[n % 2], osb[n % 2]
                    for k in range(8):
                        self.tr(p[:, k * 128:(k + 1) * 128], tmp[:, k, tt_ * 128:(tt_ + 1) * 128], self.ident_f[:], r=[tmp.t()], w=[p.t()])
                    self.cp("act" if n % 2 else "dve", ob[:], p[:], r=[p.t()], w=[ob.t()])
                    r0 = bi * NB + tt_ * 128
                    self.dma("sp", self.out.h.ap()[r0:r0 + 128, :], ob[:], r=[ob.t()], w=[self.out.t()])

    def build_all(self):
        self.declare_io()
        self.consts()
        self.prologue()
        self.modstage()
        for l in range(DEPTH):
            need_ctx = l < DEPTH - 1
            with contextlib.ExitStack() as outer:
                self.instage(l, outer)
            self.nastage(l, need_ctx)
            self.gqastage(l, need_ctx)
            self.ssdstage(l, need_ctx)
            self.mergestage(l, need_ctx)
            self.ffnstage(l, need_ctx)
        self.finalstage()

    def dump(self, name, src_ap, shape, dt, r):
        d = self.dram(name, shape, dt, out=True)
        self.dma("sp", d.h.ap(), src_ap, r=r, w=[d.t()])
        return d


NA_TABLES = [(dr, True, True) for dr in range(14)] + [(2, False, True), (10, True, False)]


def host_tables(na_rel_bias):
    L = na_rel_bias.shape[0]
    p = np.arange(128)
    half, kc = p // 64, p % 64
    qc = np.arange(64)
    dcol = np.clip(kc[:, None] - qc[None, :] + 15, 0, 30)
    win0 = np.clip(qc - 8, 0, 48)
    ok = (kc[:, None] >= win0[None, :]) & (kc[:, None] < win0[None, :] + 16)
    bias_g = np.zeros((L, 16, 128, 8, 64), np.float32)
    mask = np.zeros((16, 128, 8, 64), np.float32)
    for tb, (dr, v0, v1) in enumerate(NA_TABLES):
        drow = dr + half
        g = na_rel_bias[:, :, drow[:, None], dcol]
        bias_g[:, tb] = np.transpose(g, (0, 2, 1, 3))
        rv = np.where(half == 0, v0, v1)
        mask[tb] = (ok & rv[:, None])[:, None, :].astype(np.float32)
    n_freq = 16
    inv_freq = (10000.0 ** (-np.arange(n_freq, dtype=np.float32) / n_freq)).astype(np.float32)
    t = np.arange(SEQ)
    row = (t // GW).astype(np.float32)
    col = (t % GW).astype(np.float32)
    ang = np.concatenate([row[:, None] * inv_freq, col[:, None] * inv_freq], axis=-1).astype(np.float32)
    rope = np.stack([np.cos(ang), np.sin(ang)], axis=1).astype(np.float32)
    return bias_g.reshape(L, 16, 128, 512), mask.reshape(16, 128, 512), rope


def make_in_maps(inputs, n_cores=8):
    f = lambda a: np.ascontiguousarray(np.asarray(a, dtype=np.float32))
    bias_g, mask, rope = host_tables(f(inputs["na_rel_bias"]))
    shared = {k: f(v) for k, v in inputs.items() if k not in ("x", "c", "ctx", "na_rel_bias")}
    shared["na_bias_g"] = bias_g
    shared["na_mask"] = mask
    shared["rope"] = rope
    x, c, ctx = f(inputs["x"]), f(inputs["c"]), f(inputs["ctx"])
    maps = []
    for b in range(n_cores):
        m = dict(shared)
        m["x"], m["c"], m["ctx"] = x[b], c[b], ctx[b]
        maps.append(m)
    return maps


_CACHE = {}


def kernel(**inputs):
    n_cores = 8
    if "kb" not in _CACHE:
        kb = KB()
        kb.build_all()
        _CACHE["kb"] = kb
    kb = _CACHE["kb"]
    maps = make_in_maps(inputs, n_cores)
    used = set(kb.din.keys())
    maps = [{k: v for k, v in m.items() if k in used} for m in maps]
    res = run_bass_kernel_spmd(kb.nc, maps, core_ids=list(range(n_cores)))
    return np.stack([np.asarray(r["out"], dtype=np.float32) for r in res.results], axis=0)
```

```python
import numpy as np
import ml_dtypes
import concourse.bass as bass
import concourse.mybir as mybir
from concourse.bass_utils import run_bass_kernel_spmd

F32, BF16 = mybir.dt.float32, mybir.dt.bfloat16
AF = mybir.ActivationFunctionType
ALU = mybir.AluOpType
AX = mybir.AxisListType

D = 1024
SEQ = 4096
LC = 256
NT = SEQ + LC
NTILE = NT // 128
DEPTH = 2
GW = 64
EPS = 1e-6
DIN = 7712
FH = 2816
C_NAQ, C_NAK, C_NAV, C_Z, C_XBC, C_DT, C_GQ, C_GK, C_GV, C_GATE = 0, 512, 1024, 1536, 2560, 3840, 3872, 4384, 4512, 4640


class Op:
    __slots__ = ("eng", "fn", "deps", "dma", "sem", "val", "flag", "semi")


class Sched:
    COMPUTE = ("pe", "act", "dve", "pool")

    def __init__(self, nc):
        self.nc = nc
        self.csem = {e: nc.alloc_semaphore(name="cs_" + e) for e in self.COMPUTE}
        self.ccnt = {e: 0 for e in self.COMPUTE}
        self.dpool = {"sp": [nc.alloc_semaphore(name="dsp%d" % i) for i in range(4)],
                      "pool": [nc.alloc_semaphore(name="dpl%d" % i) for i in range(2)]}
        self.duse = {q: [0] * len(v) for q, v in self.dpool.items()}
        self.drr = {q: 0 for q in self.dpool}
        self.waited = {e: {} for e in ("pe", "act", "dve", "pool", "sp")}
        self.nops = 0
        self.psum_names = set()
        self.begin()

    def begin(self):
        self.ops = []
        self.trk = {}
        self.dlast = {}

    def _entries(self, tok, create):
        name, slot = tok
        d = self.trk.setdefault(name, {})
        if slot is None:
            if create and None not in d:
                d[None] = [None, []]
            return list(d.values())
        out = []
        if None in d:
            out.append(d[None])
        if slot not in d and create:
            d[slot] = [None, []]
        if slot in d:
            out.append(d[slot])
        return out

    def add(self, eng, fn, r=(), w=(), dma=False):
        op = Op()
        op.eng, op.fn, op.dma, op.flag, op.sem, op.val = eng, fn, dma, False, None, 0
        idx = len(self.ops)
        deps = set()

        def want(d, kind):
            if d is None:
                return
            o = self.ops[d]
            if not o.dma and not dma and o.eng == eng:
                if eng == "pe" or kind != "raw":
                    return
            deps.add(d)

        for tok in r:
            tok = tok if isinstance(tok, tuple) else (tok, None)
            for e in self._entries(tok, True):
                want(e[0], "raw")
                if tok[0] in self.psum_names:
                    for rd in e[1]:
                        want(rd, "rar")
        for tok in w:
            tok = tok if isinstance(tok, tuple) else (tok, None)
            for e in self._entries(tok, True):
                want(e[0], "waw")
                for rd in e[1]:
                    want(rd, "war")
        for tok in r:
            tok = tok if isinstance(tok, tuple) else (tok, None)
            name, slot = tok
            self._entries(tok, True)
            tgt = self.trk[name][slot]
            if dma:
                tgt[1].append(idx)
            else:
                tgt[1][:] = [x for x in tgt[1] if self.ops[x].dma or self.ops[x].eng != eng]
                tgt[1].append(idx)
        for tok in w:
            tok = tok if isinstance(tok, tuple) else (tok, None)
            name, slot = tok
            d = self.trk[name]
            if slot is None:
                d.clear()
                d[None] = [idx, []]
            else:
                d[slot] = [idx, []]
        if dma:
            q = eng
            j = self.drr[q]
            self.drr[q] = (j + 1) % len(self.dpool[q])
            op.semi = j
            prev = self.dlast.get((q, j))
            if prev is not None:
                deps.add(prev)
            self.dlast[(q, j)] = idx
        op.deps = deps
        self.ops.append(op)
        return idx

    def emit(self):
        nc = self.nc
        ops = self.ops
        for op in ops:
            for d in op.deps:
                ops[d].flag = True
        for op in ops:
            if op.dma:
                q = op.eng
                self.duse[q][op.semi] += 1
                op.sem = self.dpool[q][op.semi]
                op.val = 16 * self.duse[q][op.semi]
            elif op.flag:
                self.ccnt[op.eng] += 1
                op.sem = self.csem[op.eng]
                op.val = self.ccnt[op.eng]
        self.nops += len(ops)
        sched = self

        def run(engname, eng):
            wd = sched.waited[engname]
            for op in ops:
                if op.eng != engname:
                    continue
                need = {}
                for d in op.deps:
                    o = ops[d]
                    key = id(o.sem)
                    if key not in need or need[key][1] < o.val:
                        need[key] = (o.sem, o.val)
                lst = []
                for key, (s, v) in need.items():
                    if wd.get(key, 0) >= v:
                        continue
                    wd[key] = v
                    lst.append((s, v))
                if op.dma or len(lst) > 1:
                    extra = lst if op.dma else lst[:-1]
                    for s, v in extra:
                        eng.wait_ge(s, v)
                    lst = [] if op.dma else lst[-1:]
                ins = op.fn(eng)
                if lst:
                    ins._wait_ge(lst[0][0], lst[0][1])
                if op.dma:
                    ins.then_inc(op.sem, 16)
                elif op.flag:
                    ins.then_inc(op.sem, 1)
            if engname in sched.dpool:
                for j, s in enumerate(sched.dpool[engname]):
                    v = 16 * sched.duse[engname][j]
                    if v and wd.get(id(s), 0) < v:
                        wd[id(s)] = v
                        eng.wait_ge(s, v)

        used = set(op.eng for op in ops)
        with nc.Block() as block:
            if "pe" in used:
                @block.tensor
                def _(e):
                    run("pe", e)
            if "act" in used:
                @block.scalar
                def _(e):
                    run("act", e)
            if "dve" in used:
                @block.vector
                def _(e):
                    run("dve", e)
            if "pool" in used:
                @block.gpsimd
                def _(e):
                    run("pool", e)
            if "sp" in used:
                @block.sync
                def _(e):
                    run("sp", e)
        self.begin()


class Buf:
    def __init__(self, name, h):
        self.name, self.h = name, h

    def t(self, slot=None):
        return (self.name, slot)

    def __getitem__(self, k):
        return self.h[k]


class Sl:
    def __init__(self, buf, n):
        self.b, self.n, self.name = buf, n, buf.name

    def t(self, slot=None):
        return self.b.t(slot)

    def __getitem__(self, k):
        if not isinstance(k, tuple):
            k = (k,)
        nd = len(self.b.h.shape)
        k = tuple(k) + (slice(None),) * (nd - len(k))
        last = k[-1]
        if isinstance(last, slice) and last == slice(None):
            k = k[:-1] + (slice(0, self.n),)
        return self.b.h[k]


def rap(buf, row, p0, npart, off, dims):
    return bass.AP(buf.h, p0 * row + off, [[row, npart]] + [list(d) for d in dims])


import contextlib

NA_NOPOOL = True
PENG = "dve"
TOKBLKS = [(0, 256, 1)] + [(256 + 512 * i, 512, 0) for i in range(8)]


class KB:
    def __init__(self, dbg=()):
        self.dbg = set(dbg)
        nc = self.nc = bass.Bass("TRN2", target_bir_lowering=False)
        self.S = Sched(nc)
        self.ges = contextlib.ExitStack()
        self.es = None
        self.din = {}
        self._uid = 0

    def inp(self, name, shape, dt=F32):
        t = self.nc.dram_tensor(name, list(shape), dt, kind="ExternalInput")
        self.din[name] = t
        return t

    def dram(self, name, shape, dt, out=False):
        kind = "ExternalOutput" if (out or name in self.dbg) else "Internal"
        return Buf(name, self.nc.dram_tensor(name, list(shape), dt, kind=kind))

    def sb(self, name, shape, dt, glob=False, es=None):
        es = es if es is not None else (self.ges if glob else self.es)
        self._uid += 1
        name = "%s_u%d" % (name, self._uid)
        return Buf(name, es.enter_context(self.nc.sbuf_tensor(name, list(shape), dt)))

    def ps(self, name, shape, dt=F32):
        self._uid += 1
        name = "%s_u%d" % (name, self._uid)
        self.S.psum_names.add(name)
        return Buf(name, self.es.enter_context(self.nc.psum_tensor(name, list(shape), dt)))

    @contextlib.contextmanager
    def stage(self):
        self.es = contextlib.ExitStack()
        with self.es:
            yield
            with self.nc.allow_non_contiguous_dma(reason="small strided parameter loads"):
                self.S.emit()
        self.es = None

    def dma(self, q, out, in_, r, w):
        self.S.add(q, lambda e: e.dma_start(out=out, in_=in_), r=r, w=w, dma=True)

    def mm(self, out, lhsT, rhs, start, stop, r, w):
        self.S.add("pe", lambda e: e.matmul(out, lhsT, rhs, start=start, stop=stop), r=r, w=w)

    def tr(self, out, in_, ident, r, w):
        self.S.add("pe", lambda e: e.transpose(out, in_, ident), r=r, w=w)

    def act(self, out, in_, func, r, w, bias=0.0, scale=1.0, accum_out=None):
        if accum_out is None:
            self.S.add("act", lambda e: e.activation(out=out, in_=in_, func=func, bias=bias, scale=scale), r=r, w=w)
        else:
            self.S.add("act", lambda e: e.activation(out=out, in_=in_, func=func, bias=bias, scale=scale,
                                                     accum_out=accum_out), r=r, w=w)

    def tt(self, eng, out, in0, in1, op, r, w):
        self.S.add(eng, lambda e: e.tensor_tensor(out=out, in0=in0, in1=in1, op=op), r=r, w=w)

    def ts(self, eng, out, in0, s1, s2, op0, op1, r, w):
        if s2 is None:
            self.S.add(eng, lambda e: e.tensor_scalar(out=out, in0=in0, scalar1=s1, scalar2=None, op0=op0), r=r, w=w)
        else:
            self.S.add(eng, lambda e: e.tensor_scalar(out=out, in0=in0, scalar1=s1, scalar2=s2, op0=op0, op1=op1),
                       r=r, w=w)

    def stt(self, out, in0, scalar, in1, op0, op1, r, w):
        self.S.add("dve", lambda e: e.scalar_tensor_tensor(out=out, in0=in0, scalar=scalar, in1=in1, op0=op0, op1=op1),
                   r=r, w=w)

    def cp(self, eng, out, in_, r, w):
        if eng == "act":
            self.S.add("act", lambda e: e.copy(out=out, in_=in_), r=r, w=w)
        else:
            self.S.add(eng, lambda e: e.tensor_copy(out=out, in_=in_), r=r, w=w)

    def memset(self, eng, ap, val, w):
        self.S.add(eng, lambda e: e.memset(ap, val), r=(), w=w)

    def declare_io(self):
        L = DEPTH
        self.x = self.inp("x", [SEQ, D])
        self.c = self.inp("c", [D])
        self.ctx = self.inp("ctx", [LC, D])
        self.c_ctx = self.inp("c_ctx", [D])
        self.w_mod = self.inp("w_mod", [L, D, 6 * D])
        self.b_mod = self.inp("b_mod", [L, 6 * D])
        self.norm1_w = self.inp("norm1_w", [L, D])
        self.norm2_w = self.inp("norm2_w", [L, D])
        self.w_in = self.inp("w_in", [L, D, DIN])
        self.na_bias_g = self.inp("na_bias_g", [L, 16, 128, 512])
        self.na_mask = self.inp("na_mask", [16, 128, 512])
        self.rope = self.inp("rope", [SEQ, 2, 32])
        self.ssm_conv_w = self.inp("ssm_conv_w", [L, 3, 1280])
        self.ssm_conv_b = self.inp("ssm_conv_b", [L, 1280])
        self.ssm_a_log = self.inp("ssm_a_log", [L, 2, 16])
        self.ssm_dt_bias = self.inp("ssm_dt_bias", [L, 2, 16])
        self.ssm_d = self.inp("ssm_d", [L, 16])
        self.ssm_norm_w = self.inp("ssm_norm_w", [L, 1024])
        self.q_norm_w = self.inp("q_norm_w", [L, 64])
        self.k_norm_w = self.inp("k_norm_w", [L, 64])
        self.w_out_na = self.inp("w_out_na", [L, 512, D])
        self.w_out_ssm = self.inp("w_out_ssm", [L, 1024, D])
        self.w_out_gqa = self.inp("w_out_gqa", [L, 512, D])
        self.w_o = self.inp("w_o", [L, D, D])
        self.ffn_w_up = self.inp("ffn_w_up", [L, D, 2 * FH])
        self.ffn_conv_w = self.inp("ffn_conv_w", [L, 3, 2 * FH])
        self.ffn_conv_b = self.inp("ffn_conv_b", [L, 2 * FH])
        self.ffn_w_down = self.inp("ffn_w_down", [L, FH, D])
        self.final_norm_w = self.inp("final_norm_w", [D])
        self.out = self.dram("out", [SEQ, D], F32, out=True)
        self.hT = self.dram("hT", [D, NT], F32)
        self.hT2 = self.dram("hT2", [D, NT], F32)
        self.hcur = self.hT
        self.naqT = self.dram("naqT", [512, NT], BF16)
        self.nakT = self.dram("nakT", [512, NT], BF16)
        self.nav = self.dram("nav", [NT, 512], BF16)
        self.zs = self.dram("zs", [NT, 1024], BF16)
        self.xbcT = self.dram("xbcT", [1280, NT], BF16)
        self.dtr = self.dram("dtr", [NT, 32], F32)
        self.gqT = self.dram("gqT", [512, NT], BF16)
        self.gkT2 = self.dram("gkT2", [2, 128, NT], BF16)
        self.gv = self.dram("gv", [NT, 128], BF16)
        self.gatesT = self.dram("gatesT", [3072, NT], BF16)
        self.onaT = self.dram("onaT", [8, 64, NT], BF16)
        self.ogT = self.dram("ogT", [8, 64, NT], BF16)
        self.ossmT = self.dram("ossmT", [1024, NT], BF16)
        self.yf = self.dram("yf", [NT, 1024], F32)

    def consts(self):
        nc = self.nc
        self.ident_f = self.sb("ident_f", [128, 128], F32, glob=True)
        self.ident_b = self.sb("ident_b", [128, 128], BF16, glob=True)
        self.ones_b = self.sb("ones_b", [128, 128], BF16, glob=True)
        self.ones_f = self.sb("ones_f", [128, 128], F32, glob=True)
        self.modv = [self.sb("modv%d" % l, [128, 48, 2], F32, glob=True) for l in range(DEPTH)]
        self.A1 = [self.sb("A1_%d" % l, [128, 8, 2], F32, glob=True) for l in range(DEPTH)]
        self.A2 = [self.sb("A2_%d" % l, [128, 8, 2], F32, glob=True) for l in range(DEPTH)]
        self.eps_t = self.sb("eps_t", [128, 1], F32, glob=True)
        self.zero_t = self.sb("zero_t", [128, 1], F32, glob=True)
        with self.stage():
            idf, idb, ob, of = self.ident_f, self.ident_b, self.ones_b, self.ones_f
            self.memset("dve", self.eps_t[:], EPS, w=[self.eps_t.t()])
            self.memset("dve", self.zero_t[:], 0.0, w=[self.zero_t.t()])
            self.memset("pool", idf[:], 0.0, w=[idf.t()])
            self.S.add("pool", lambda e: e.affine_select(out=idf[:], in_=idf[:], pattern=[[-1, 128]],
                                                         compare_op=ALU.not_equal, fill=1.0, base=0,
                                                         channel_multiplier=1), r=[idf.t()], w=[idf.t()])
            self.cp("dve", idb[:], idf[:], r=[idf.t()], w=[idb.t()])
            self.memset("dve", ob[:], 1.0, w=[ob.t()])
            self.memset("dve", of[:], 1.0, w=[of.t()])

    def prologue(self):
        with self.stage():
            xin = [self.sb("xin%d" % i, [128, D], F32) for i in range(3)]
            pt = [self.ps("ptr%d" % i, [128, 1024], F32) for i in range(2)]
            stg = [self.sb("stg%d" % i, [128, 8, 512], F32) for i in range(2)]
            hTv = self.hT.h.ap().rearrange("(k p) t -> p k t", p=128)
            groups = [(0, 2)] + [(2 + 4 * g, 4) for g in range(8)]
            craw = self.sb("craw", [128, 8, 2], F32)
            csil = self.sb("csil", [128, 8, 2], F32)
            self.dma("sp", craw[:, :, 0], self.c.ap().rearrange("(k p) -> p k", p=128), r=[], w=[craw.t(0)])
            self.dma("sp", craw[:, :, 1], self.c_ctx.ap().rearrange("(k p) -> p k", p=128), r=[], w=[craw.t(1)])
            self.act(csil[:], craw[:], AF.Silu, r=[craw.t()], w=[csil.t()])
            wm = [self.sb("wm%d" % i, [128, 8, 512], F32) for i in range(2)]
            bm = self.sb("bm", [128, 48], F32)
            nw = self.sb("nw", [128, 8], F32)
            pms = [self.ps("pmod%d" % l, [128, 512], F32) for l in range(DEPTH)]
            cnt = {"n": 0}

            def mod_group(l, og):
                pm = pms[l]
                wv = self.w_mod.ap()[l].rearrange("(k p) n -> p k n", p=128)
                w = wm[cnt["n"] % 2]
                cnt["n"] += 1
                self.dma("sp", w[:], wv[:, :, og * 512:(og + 1) * 512], r=[], w=[w.t()])
                for oc in range(4):
                    o = og * 4 + oc
                    for k in range(8):
                        self.mm(pm[:, 2 * o:2 * o + 2], w[:, k, oc * 128:(oc + 1) * 128], csil[:, k, :],
                                k == 0, k == 7, r=[w.t(), csil.t()], w=[pm.t()])

            def mod_finish(l):
                pm = pms[l]
                self.dma("sp", bm[:], self.b_mod.ap()[l].rearrange("(o p) -> p o", p=128), r=[], w=[bm.t()])
                mv = self.modv[l]
                self.tt("dve", mv[:], pm[:, 0:96].rearrange("p (o j) -> p o j", j=2),
                        bm[:].unsqueeze(2).broadcast_to([128, 48, 2]), ALU.add, r=[pm.t(), bm.t()], w=[mv.t()])
                for (A, nwin, sc0) in ((self.A1[l], self.norm1_w, 8), (self.A2[l], self.norm2_w, 32)):
                    self.dma("sp", nw[:], nwin.ap()[l].rearrange("(k p) -> p k", p=128), r=[], w=[nw.t()])
                    self.ts("dve", A[:], mv[:, sc0:sc0 + 8, :], 1.0, None, ALU.add, None, r=[mv.t()], w=[A.t()])
                    self.tt("dve", A[:], A[:], nw[:].unsqueeze(2).broadcast_to([128, 8, 2]), ALU.mult,
                            r=[A.t(), nw.t()], w=[A.t()])

            todo = [(l, og) for l in range(DEPTH) for og in range(12)]
            n = 0
            for gi, (ti0, cntg) in enumerate(groups):
                sg = stg[gi % 2]
                for j in range(cntg):
                    ti = ti0 + j
                    xb = xin[n % 3]
                    p = pt[n % 2]
                    n += 1
                    src = self.ctx.ap()[ti * 128:(ti + 1) * 128, :] if ti < 2 else \
                        self.x.ap()[(ti - 2) * 128:(ti - 1) * 128, :]
                    self.dma("sp", xb[:], src, r=[], w=[xb.t()])
                    for k in range(8):
                        self.tr(p[:, k * 128:(k + 1) * 128], xb[:, k * 128:(k + 1) * 128], self.ident_f[:],
                                r=[xb.t()], w=[p.t()])
                    dst = sg[:, :, j * 128:(j + 1) * 128]
                    src_ps = p[:].rearrange("p (k t) -> p k t", k=8)
                    self.cp("act" if n % 2 else "dve", dst, src_ps, r=[p.t()], w=[sg.t(j)])
                t0 = ti0 * 128
                self.dma("sp", hTv[:, :, t0:t0 + cntg * 128], sg[:, :, 0:cntg * 128], r=[sg.t()], w=[self.hT.t()])
                for _ in range(3):
                    if todo:
                        l_, og_ = todo.pop(0)
                        mod_group(l_, og_)
                        if og_ == 11:
                            mod_finish(l_)
            while todo:
                l_, og_ = todo.pop(0)
                mod_group(l_, og_)
                if og_ == 11:
                    mod_finish(l_)

    def modstage(self):
        pass

    def norm_block(self, hb, N, A, mv, b0, j, out_fn, out_w, sq, ssp, rs, tmp):
        self.act(sq[:, :, 0:N], hb[:, :, 0:N], AF.Square, r=[hb.t()], w=[sq.t()])
        for k in range(8):
            self.mm(ssp[:, 0:N], self.ones_b[:], sq[:, k, 0:N], k == 0, k == 7, r=[sq.t()], w=[ssp.t()])
        self.act(rs[:, 0:N], ssp[:, 0:N], AF.Sqrt, bias=self.eps_t[:, 0:1], scale=1.0 / D, r=[ssp.t()], w=[rs.t()])
        self.S.add("dve", lambda e: e.reciprocal(out=rs[:, 0:N], in_=rs[:, 0:N]), r=[rs.t()], w=[rs.t()])
        if isinstance(tmp, list):
            for k in range(8):
                tk = tmp[k % len(tmp)]
                self.tt("dve", tk[:, 0:N], hb[:, k, 0:N], rs[:, 0:N], ALU.mult, r=[hb.t(), rs.t()], w=[tk.t()])
                self.act(out_fn(k), tk[:, 0:N], AF.Identity, scale=A[:, k, j:j + 1], bias=mv[:, b0 + k, j:j + 1],
                         r=[tk.t()], w=out_w)
            return
        self.tt("dve", tmp[:, :, 0:N], hb[:, :, 0:N], rs[:, 0:N].unsqueeze(1).broadcast_to([128, 8, N]), ALU.mult,
                r=[hb.t(), rs.t()], w=[tmp.t()])
        for k in range(8):
            self.act(out_fn(k), tmp[:, k, 0:N], AF.Identity, scale=A[:, k, j:j + 1], bias=mv[:, b0 + k, j:j + 1],
                     r=[tmp.t()], w=out_w)

    def instage(self, l, outer):
        xnT = self.sb("xnT", [128, 8, NT], BF16, es=outer)
        hTv = self.hcur.h.ap().rearrange("(k p) t -> p k t", p=128)
        with self.stage():
            hbs = [self.sb("hb%d" % i, [128, 8, 512], F32) for i in range(3)]
            sqs = [self.sb("sq%d" % i, [128, 8, 512], BF16) for i in range(2)]
            tmps = [self.sb("ntmp%d" % i, [128, 8, 512], F32) for i in range(2)]
            rss = [self.sb("nrs%d" % i, [128, 512], F32) for i in range(2)]
            ssps = [self.ps("nssp%d" % i, [128, 512], F32) for i in range(2)]
            A, mv = self.A1[l], self.modv[l]

            def X(bi):
                t0, N, isctx = TOKBLKS[bi]
                hb, sq, rs, ssp = hbs[bi % 3], sqs[bi % 2], rss[bi % 2], ssps[bi % 2]
                self.dma("sp", hb[:, :, 0:N], hTv[:, :, t0:t0 + N], r=[self.hcur.t()], w=[hb.t()])
                self.act(sq[:, :, 0:N], hb[:, :, 0:N], AF.Square, r=[hb.t()], w=[sq.t()])
                for k in range(8):
                    self.mm(ssp[:, 0:N], self.ones_b[:], sq[:, k, 0:N], k == 0, k == 7, r=[sq.t()], w=[ssp.t()])
                self.act(rs[:, 0:N], ssp[:, 0:N], AF.Sqrt, bias=self.eps_t[:, 0:1], scale=1.0 / D, r=[ssp.t()], w=[rs.t()])
                self.S.add("dve", lambda e: e.reciprocal(out=rs[:, 0:N], in_=rs[:, 0:N]), r=[rs.t()], w=[rs.t()])

            def Y(bi):
                t0, N, j = TOKBLKS[bi]
                hb, rs, tmp = hbs[bi % 3], rss[bi % 2], tmps[bi % 2]
                self.tt("dve", tmp[:, :, 0:N], hb[:, :, 0:N], rs[:, 0:N].unsqueeze(1).broadcast_to([128, 8, N]), ALU.mult,
                        r=[hb.t(), rs.t()], w=[tmp.t()])
                for k in range(8):
                    self.act(xnT[:, k, t0:t0 + N], tmp[:, k, 0:N], AF.Identity, scale=A[:, k, j:j + 1], bias=mv[:, k, j:j + 1],
                             r=[tmp.t()], w=[xnT.t(bi)])

            X(0)
            for bi in range(len(TOKBLKS)):
                if bi + 1 < len(TOKBLKS):
                    X(bi + 1)
                Y(bi)
        wv = self.w_in.ap()[l].rearrange("(k p) n -> p k n", p=128)
        with self.stage():
            Ws = [self.sb("Wc%d" % i, [128, 8, 512], BF16) for i in range(2)]
            pss = [self.ps("pin%d" % i, [128, 512], F32) for i in range(4)]
            stg = [self.sb("stgi%d" % i, [128, 4, 512], BF16) for i in range(3)]
            stgf = [self.sb("stgf%d" % i, [128, 4, 32], F32) for i in range(2)]
            cnt = {"w": 0, "p": 0, "s": 0, "e": 0}

            def nxt(lst, key):
                b = lst[cnt[key] % len(lst)]
                cnt[key] += 1
                return b

            def evac(func, out, in_, r, w):
                if func == "copy":
                    cnt["e"] += 1
                    self.cp("act" if cnt["e"] % 2 else "dve", out, in_, r=r, w=w)
                elif func == "silu":
                    self.act(out, in_, AF.Silu, r=r, w=w)
                elif func == "sigmoid":
                    self.act(out, in_, AF.Sigmoid, r=r, w=w)
                elif func == "q8":
                    self.ts("dve", out, in_, 0.125, None, ALU.mult, None, r=r, w=w)

            def blk_of_tile(ti):
                return 0 if ti < 2 else 1 + (ti - 2) // 4

            def fm_group(c0, n, func, dstT, drow0):
                W = nxt(Ws, "w")
                self.dma("pool", W[:, :, 0:n], wv[:, :, c0:c0 + n], r=[], w=[W.t()])
                for bi, (t0, N, isctx) in enumerate(TOKBLKS):
                    sg = nxt(stg, "s")
                    for oc in range(n // 128):
                        ps = nxt(pss, "p")
                        for k in range(8):
                            self.mm(ps[:, 0:N], W[:, k, oc * 128:(oc + 1) * 128], xnT[:, k, t0:t0 + N], k == 0, k == 7,
                                    r=[W.t(), xnT.t(bi)], w=[ps.t()])
                        evac(func, sg[:, oc, 0:N], ps[:, 0:N], r=[ps.t()], w=[sg.t(oc)])
                    dst = dstT.h.ap()[drow0:drow0 + n, t0:t0 + N].rearrange("(o p) t -> p o t", p=128)
                    self.dma("sp", dst, sg[:, 0:n // 128, 0:N], r=[sg.t()], w=[dstT.t()])

            def tm_group(c0, n, func, dst, dcol0, fp32=False):
                W = nxt(Ws, "w")
                self.dma("pool", W[:, :, 0:n], wv[:, :, c0:c0 + n], r=[], w=[W.t()])
                for (ti0, tc) in [(0, 2)] + [(2 + 4 * g, 4) for g in range(8)]:
                    sg = nxt(stgf, "s") if fp32 else nxt(stg, "s")
                    for j in range(tc):
                        ti = ti0 + j
                        ps = nxt(pss, "p")
                        for k in range(8):
                            self.mm(ps[:, 0:n], xnT[:, k, ti * 128:(ti + 1) * 128], W[:, k, 0:n], k == 0, k == 7,
                                    r=[W.t(), xnT.t(blk_of_tile(ti))], w=[ps.t()])
                        evac(func, sg[:, j, 0:n], ps[:, 0:n], r=[ps.t()], w=[sg.t(j)])
                    d = dst.h.ap()[ti0 * 128:(ti0 + tc) * 128, dcol0:dcol0 + n].rearrange("(j p) n -> p j n", p=128)
                    self.dma("sp", d, sg[:, 0:tc, 0:n], r=[sg.t()], w=[dst.t()])

            fm_group(C_NAQ, 512, "q8", self.naqT, 0)
            fm_group(C_NAK, 512, "copy", self.nakT, 0)
            tm_group(C_NAV, 512, "copy", self.nav, 0)
            tm_group(C_Z, 512, "silu", self.zs, 0)
            tm_group(C_Z + 512, 512, "silu", self.zs, 512)
            fm_group(C_XBC, 512, "copy", self.xbcT, 0)
            fm_group(C_XBC + 512, 512, "copy", self.xbcT, 512)
            fm_group(C_XBC + 1024, 256, "copy", self.xbcT, 1024)
            tm_group(C_DT, 32, "copy", self.dtr, 0, fp32=True)
            tm_group(C_GV, 128, "copy", self.gv, 0)
            for g in range(6):
                fm_group(C_GATE + 512 * g, 512, "sigmoid", self.gatesT, 512 * g)
            self.gqa_group(l, xnT, wv, blk_of_tile)

    def gqa_group(self, l, xnT, wv, blk_of_tile):
        wg = self.sb("wg", [128, 8, 640], BF16)
        self.dma("pool", wg[:], wv[:, :, C_GQ:C_GQ + 640], r=[], w=[wg.t()])
        ropeT = self.sb("ropeT", [128, 32, 64], F32)
        self.dma("sp", ropeT[:], self.rope.ap().rearrange("(i p) a b -> p i (a b)", p=128), r=[], w=[ropeT.t()])
        wqk = self.sb("wqk", [128, 640], F32)
        self.dma("sp", wqk[:, 0:64], self.q_norm_w.ap()[l:l + 1, :].broadcast_to([128, 64]), r=[], w=[wqk.t()])
        self.dma("sp", wqk[:, 512:576], self.k_norm_w.ap()[l:l + 1, :].broadcast_to([128, 64]), r=[], w=[wqk.t()])
        self.ts("dve", wqk[:, 0:64], wqk[:, 0:64], 0.125, None, ALU.mult, None, r=[wqk.t()], w=[wqk.t()])
        for h in range(1, 8):
            self.cp("dve", wqk[:, h * 64:(h + 1) * 64], wqk[:, 0:64], r=[wqk.t()], w=[wqk.t()])
        self.cp("dve", wqk[:, 576:640], wqk[:, 512:576], r=[wqk.t()], w=[wqk.t()])
        ps2 = self.ps("pg2", [128, 1024], F32)
        pT = self.ps("pgT", [128, 6, 128], BF16)
        sqv = self.sb("gsqv", [128, 640], F32)
        ss10 = self.sb("gss", [128, 10], F32)
        xn = self.sb("gxn", [128, 640], F32)
        t1 = self.sb("gt1", [128, 320], F32)
        t2 = self.sb("gt2", [128, 320], F32)
        t3 = self.sb("gt3", [128, 320], F32)
        t4 = self.sb("gt4", [128, 320], F32)
        qkb = self.sb("gqkb", [128, 640], BF16)
        kd = self.sb("gkd", [128, 256], BF16)
        stq = [self.sb("gstq%d" % i, [128, 6, 512], BF16) for i in range(2)]
        v3 = lambda ap, a: ap.rearrange("p (a b) -> p a b", a=a)
        gi = 0
        for (ti0, tc) in [(0, 2)] + [(2 + 4 * g, 4) for g in range(8)]:
            sg = stq[gi % 2]
            gi += 1
            for j in range(tc):
                ti = ti0 + j
                xs = [wg.t(), xnT.t(blk_of_tile(ti))]
                for k in range(8):
                    self.mm(ps2[:, 0:512], xnT[:, k, ti * 128:(ti + 1) * 128], wg[:, k, 0:512], k == 0, k == 7,
                            r=xs, w=[ps2.t(0)])
                for k in range(8):
                    self.mm(ps2[:, 512:640], xnT[:, k, ti * 128:(ti + 1) * 128], wg[:, k, 512:640], k == 0, k == 7,
                            r=xs, w=[ps2.t(1)])
                self.act(sqv[:], ps2[:, 0:640], AF.Square, r=[ps2.t()], w=[sqv.t()])
                self.S.add("dve", lambda e: e.tensor_reduce(out=ss10[:], in_=v3(sqv[:], 10), axis=AX.X, op=ALU.add),
                           r=[sqv.t()], w=[ss10.t()])
                self.act(ss10[:], ss10[:], AF.Sqrt, bias=self.eps_t[:, 0:1], scale=1.0 / 64, r=[ss10.t()], w=[ss10.t()])
                self.S.add("dve", lambda e: e.reciprocal(out=ss10[:], in_=ss10[:]), r=[ss10.t()], w=[ss10.t()])
                self.tt("dve", v3(xn[:], 10), v3(ps2[:, 0:640], 10), ss10[:].unsqueeze(2).broadcast_to([128, 10, 64]),
                        ALU.mult, r=[ps2.t(), ss10.t()], w=[xn.t()])
                self.tt("dve", xn[:], xn[:], wqk[:], ALU.mult, r=[xn.t(), wqk.t()], w=[xn.t()])
                if ti >= 2:
                    x4 = xn[:].rearrange("p (h i two) -> p h i two", h=10, two=2)
                    o4 = qkb[:].rearrange("p (h i two) -> p h i two", h=10, two=2)
                    xe, xo, oe, oo = x4[:, :, :, 0], x4[:, :, :, 1], o4[:, :, :, 0], o4[:, :, :, 1]
                    cosb = ropeT[:, ti - 2, 0:32].unsqueeze(1).broadcast_to([128, 10, 32])
                    sinb = ropeT[:, ti - 2, 32:64].unsqueeze(1).broadcast_to([128, 10, 32])
                    a1, a2, a3, a4 = v3(t1[:], 10), v3(t2[:], 10), v3(t3[:], 10), v3(t4[:], 10)
                    self.tt("dve", a1, xe, cosb, ALU.mult, r=[xn.t(), ropeT.t()], w=[t1.t()])
                    self.tt("dve", a2, xo, sinb, ALU.mult, r=[xn.t(), ropeT.t()], w=[t2.t()])
                    self.tt("dve", oe, a1, a2, ALU.subtract, r=[t1.t(), t2.t()], w=[qkb.t(0)])
                    self.tt(PENG, a3, xe, sinb, ALU.mult, r=[xn.t(), ropeT.t()], w=[t3.t()])
                    self.tt(PENG, a4, xo, cosb, ALU.mult, r=[xn.t(), ropeT.t()], w=[t4.t()])
                    self.tt(PENG, oo, a3, a4, ALU.add, r=[t3.t(), t4.t()], w=[qkb.t(1)])
                else:
                    self.cp("dve", qkb[:], xn[:], r=[xn.t()], w=[qkb.t()])
                kd4 = kd[:].rearrange("p (g c d) -> p g c d", g=2, c=2)
                ksrc = v3(qkb[:, 512:640], 2).unsqueeze(2).broadcast_to([128, 2, 2, 64])
                self.cp("act", kd4, ksrc, r=[qkb.t()], w=[kd.t()])
                for c in range(4):
                    self.tr(pT[:, c, :], qkb[:, c * 128:(c + 1) * 128], self.ident_b[:], r=[qkb.t()], w=[pT.t()])
                for g in range(2):
                    self.tr(pT[:, 4 + g, :], kd[:, g * 128:(g + 1) * 128], self.ident_b[:], r=[kd.t()], w=[pT.t()])
                self.cp("act", sg[:, :, j * 128:(j + 1) * 128], pT[:], r=[pT.t()], w=[sg.t(j)])
            t0, n = ti0 * 128, tc * 128
            self.dma("sp", self.gqT.h.ap()[:, t0:t0 + n].rearrange("(c p) t -> p c t", p=128), sg[:, 0:4, 0:n],
                     r=[sg.t()], w=[self.gqT.t()])
            self.dma("sp", self.gkT2.h.ap()[:, :, t0:t0 + n].rearrange("g p t -> p g t"), sg[:, 4:6, 0:n],
                     r=[sg.t()], w=[self.gkT2.t()])

    def softmax_finish(self, po, N, rrow, pb, pbs, out_ap, out_w):
        self.S.add("dve", lambda e: e.reciprocal(out=rrow[64:65, 0:N], in_=po[64:65, 0:N]), r=[po.t()], w=[rrow.t()])
        self.mm(pb[0:64, 0:N], self.ones_f[64:65, 0:64], rrow[64:65, 0:N], True, True, r=[rrow.t()], w=[pb.t()])
        self.cp("act", pbs[0:64, 0:N], pb[0:64, 0:N], r=[pb.t()], w=[pbs.t()])
        self.tt("dve", out_ap, po[0:64, 0:N], pbs[0:64, 0:N], ALU.mult, r=[po.t(), pbs.t()], w=out_w)

    def nastage(self, l, need_ctx, lim=None):
        with self.stage():
            kT = self.sb("nkT", [128, 4, NT], BF16)
            qT = self.sb("nqT", [128, 4, NT], BF16)
            V = self.sb("nV", [128, NTILE, 8, 65], BF16)
            TB = self.sb("nTB", [128, 16, 512], BF16)
            self.dma("sp", kT[:], self.nakT.h.ap().rearrange("(c p) t -> p c t", p=128), r=[self.nakT.t()], w=[kT.t()])
            self.dma("sp", qT[:], self.naqT.h.ap().rearrange("(c p) t -> p c t", p=128), r=[self.naqT.t()], w=[qT.t()])
            self.memset("dve", V[:, :, :, 64:65], 1.0, w=[V.t("ones")])
            nv = self.nav.h.ap().rearrange("(i p) (h d) -> p i h d", p=128, h=8)
            for i0 in range(NTILE):
                self.dma("sp", V[:, i0, :, 0:64], nv[:, i0], r=[self.nav.t()], w=[V.t("d%d" % i0)])
            bst = [self.sb("nbst%d" % i, [128, 2, 512], F32) for i in range(2)]
            mst = [self.sb("nmst%d" % i, [128, 2, 512], F32) for i in range(2)]
            for ch in range(8):
                b, m = bst[ch % 2], mst[ch % 2]
                self.dma("sp", b[:], self.na_bias_g.ap()[l, 2 * ch:2 * ch + 2].rearrange("t p n -> p t n"), r=[], w=[b.t()])
                self.dma("sp", m[:], self.na_mask.ap()[2 * ch:2 * ch + 2].rearrange("t p n -> p t n"), r=[], w=[m.t()])
                self.act(b[:], b[:], AF.Exp, r=[b.t()], w=[b.t()])
                self.tt("dve", TB[:, 2 * ch:2 * ch + 2, :], b[:], m[:], ALU.mult, r=[b.t(), m.t()], w=[TB.t(ch)])
            TB5 = TB[:].rearrange("p t (c e q) -> p t c e q", c=4, e=2)
            pS = [self.ps("nps%d" % i, [128, 512], F32) for i in range(4)]
            pO = [self.ps("npo%d" % i, [128, 512], F32) for i in range(2)]
            pb = self.ps("npb", [128, 512], F32)
            P = [[[self.sb("nP%d_%d_%d" % (u, a, e), [128, 512], BF16) for e in range(2)] for a in range(4)]
                 for u in range(2)]
            rrow = self.sb("nrrow", [128, 512], F32)
            pbs = self.sb("npbs", [64, 512], F32)
            osb = [self.sb("nosb%d" % i, [64, 8, 256], BF16) for i in range(2)]
            units = []
            if need_ctx:
                for u in range(4):
                    units.append((u * 64, [(0, 0, None), (1, 128, None)]))
            for r_ in range(64):
                rs = min(max(r_ - 4, 0), 56)
                kts = [(0, 0, None), (1, 128, None)]
                if rs % 2 == 0:
                    for j in range(4):
                        a = rs + 2 * j
                        kts.append((2 + a // 2, 256 + a * 64, a - r_ + 7))
                else:
                    for j in range(5):
                        a = rs - 1 + 2 * j
                        dr = a - r_ + 7
                        tb = 14 if j == 0 else (15 if j == 4 else dr)
                        kts.append((2 + a // 2, 256 + a * 64, tb))
                units.append((256 + r_ * 64, kts))
            psn = 0
            mtog = 0
            npend = []
            if lim:
                units = units[:lim]
            def phaseA(ui, hook=None):
                nonlocal psn, mtog
                tq0, kts = units[ui]
                ub = ui % 2
                nk = len(kts)
                npair = (nk + 1) // 2
                for a in range(npair):
                    pair = kts[2 * a:2 * a + 2]
                    banks = [pS[psn % 4], pS[(psn + 1) % 4]]
                    psn += 2
                    for s_, (vt, kc0, tb) in enumerate(pair):
                        for h in range(8):
                            e, c = h % 2, h // 2
                            self.mm(banks[e][:, s_ * 256 + c * 64: s_ * 256 + c * 64 + 64],
                                    kT[e * 64:(e + 1) * 64, c, kc0:kc0 + 128], qT[e * 64:(e + 1) * 64, c, tq0:tq0 + 64],
                                    True, True, r=[kT.t(), qT.t()], w=[banks[e].t()])
                    ncol = 256 * len(pair)
                    for e in range(2):
                        pt = P[ub][a][e]
                        self.act(pt[:, 0:ncol], banks[e][:, 0:ncol], AF.Exp, r=[banks[e].t()], w=[pt.t()])
                        for s_, (vt, kc0, tb) in enumerate(pair):
                            if tb is None:
                                continue
                            mtog += 1
                            sl = pt[:, s_ * 256:(s_ + 1) * 256].rearrange("p (c q) -> p c q", c=4)
                            self.tt("dve" if (mtog % 2 or NA_NOPOOL) else "pool", sl, sl, TB5[:, tb, :, e, :], ALU.mult,
                                    r=[pt.t(), TB.t()], w=[pt.t()])
                    if hook:
                        hook(a, npair)

            def pv_heads(ui, heads):
                tq0, kts = units[ui]
                ub = ui % 2
                nk = len(kts)
                po = pO[ui % 2]
                for h in heads:
                    e, c = h % 2, h // 2
                    for ki, (vt, kc0, tb) in enumerate(kts):
                        pt = P[ub][ki // 2][e]
                        s_ = ki % 2
                        self.mm(po[0:65, h * 64:(h + 1) * 64], V[:, vt, h, :],
                                pt[:, s_ * 256 + c * 64: s_ * 256 + c * 64 + 64], ki == 0, ki == nk - 1,
                                r=[V.t(), pt.t()], w=[po.t()])

            def finish(ui):
                tq0, kts = units[ui]
                po = pO[ui % 2]
                ob = osb[(ui // 4) % 2]
                j = ui % 4
                self.act(rrow[64:65, 0:512], po[64:65, 0:512], AF.Ln, r=[po.t()], w=[rrow.t()])
                self.act(rrow[64:65, 0:512], rrow[64:65, 0:512], AF.Exp, scale=-1.0, r=[rrow.t()], w=[rrow.t()])
                self.mm(pb[0:64, 0:512], self.ones_f[64:65, 0:64], rrow[64:65, 0:512], True, True, r=[rrow.t()], w=[pb.t()])
                self.cp("act", pbs[0:64, 0:512], pb[0:64, 0:512], r=[pb.t()], w=[pbs.t()])
                self.tt("dve", ob[:, :, j * 64:(j + 1) * 64], po[0:64, 0:512], pbs[0:64, 0:512], ALU.mult,
                        r=[po.t(), pbs.t()], w=[ob.t(j)])
                if j == 3:
                    t0 = tq0 - 192
                    self.dma("sp", self.onaT.h.ap()[:, :, t0:t0 + 256].rearrange("h d t -> d h t"), ob[:],
                             r=[ob.t()], w=[self.onaT.t()])

            phaseA(0)
            nu = len(units)
            for ui in range(nu):
                if ui + 1 < nu:
                    def hook(a, npair, ui=ui):
                        lo, hi = (8 * a) // npair, (8 * (a + 1)) // npair
                        pv_heads(ui, range(lo, hi))
                        if a == 0 and ui > 0:
                            finish(ui - 1)
                    phaseA(ui + 1, hook)
                else:
                    pv_heads(ui, range(8))
                    if ui > 0:
                        finish(ui - 1)
            finish(nu - 1)

    def gqastage(self, l, need_ctx, lim=None):
        with self.stage():
            kT2 = self.sb("gkT", [128, 2, NT], BF16)
            qT = self.sb("gqTz", [128, 8, NT], BF16)
            V = self.sb("gV", [128, NTILE, 2, 65], BF16)
            self.dma("sp", kT2[:], self.gkT2.h.ap().rearrange("g p t -> p g t"), r=[self.gkT2.t()], w=[kT2.t()])
            for h in range(8):
                e_, c_ = h % 2, h // 2
                self.memset("dve", qT[(1 - e_) * 64:(2 - e_) * 64, h, :], 0.0, w=[qT.t((h, 0))])
                self.dma("sp", qT[e_ * 64:(e_ + 1) * 64, h, :], self.gqT.h.ap()[c_ * 128 + e_ * 64:c_ * 128 + (e_ + 1) * 64, :],
                         r=[self.gqT.t()], w=[qT.t((h, 1))])
            self.memset("dve", V[:, :, :, 64:65], 1.0, w=[V.t("ones")])
            for g in range(2):
                self.dma("sp", V[:, :, g, 0:64], self.gv.h.ap()[:, g * 64:(g + 1) * 64].rearrange("(i p) d -> p i d", p=128),
                         r=[self.gv.t()], w=[V.t("d%d" % g)])
            pS = [self.ps("gps%d" % i, [128, 1024], F32) for i in range(2)]
            pO = [self.ps("gpo%d" % i, [128, 512], F32) for i in range(2)]
            pb = self.ps("gpb", [128, 512], F32)
            Pb = [self.sb("gP%d" % i, [128, 1024], BF16) for i in range(3)]
            rrow = self.sb("grrow", [128, 512], F32)
            pbs = self.sb("gpbs", [64, 512], F32)
            osb = [self.sb("gosb%d" % i, [64, 512], BF16) for i in range(2)]
            blocks = []
            if need_ctx:
                blocks.append((0, 256, [0, 1]))
            for qb in range(8):
                blocks.append((256 + qb * 512, 512, list(range(NTILE))))
            n = 0
            ui = 0
            pend = []
            if lim:
                blocks = blocks[:lim]
            for h in range(8 if not lim else 2):
                g, e, c = h // 4, h % 2, h // 2
                for (tq0, N, kts) in blocks:
                    po = pO[ui % 2]
                    ob = osb[ui % 2]
                    ui += 1
                    nk = len(kts)
                    npair = nk // 2
                    LA = 1
                    bufs = {}

                    def qk(i):
                        nonlocal n
                        ps, pt = pS[n % 2], Pb[n % 3]
                        n += 1
                        for a_ in range(2):
                            kt = kts[2 * i + a_]
                            self.mm(ps[:, a_ * 512:a_ * 512 + N], kT2[:, g, kt * 128:(kt + 1) * 128],
                                    qT[:, h, tq0:tq0 + N], True, True, r=[kT2.t(), qT.t()], w=[ps.t(a_)])
                        v2 = lambda ap: ap.rearrange("p (a n) -> p a n", a=2)[:, :, 0:N]
                        self.act(v2(pt[:]), v2(ps[:]), AF.Exp, r=[ps.t()], w=[pt.t()])
                        bufs[i] = pt

                    for i in range(min(LA, npair)):
                        qk(i)
                    for i in range(npair):
                        if i + LA < npair:
                            qk(i + LA)
                        if i == 3 and pend:
                            pend.pop()()
                        pt = bufs.pop(i)
                        for a_ in range(2):
                            self.mm(po[0:65, 0:N], V[:, kts[2 * i + a_], g, :], pt[:, a_ * 512:a_ * 512 + N],
                                    i == 0 and a_ == 0, i == npair - 1 and a_ == 1, r=[V.t(), pt.t()], w=[po.t()])
                    if pend:
                        pend.pop()()

                    def fin(po=po, N=N, ob=ob, h=h, tq0=tq0):
                        self.softmax_finish(po, N, rrow, pb, pbs, ob[:, 0:N], [ob.t()])
                        self.dma("sp", self.ogT.h.ap()[h, :, tq0:tq0 + N], ob[:, 0:N], r=[ob.t()], w=[self.ogT.t()])
                    pend.append(fin)
            if pend:
                pend.pop()()

    def ssdstage(self, l, need_ctx):
        with contextlib.ExitStack() as outer:
            sbo = lambda name, shape, dt: self.sb(name, shape, dt, es=outer)
            xT = sbo("sxT", [128, NTILE, 1024], BF16)
            BT = sbo("sBT", [128, NT], BF16)
            CT0 = sbo("sCT0", [128, NT], BF16)
            CT1 = sbo("sCT1", [128, NT], BF16)
            Btm = sbo("sBtm", [128, NTILE, 128], BF16)
            dt = sbo("sdt", [128, NTILE, 32], F32)
            da = sbo("sda", [128, NTILE, 32], F32)
            with self.stage():
                cw = self.sb("scw", [128, 10, 3], F32)
                cbias = self.sb("scb", [128, 10], F32)
                for k in range(3):
                    self.dma("sp", cw[:, :, k], self.ssm_conv_w.ap()[l, k].rearrange("(c p) -> p c", p=128),
                             r=[], w=[cw.t(k)])
                self.dma("sp", cbias[:], self.ssm_conv_b.ap()[l].rearrange("(c p) -> p c", p=128), r=[], w=[cbias.t()])
                raws = [self.sb("sraw%d" % i, [128, NT], BF16) for i in range(2)]
                acc = self.sb("sacc", [128, NT], F32)
                xF = self.sb("sxF", [128, NT], BF16)
                pT = [self.ps("spT%d" % i, [128, 8, 128], BF16) for i in range(2)]
                xv = self.xbcT.h.ap().rearrange("(c p) t -> p c t", p=128)
                ntr = 0
                for c in range(10):
                    raw = raws[c % 2]
                    self.dma("sp", raw[:], xv[:, c, :], r=[self.xbcT.t()], w=[raw.t()])
                    self.ts("dve", acc[:], raw[:], cw[:, c, 1:2], cbias[:, c:c + 1], ALU.mult, ALU.add,
                            r=[raw.t(), cw.t(), cbias.t()], w=[acc.t()])
                    for (a, b) in ((0, LC), (LC, NT)):
                        self.stt(acc[:, a + 1:b], raw[:, a:b - 1], cw[:, c, 0:1], acc[:, a + 1:b], ALU.mult, ALU.add,
                                 r=[raw.t(), acc.t()], w=[acc.t()])
                        self.stt(acc[:, a:b - 1], raw[:, a + 1:b], cw[:, c, 2:3], acc[:, a:b - 1], ALU.mult, ALU.add,
                                 r=[raw.t(), acc.t()], w=[acc.t()])
                    if c < 9:
                        dstF = xF if c < 8 else BT
                        self.act(dstF[:], acc[:], AF.Silu, r=[acc.t()], w=[dstF.t()])
                        for (ti0, tc) in [(0, 2)] + [(2 + 4 * g, 4) for g in range(8)]:
                            p = pT[ntr % 2]
                            ntr += 1
                            for j in range(tc):
                                ti = ti0 + j
                                self.tr(p[:, j, :], dstF[:, ti * 128:(ti + 1) * 128], self.ident_b[:], r=[dstF.t()], w=[p.t()])
                            if c < 8:
                                self.cp("act" if ntr % 2 else "dve", xT[:, ti0:ti0 + tc, c * 128:(c + 1) * 128], p[:, 0:tc, :],
                                        r=[p.t()], w=[xT.t((c, ti0))])
                            else:
                                self.cp("dve", Btm[:, ti0:ti0 + tc, :], p[:, 0:tc, :], r=[p.t()], w=[Btm.t(ti0)])
                    else:
                        self.act(CT0[:], acc[:], AF.Silu, r=[acc.t()], w=[CT0.t()])
                        self.cp("dve", CT1[:], CT0[:], r=[CT0.t()], w=[CT1.t()])
                        self.memset("dve", CT0[64:128, :], 0.0, w=[CT0.t()])
                        self.memset("dve", CT1[0:64, :], 0.0, w=[CT1.t()])
                dtb = self.sb("sdtb", [128, 32], F32)
                ab = self.sb("sab", [128, 32], F32)
                self.dma("sp", dtb[:], self.ssm_dt_bias.ap()[l:l + 1].rearrange("o a b -> o (a b)").broadcast_to([128, 32]),
                         r=[], w=[dtb.t()])
                self.dma("sp", ab[:], self.ssm_a_log.ap()[l:l + 1].rearrange("o a b -> o (a b)").broadcast_to([128, 32]),
                         r=[], w=[ab.t()])
                self.act(ab[:], ab[:], AF.Exp, r=[ab.t()], w=[ab.t()])
                self.ts("dve", ab[:], ab[:], -1.0, None, ALU.mult, None, r=[ab.t()], w=[ab.t()])
                self.dma("sp", dt[:], self.dtr.h.ap().rearrange("(i p) n -> p i n", p=128), r=[self.dtr.t()], w=[dt.t()])
                self.tt("dve", dt[:], dt[:], dtb[:].unsqueeze(1).broadcast_to([128, NTILE, 32]), ALU.add,
                        r=[dt.t(), dtb.t()], w=[dt.t()])
                self.act(dt[:], dt[:], AF.Exp, r=[dt.t()], w=[dt.t()])
                self.act(dt[:], dt[:], AF.Ln, bias=self.ones_f[:, 0:1], r=[dt.t()], w=[dt.t()])
                self.tt("dve", da[:], dt[:], ab[:].unsqueeze(1).broadcast_to([128, NTILE, 32]), ALU.mult,
                        r=[dt.t(), ab.t()], w=[da.t()])
            with self.stage():
                U = self.sb("sU", [128, 128], F32)
                Lo = self.sb("sLo", [128, 128], F32)
                Ls = self.sb("sLs", [128, 128], F32)
                Us = self.sb("sUs", [128, 128], F32)
                for (mtx, op, sg) in ((U, ALU.is_ge, -1), (Lo, ALU.is_ge, 1), (Ls, ALU.is_gt, 1), (Us, ALU.is_gt, -1)):
                    self.memset("pool", mtx[:], 1.0, w=[mtx.t()])
                    self.S.add("pool", lambda e, mtx=mtx, op=op, sg=sg: e.affine_select(
                        out=mtx[:], in_=mtx[:], pattern=[[-sg, 128]], compare_op=op, fill=0.0, base=0,
                        channel_multiplier=sg), r=[mtx.t()], w=[mtx.t()])
                dsk = self.sb("sdsk", [128, 16], F32)
                nwT = self.sb("snwT", [128, 8], F32)
                self.dma("sp", dsk[:], self.ssm_d.ap()[l:l + 1, :].broadcast_to([128, 16]), r=[], w=[dsk.t()])
                self.dma("sp", nwT[:], self.ssm_norm_w.ap()[l].rearrange("(k p) -> p k", p=128), r=[], w=[nwT.t()])
                LA = self.sb("sLA", [128, 16, 128], F32)
                expD = self.sb("sexpD", [128, 16, 128], BF16)
                mcb = self.sb("smcb", [128, 2, 128], BF16)
                Mt = [self.sb("sM%d" % i, [128, 16, 128], BF16) for i in range(2)]
                xs = [self.sb("sxs%d" % i, [128, 16, 64], BF16) for i in range(2)]
                xsw = [self.sb("sxsw%d" % i, [128, 16, 64], BF16) for i in range(2)]
                E = self.sb("sE", [128, 16], F32)
                dec = self.sb("sdec", [128, 8], F32)
                H = self.sb("sH", [128, 512], F32)
                Hb = self.sb("sHb", [128, 512], BF16)
                tmpH = self.sb("stmpH", [128, 512], F32)
                yt = self.sb("syt", [128, 1024], F32)
                ys = [self.sb("sy%d" % i, [128, 1024], F32) for i in range(2)]
                yfb = self.sb("syfb", [128, 1024], F32)
                zt = self.sb("szt", [128, 1024], BF16)
                gg = self.sb("sgg", [128, 1024], F32)
                gjunk = self.sb("sgjunk", [128, 1024], BF16)
                ssq = self.sb("sssq", [128, 1], F32)
                ob = self.sb("sob", [128, 1024], BF16)
                ost = self.sb("sost", [128, 8, 512], BF16)
                pD = self.ps("spD", [128, 2048], F32)
                pm = self.ps("spm", [128, 512], F32)
                pm2 = self.ps("spm2", [128, 512], F32)
                pS = self.ps("spS", [128, 512], F32)
                pTo = self.ps("spTo", [128, 8, 128], BF16)
                oT = self.ossmT.h.ap().rearrange("(k p) t -> p k t", p=128)
                decs = [dec, self.sb("sdec2", [128, 8], F32)]
                steps = []
                for d in range(2):
                    order = [0, 1] + list(range(2, NTILE)) if d == 0 else [1, 0] + list(range(NTILE - 1, 1, -1))
                    for oi, ti in enumerate(order):
                        steps.append((d, ti, oi == 0))

                def par(i):
                    d, ti, first = steps[i]
                    A_, Bm_, mask_, wcol = (Ls, U, U, 127) if d == 0 else (Us, Lo, Lo, 0)
                    return dict(d=d, ti=ti, first=first, A_=A_, Bm_=Bm_, mask_=mask_, wcol=wcol,
                                do_y=(need_ctx or ti >= 2), cols=slice(ti * 128, (ti + 1) * 128),
                                dav=da[:, ti, d * 16:(d + 1) * 16], dtv=dt[:, ti, d * 16:(d + 1) * 16],
                                M=Mt[i % 2], xs_=xs[i % 2], xsw_=xsw[i % 2], y=ys[i % 2], dec=decs[i % 2],
                                x3=xT[:, ti, :].rearrange("p (h q) -> p h q", h=16))

                def stepA(i):
                    p = par(i)
                    A_, Bm_, dav, dtv, xs_, xsw_, dec_ = p["A_"], p["Bm_"], p["dav"], p["dtv"], p["xs_"], p["xsw_"], p["dec"]
                    for h in range(16):
                        self.act(LA[:, h, :], A_[:], AF.Identity, scale=dav[:, h:h + 1], bias=self.zero_t[:, 0:1],
                                 r=[A_.t(), da.t()], w=[LA.t(h)])
                    for h in range(16):
                        self.mm(pD[:, h * 128:(h + 1) * 128], LA[:, h, :], Bm_[:], True, True,
                                r=[LA.t(h), Bm_.t()], w=[pD.t(h // 4)])
                    self.mm(pm2[:, 0:16], Bm_[:], dav, True, True, r=[Bm_.t(), da.t()], w=[pm2.t()])
                    self.mm(pm2[:, 16:32], self.ones_f[:], dav, True, True, r=[da.t()], w=[pm2.t()])
                    for q in range(4):
                        self.act(expD[:, 4 * q:4 * q + 4, :], pD[:, q * 512:(q + 1) * 512].rearrange("p (h t) -> p h t", h=4),
                                 AF.Exp, r=[pD.t(q)], w=[expD.t(q)])
                    self.act(E[:], pm2[:, 0:16], AF.Exp, r=[pm2.t()], w=[E.t()])
                    self.act(dec_[0:64, :], pm2[0:64, 16:24], AF.Exp, r=[pm2.t()], w=[dec_.t(0)])
                    self.act(dec_[64:128, :], pm2[64:128, 24:32], AF.Exp, r=[pm2.t()], w=[dec_.t(1)])

                def stepB(i):
                    p = par(i)
                    if p["first"]:
                        self.memset("dve", H[:], 0.0, w=[H.t()])
                        self.memset("dve", Hb[:], 0.0, w=[Hb.t()])
                    self.tt(PENG, p["xs_"][:], p["x3"], p["dtv"].unsqueeze(2).broadcast_to([128, 16, 64]), ALU.mult,
                            r=[xT.t(), dt.t()], w=[p["xs_"].t()])
                    if p["do_y"]:
                        stepB_y(p)
                    self.tt(PENG, p["xsw_"][:], p["xs_"][:], expD[:, :, p["wcol"]].unsqueeze(2).broadcast_to([128, 16, 64]),
                            ALU.mult, r=[p["xs_"].t(), expD.t()], w=[p["xsw_"].t()])

                def stepB_y(p):
                    cols, mask_, M, xs_, y = p["cols"], p["mask_"], p["M"], p["xs_"], p["y"]
                    for g, CTg in enumerate((CT0, CT1)):
                        self.mm(pm[:, g * 128:(g + 1) * 128], BT[:, cols], CTg[:, cols], True, True,
                                r=[BT.t(), CTg.t()], w=[pm.t()])
                    self.tt("dve", mcb[:], pm[:, 0:256].rearrange("p (g t) -> p g t", g=2),
                            mask_[:].unsqueeze(1).broadcast_to([128, 2, 128]), ALU.mult,
                            r=[pm.t(), mask_.t()], w=[mcb.t()])
                    for g in range(2):
                        self.tt("dve", M[:, 8 * g:8 * g + 8, :], expD[:, 8 * g:8 * g + 8, :],
                                mcb[:, g:g + 1, :].broadcast_to([128, 8, 128]), ALU.mult,
                                r=[expD.t(), mcb.t()], w=[M.t(g)])
                    for h in range(16):
                        self.mm(pD[:, h * 64:(h + 1) * 64], M[:, h, :], xs_[:, h, :], True, True,
                                r=[M.t(), xs_.t()], w=[pD.t(h // 8)])
                    for g, CTg in enumerate((CT0, CT1)):
                        self.mm(pD[:, 1024 + g * 512:1024 + (g + 1) * 512], CTg[:, cols], Hb[:], True, True,
                                r=[CTg.t(), Hb.t()], w=[pD.t(2 + g)])
                    for g in range(2):
                        self.tt("dve", yt[:, g * 512:(g + 1) * 512].rearrange("p (h q) -> p h q", h=8),
                                pD[:, 1024 + g * 512:1024 + (g + 1) * 512].rearrange("p (h q) -> p h q", h=8),
                                E[:, 8 * g:8 * g + 8].unsqueeze(2).broadcast_to([128, 8, 64]), ALU.mult,
                                r=[pD.t(2 + g), E.t()], w=[yt.t(g)])
                    self.tt("dve", y[:], yt[:], pD[:, 0:1024], ALU.add, r=[yt.t(), pD.t(0), pD.t(1)], w=[y.t()])

                def stepC(i):
                    p = par(i)
                    ti, xsw_, dec_ = p["ti"], p["xsw_"], p["dec"]
                    for g in range(2):
                        self.mm(pS[g * 64:(g + 1) * 64, :], Btm[:, ti, g * 64:(g + 1) * 64],
                                xsw_[:, 8 * g:8 * g + 8, :].rearrange("p h q -> p (h q)"), True, True,
                                r=[Btm.t(), xsw_.t()], w=[pS.t(g)])
                    self.tt("dve", tmpH[:].rearrange("p (h q) -> p h q", h=8), H[:].rearrange("p (h q) -> p h q", h=8),
                            dec_[:].unsqueeze(2).broadcast_to([128, 8, 64]), ALU.mult, r=[H.t(), dec_.t()], w=[tmpH.t()])
                    self.tt("dve", H[:], tmpH[:], pS[:], ALU.add, r=[tmpH.t(), pS.t()], w=[H.t()])
                    self.cp("act", Hb[:], H[:], r=[H.t()], w=[Hb.t()])

                def stepD(i):
                    p = par(i)
                    if not p["do_y"]:
                        return
                    d, ti, y, x3 = p["d"], p["ti"], p["y"], p["x3"]
                    rows = self.yf.h.ap()[ti * 128:(ti + 1) * 128, :]
                    if d == 0:
                        self.dma("sp", rows, y[:], r=[y.t()], w=[self.yf.t(ti)])
                        return
                    self.dma("sp", yfb[:], rows, r=[self.yf.t(ti)], w=[yfb.t()])
                    self.dma("sp", zt[:], self.zs.h.ap()[ti * 128:(ti + 1) * 128, :], r=[self.zs.t()], w=[zt.t()])
                    self.tt("dve", y[:], y[:], yfb[:], ALU.add, r=[y.t(), yfb.t()], w=[y.t()])
                    self.tt(PENG, gg[:].rearrange("p (h q) -> p h q", h=16), x3,
                            dsk[:].unsqueeze(2).broadcast_to([128, 16, 64]), ALU.mult, r=[xT.t(), dsk.t()], w=[gg.t()])
                    self.tt("dve", y[:], y[:], gg[:], ALU.add, r=[y.t(), gg.t()], w=[y.t()])
                    self.tt("dve", gg[:], y[:], zt[:], ALU.mult, r=[y.t(), zt.t()], w=[gg.t()])
                    self.act(gjunk[:], gg[:], AF.Square, r=[gg.t()], w=[gjunk.t(), ssq.t()], accum_out=ssq[:])
                    self.act(ssq[:], ssq[:], AF.Sqrt, bias=self.eps_t[:, 0:1], scale=1.0 / 1024, r=[ssq.t()], w=[ssq.t()])
                    self.S.add("dve", lambda e: e.reciprocal(out=ssq[:], in_=ssq[:]), r=[ssq.t()], w=[ssq.t()])
                    self.ts("dve", ob[:], gg[:], ssq[:, 0:1], None, ALU.mult, None, r=[gg.t(), ssq.t()], w=[ob.t()])
                    for k in range(8):
                        self.tr(pTo[:, k, :], ob[:, k * 128:(k + 1) * 128], self.ident_b[:], r=[ob.t()], w=[pTo.t()])
                    j = ti if ti < 2 else (ti - 2) % 4
                    self.tt("dve", ost[:, :, j * 128:(j + 1) * 128], pTo[:], nwT[:].unsqueeze(2).broadcast_to([128, 8, 128]),
                            ALU.mult, r=[pTo.t(), nwT.t()], w=[ost.t(j)])
                    if j == 0:
                        t0, n = (0, 256) if ti < 2 else (ti * 128, 512)
                        self.dma("sp", oT[:, :, t0:t0 + n], ost[:, :, 0:n], r=[ost.t()], w=[self.ossmT.t()])

                stepA(0)
                for i in range(len(steps)):
                    stepB(i)
                    if i + 1 < len(steps):
                        stepA(i + 1)
                    stepC(i)
                    stepD(i)

    def mergestage(self, l, need_ctx):
        with self.stage():
            wna = self.sb("mwna", [64, 8, 1024], BF16)
            wgq = self.sb("mwgq", [64, 8, 1024], BF16)
            wss = self.sb("mwss", [128, 8, 1024], BF16)
            wo = self.sb("mwo", [128, 8, 1024], BF16)
            self.dma("pool", wna[:], self.w_out_na.ap()[l].rearrange("(h d) n -> d h n", d=64), r=[], w=[wna.t()])
            self.dma("pool", wgq[:], self.w_out_gqa.ap()[l].rearrange("(h d) n -> d h n", d=64), r=[], w=[wgq.t()])
            self.dma("pool", wss[:], self.w_out_ssm.ap()[l].rearrange("(k p) n -> p k n", p=128), r=[], w=[wss.t()])
            self.dma("pool", wo[:], self.w_o.ap()[l].rearrange("(k p) n -> p k n", p=128), r=[], w=[wo.t()])
            NB = 512
            ona = [self.sb("mona%d" % i, [64, 8, NB], BF16) for i in range(2)]
            ogq = [self.sb("mogq%d" % i, [64, 8, NB], BF16) for i in range(2)]
            oss = [self.sb("moss%d" % i, [128, 8, NB], BF16) for i in range(2)]
            gts = [self.sb("mgt%d" % i, [128, 24, NB], BF16) for i in range(2)]
            hbs = [self.sb("mhb%d" % i, [128, 8, NB], F32) for i in range(1)] * 2
            yTs = [self.sb("myT%d" % i, [128, 8, NB], BF16) for i in range(1)] * 2
            t1 = [self.sb("mt1_%d" % i, [128, NB], F32) for i in range(1)] * 2
            t2 = [self.sb("mt2_%d" % i, [128, NB], F32) for i in range(1)] * 2
            t3 = [self.sb("mt3_%d" % i, [128, NB], F32) for i in range(1)] * 2
            pA = [self.ps("mpA%d" % i, [128, 512], F32) for i in range(2)]
            pB = [self.ps("mpB%d" % i, [128, 512], F32) for i in range(2)]
            pC = [self.ps("mpC%d" % i, [128, 512], F32) for i in range(2)]
            pM = [self.ps("mpM%d" % i, [128, 512], F32) for i in range(2)]
            hv = self.hcur.h.ap().rearrange("(k p) t -> p k t", p=128)
            blocks = ([(0, LC, 1)] if need_ctx else []) + [(LC + NB * i, NB, 0) for i in range(SEQ // NB)]
            mv = self.modv[l]
            n = 0
            for bi, (t0, N, j) in enumerate(blocks):
                a, g_, s_, gt, hb, yT = ona[bi % 2], ogq[bi % 2], oss[bi % 2], gts[bi % 2], hbs[bi % 2], yTs[bi % 2]
                a, g_, s_, gt, hb, yT = (Sl(a, N), Sl(g_, N), Sl(s_, N), Sl(gt, N), Sl(hb, N), Sl(yT, N))
                self.dma("sp", a[:], self.onaT.h.ap()[:, :, t0:t0 + N].rearrange("h d t -> d h t"), r=[self.onaT.t()], w=[a.t()])
                self.dma("sp", g_[:], self.ogT.h.ap()[:, :, t0:t0 + N].rearrange("h d t -> d h t"), r=[self.ogT.t()], w=[g_.t()])
                self.dma("sp", s_[:], self.ossmT.h.ap()[:, t0:t0 + N].rearrange("(k p) t -> p k t", p=128),
                         r=[self.ossmT.t()], w=[s_.t()])
                self.dma("sp", gt[:], self.gatesT.h.ap()[:, t0:t0 + N].rearrange("(c p) t -> p c t", p=128),
                         r=[self.gatesT.t()], w=[gt.t()])
                self.dma("sp", hb[:], hv[:, :, t0:t0 + N], r=[self.hcur.t()], w=[hb.t()])
                for oc in range(8):
                    n += 1
                    A_, B_, C_ = pA[n % 2], pB[n % 2], pC[n % 2]
                    u1, u2, u3 = Sl(t1[n % 2], N), Sl(t2[n % 2], N), Sl(t3[n % 2], N)
                    cs = slice(oc * 128, (oc + 1) * 128)
                    for h in range(8):
                        self.mm(A_[:, 0:N], wna[:, h, cs], a[:, h, :], h == 0, h == 7, r=[wna.t(), a.t()], w=[A_.t()])
                    for k in range(8):
                        self.mm(B_[:, 0:N], wss[:, k, cs], s_[:, k, :], k == 0, k == 7, r=[wss.t(), s_.t()], w=[B_.t()])
                    for h in range(8):
                        self.mm(C_[:, 0:N], wgq[:, h, cs], g_[:, h, :], h == 0, h == 7, r=[wgq.t(), g_.t()], w=[C_.t()])
                    self.tt("dve", u1[:], A_[:, 0:N], gt[:, oc, :], ALU.mult, r=[A_.t(), gt.t()], w=[u1.t()])
                    self.tt("dve", u2[:], B_[:, 0:N], gt[:, 8 + oc, :], ALU.mult, r=[B_.t(), gt.t()], w=[u2.t()])
                    self.tt("dve", u3[:], C_[:, 0:N], gt[:, 16 + oc, :], ALU.mult, r=[C_.t(), gt.t()], w=[u3.t()])
                    self.tt("dve", u1[:], u1[:], u2[:], ALU.add, r=[u1.t(), u2.t()], w=[u1.t()])
                    self.tt("dve", yT[:, oc, :], u1[:], u3[:], ALU.add, r=[u1.t(), u3.t()], w=[yT.t(oc)])
                for oc in range(8):
                    n += 1
                    M_ = pM[n % 2]
                    cs = slice(oc * 128, (oc + 1) * 128)
                    for k in range(8):
                        self.mm(M_[:, 0:N], wo[:, k, cs], yT[:, k, :], k == 0, k == 7, r=[wo.t(), yT.t()], w=[M_.t()])
                    self.stt(hb[:, oc, :], M_[:, 0:N], mv[:, 16 + oc, j:j + 1], hb[:, oc, :], ALU.mult, ALU.add,
                             r=[M_.t(), hb.t(oc)], w=[hb.t(oc)])
                self.dma("sp", hv[:, :, t0:t0 + N], hb[:], r=[hb.t()], w=[self.hcur.t()])

    def ffnstage(self, l, need_ctx):
        src, dst = self.hcur, (self.hT2 if self.hcur is self.hT else self.hT)
        with self.stage():
            wup = self.sb("fwup", [128, 8, 2 * FH], BF16)
            wdn = self.sb("fwdn", [128, 22, 1024], BF16)
            uv = self.ffn_w_up.ap()[l].rearrange("(k p) n -> p k n", p=128)
            for c in (0, 5, 6, 1, 7, 2, 8, 3, 9, 4, 10):
                self.dma("pool", wup[:, :, c * 512:(c + 1) * 512], uv[:, :, c * 512:(c + 1) * 512], r=[], w=[wup.t(c)])
            dv = self.ffn_w_down.ap()[l].rearrange("(c p) n -> p c n", p=128)
            for c in range(2):
                self.dma("pool", wdn[:, c * 11:(c + 1) * 11, :], dv[:, c * 11:(c + 1) * 11, :], r=[], w=[wdn.t(c)])
            fcw = self.sb("ffcw", [128, 44, 3], F32)
            fcb = self.sb("ffcb", [128, 44], F32)
            for k in range(3):
                self.dma("sp", fcw[:, :, k], self.ffn_conv_w.ap()[l, k].rearrange("(c p) -> p c", p=128), r=[], w=[fcw.t(k)])
            self.dma("sp", fcb[:], self.ffn_conv_b.ap()[l].rearrange("(c p) -> p c", p=128), r=[], w=[fcb.t()])
            NB = 384
            NH = NB + 2
            hbs = [self.sb("fhb%d" % i, [128, 8, NH], F32) for i in range(2)]
            tmp = [self.sb("ftmp%d" % i, [128, NH], F32) for i in range(2)]
            rs = self.sb("frs", [128, NH], F32)
            xns = [self.sb("fxn%d" % i, [128, 8, NH], BF16) for i in range(2)]
            ua = [self.sb("fua%d" % i, [128, NB], F32) for i in range(2)]
            ub = [self.sb("fub%d" % i, [128, NB], F32) for i in range(2)]
            sa = [self.sb("fsa%d" % i, [128, NB], BF16) for i in range(2)]
            t0s = [self.sb("ft0_%d" % i, [128, NB], F32) for i in range(4)]
            actb = self.sb("factb", [128, 22, NB], BF16)

            class SqView:
                name = actb.name

                def t(self_, slot=None):
                    return actb.t()

                def __getitem__(self_, k):
                    v = actb[:].rearrange("p c n -> p (c n)")[:, 0:8 * NH].rearrange("p (k n) -> p k n", k=8)
                    return v[k]
            sq = SqView()
            ssp = self.ps("fssp", [128, 512], F32)
            pa = [self.ps("fpa%d" % i, [128, 512], F32) for i in range(2)]
            pb_ = [self.ps("fpb%d" % i, [128, 512], F32) for i in range(2)]
            pd = [self.ps("fpd%d" % i, [128, 512], F32) for i in range(2)]
            sv = src.h.ap().rearrange("(k p) t -> p k t", p=128)
            dvw = dst.h.ap().rearrange("(k p) t -> p k t", p=128)
            mv = self.modv[l]
            blocks = [(0, LC, 1, 0, LC)] if need_ctx else []
            t_ = LC
            while t_ < NT:
                nb_ = min(NB, NT - t_)
                blocks.append((t_, nb_, 0, LC, NT))
                t_ += nb_
            n = 0

            def prep(bi):
                t0, nb, j, s0, s1 = blocks[bi]
                nh = nb + 2
                hb, xn = hbs[bi % 2], xns[bi % 2]
                lo, hi = max(t0 - 1, s0), min(t0 + nb + 1, s1)
                a0 = lo - (t0 - 1)
                if lo > t0 - 1:
                    self.memset("dve", hb[:, :, 0:1], 0.0, w=[hb.t()])
                if hi < t0 + nb + 1:
                    self.memset("dve", hb[:, :, nh - 1:nh], 0.0, w=[hb.t()])
                self.dma("sp", hb[:, :, a0:a0 + hi - lo], sv[:, :, lo:hi], r=[src.t()], w=[hb.t()])
                self.norm_block(hb, nh, self.A2[l], mv, 24, j, lambda k, xn=xn, nh=nh: xn[:, k, 0:nh], [xn.t()], xn, ssp, rs, tmp)
                if lo > t0 - 1:
                    self.memset("dve", xn[:, :, 0:1], 0.0, w=[xn.t()])
                if hi < t0 + nb + 1:
                    self.memset("dve", xn[:, :, nh - 1:nh], 0.0, w=[xn.t()])

            prep(0)
            for bi, (t0, nb, j, s0, s1) in enumerate(blocks):
                nh = nb + 2
                hb, xn = hbs[bi % 2], xns[bi % 2]
                for c in range(22):
                    n += 1
                    A_, B_ = pa[n % 2], pb_[n % 2]
                    va, vb, vs = ua[n % 2], ub[n % 2], sa[n % 2]
                    for (P_, cc, v) in ((A_, c, va), (B_, 22 + c, vb)):
                        for k in range(8):
                            self.mm(P_[:, 0:nh], wup[:, k, cc * 128:(cc + 1) * 128], xn[:, k, 0:nh], k == 0, k == 7,
                                    r=[wup.t(cc // 4), xn.t()], w=[P_.t()])
                        self.act(v[:, 0:nb], P_[:, 1:nb + 1], AF.Identity, scale=fcw[:, cc, 1:2], bias=fcb[:, cc:cc + 1],
                                 r=[P_.t(), fcw.t(), fcb.t()], w=[v.t()])
                        t0_ = t0s[(2 * n + (cc >= 22)) % 4]
                        self.act(t0_[:, 0:nb], P_[:, 0:nb], AF.Identity, scale=fcw[:, cc, 0:1], bias=self.zero_t[:, 0:1], r=[P_.t(), fcw.t()], w=[t0_.t()])
                        self.tt("dve", v[:, 0:nb], v[:, 0:nb], t0_[:, 0:nb], ALU.add, r=[v.t(), t0_.t()], w=[v.t()])
                        self.stt(v[:, 0:nb], P_[:, 2:nb + 2], fcw[:, cc, 2:3], v[:, 0:nb], ALU.mult, ALU.add, r=[P_.t(), v.t()], w=[v.t()])
                    self.act(vs[:, 0:nb], va[:, 0:nb], AF.Silu, r=[va.t()], w=[vs.t()])
                    self.tt("dve", actb[:, c, 0:nb], vs[:, 0:nb], vb[:, 0:nb], ALU.mult, r=[vs.t(), vb.t()], w=[actb.t()])
                if bi + 1 < len(blocks):
                    prep(bi + 1)
                for oc in range(8):
                    n += 1
                    D_ = pd[n % 2]
                    for c in range(22):
                        self.mm(D_[:, 0:nb], wdn[:, c, oc * 128:(oc + 1) * 128], actb[:, c, 0:nb], c == 0, c == 21,
                                r=[wdn.t(c // 11), actb.t()], w=[D_.t()])
                    self.stt(hb[:, oc, 1:nb + 1], D_[:, 0:nb], mv[:, 40 + oc, j:j + 1], hb[:, oc, 1:nb + 1], ALU.mult, ALU.add,
                             r=[D_.t(), hb.t()], w=[hb.t()])
                self.dma("sp", dvw[:, :, t0:t0 + nb], hb[:, :, 1:nb + 1], r=[hb.t()], w=[dst.t()])
        self.hcur = dst

    def finalstage(self):
        with self.stage():
            fw = self.sb("zfw", [128, 8], F32)
            self.dma("sp", fw[:], self.final_norm_w.ap().rearrange("(k p) -> p k", p=128), r=[], w=[fw.t()])
            NB = 256
            hbs = [self.sb("zhb%d" % i, [128, 8, NB], F32) for i in range(3)]
            sqs = [self.sb("zsq%d" % i, [128, 8, NB], BF16) for i in range(2)]
            rss = [self.sb("zrs%d" % i, [128, NB], F32) for i in range(2)]
            tmps = [self.sb("ztmp%d" % i, [128, 8, NB], F32) for i in range(2)]
            osb = [self.sb("zosb%d" % i, [128, 1024], F32) for i in range(2)]
            ssps = [self.ps("zssp%d" % i, [128, 512], F32) for i in range(2)]
            pT = [self.ps("zpT%d" % i, [128, 1024], F32) for i in range(2)]
            sv = self.hcur.h.ap().rearrange("(k p) t -> p k t", p=128)
            nblk = SEQ // NB
            cnt = {"n": 0}

            def X(bi):
                t0 = LC + bi * NB
                hb, sq, rs, ssp, tmp = hbs[bi % 3], sqs[bi % 2], rss[bi % 2], ssps[bi % 2], tmps[bi % 2]
                self.dma("sp", hb[:], sv[:, :, t0:t0 + NB], r=[self.hcur.t()], w=[hb.t()])
                self.act(sq[:], hb[:], AF.Square, r=[hb.t()], w=[sq.t()])
                for k in range(8):
                    self.mm(ssp[:, 0:NB], self.ones_b[:], sq[:, k, :], k == 0, k == 7, r=[sq.t()], w=[ssp.t()])
                self.act(rs[:], ssp[:, 0:NB], AF.Sqrt, bias=self.eps_t[:, 0:1], scale=1.0 / D, r=[ssp.t()], w=[rs.t()])
                self.S.add("dve", lambda e: e.reciprocal(out=rs[:], in_=rs[:]), r=[rs.t()], w=[rs.t()])
                self.tt("dve", tmp[:], hb[:], rs[:].unsqueeze(1).broadcast_to([128, 8, NB]), ALU.mult, r=[hb.t(), rs.t()], w=[tmp.t()])
                self.tt("dve", tmp[:], tmp[:], fw[:].unsqueeze(2).broadcast_to([128, 8, NB]), ALU.mult, r=[tmp.t(), fw.t()], w=[tmp.t()])

            def Y(bi):
                tmp = tmps[bi % 2]
                for tt_ in range(NB // 128):
                    cnt["n"] += 1
                    n = cnt["n"]
                    p, ob = pT[n % 2], osb[n % 2]
                    for k in range(8):
                        self.tr(p[:, k * 128:(k + 1) * 128], tmp[:, k, tt_ * 128:(tt_ + 1) * 128], self.ident_f[:], r=[tmp.t()], w=[p.t()])
                    self.cp("act" if n % 2 else "dve", ob[:], p[:], r=[p.t()], w=[ob.t()])
                    r0 = bi * NB + tt_ * 128
                    self.dma("sp", self.out.h.ap()[r0:r0 + 128, :], ob[:], r=[ob.t()], w=[self.out.t()])

            X(0)
            for bi in range(nblk):
                if bi + 1 < nblk:
                    X(bi + 1)
                Y(bi)

    def build_all(self):
        self.declare_io()
        self.consts()
        self.prologue()
        self.modstage()
        for l in range(DEPTH):
            need_ctx = l < DEPTH - 1
            with contextlib.ExitStack() as outer:
                self.instage(l, outer)
            self.nastage(l, need_ctx)
            self.gqastage(l, need_ctx)
            self.ssdstage(l, need_ctx)
            self.mergestage(l, need_ctx)
            self.ffnstage(l, need_ctx)
        self.finalstage()

    def dump(self, name, src_ap, shape, dt, r):
        d = self.dram(name, shape, dt, out=True)
        self.dma("sp", d.h.ap(), src_ap, r=r, w=[d.t()])
        return d


NA_TABLES = [(dr, True, True) for dr in range(14)] + [(2, False, True), (10, True, False)]


def host_tables(na_rel_bias):
    L = na_rel_bias.shape[0]
    p = np.arange(128)
    half, kc = p // 64, p % 64
    qc = np.arange(64)
    dcol = np.clip(kc[:, None] - qc[None, :] + 15, 0, 30)
    win0 = np.clip(qc - 8, 0, 48)
    ok = (kc[:, None] >= win0[None, :]) & (kc[:, None] < win0[None, :] + 16)
    bias_g = np.zeros((L, 16, 128, 8, 64), np.float32)
    mask = np.zeros((16, 128, 8, 64), np.float32)
    for tb, (dr, v0, v1) in enumerate(NA_TABLES):
        drow = dr + half
        g = na_rel_bias[:, :, drow[:, None], dcol]
        bias_g[:, tb] = np.transpose(g, (0, 2, 1, 3))
        rv = np.where(half == 0, v0, v1)
        mask[tb] = (ok & rv[:, None])[:, None, :].astype(np.float32)
    n_freq = 16
    inv_freq = (10000.0 ** (-np.arange(n_freq, dtype=np.float32) / n_freq)).astype(np.float32)
    t = np.arange(SEQ)
    row = (t // GW).astype(np.float32)
    col = (t % GW).astype(np.float32)
    ang = np.concatenate([row[:, None] * inv_freq, col[:, None] * inv_freq], axis=-1).astype(np.float32)
    rope = np.stack([np.cos(ang), np.sin(ang)], axis=1).astype(np.float32)
    return bias_g.reshape(L, 16, 128, 512), mask.reshape(16, 128, 512), rope


def make_in_maps(inputs, n_cores=8):
    f = lambda a: np.ascontiguousarray(np.asarray(a, dtype=np.float32))
    bias_g, mask, rope = host_tables(f(inputs["na_rel_bias"]))
    shared = {k: f(v) for k, v in inputs.items() if k not in ("x", "c", "ctx", "na_rel_bias")}
    shared["na_bias_g"] = bias_g
    shared["na_mask"] = mask
    shared["rope"] = rope
    x, c, ctx = f(inputs["x"]), f(inputs["c"]), f(inputs["ctx"])
    maps = []
    for b in range(n_cores):
        m = dict(shared)
        m["x"], m["c"], m["ctx"] = x[b], c[b], ctx[b]
        maps.append(m)
    return maps


_CACHE = {}


def kernel(**inputs):
    n_cores = 8
    if "kb" not in _CACHE:
        kb = KB()
        kb.build_all()
        _CACHE["kb"] = kb
    kb = _CACHE["kb"]
    maps = make_in_maps(inputs, n_cores)
    used = set(kb.din.keys())
    maps = [{k: v for k, v in m.items() if k in used} for m in maps]
    res = run_bass_kernel_spmd(kb.nc, maps, core_ids=list(range(n_cores)))
    return np.stack([np.asarray(r["out"], dtype=np.float32) for r in res.results], axis=0)
```

```python
import numpy as np
import ml_dtypes
import concourse.bass as bass
import concourse.mybir as mybir
from concourse.bass_utils import run_bass_kernel_spmd

F32, BF16 = mybir.dt.float32, mybir.dt.bfloat16
AF = mybir.ActivationFunctionType
ALU = mybir.AluOpType
AX = mybir.AxisListType

D = 1024
SEQ = 4096
LC = 256
NT = SEQ + LC
NTILE = NT // 128
DEPTH = 2
GW = 64
EPS = 1e-6
DIN = 7712
FH = 2816
C_NAQ, C_NAK, C_NAV, C_Z, C_XBC, C_DT, C_GQ, C_GK, C_GV, C_GATE = 0, 512, 1024, 1536, 2560, 3840, 3872, 4384, 4512, 4640


class Op:
    __slots__ = ("eng", "fn", "deps", "dma", "sem", "val", "flag", "semi")


class Sched:
    COMPUTE = ("pe", "act", "dve", "pool")

    def __init__(self, nc):
        self.nc = nc
        self.csem = {e: nc.alloc_semaphore(name="cs_" + e) for e in self.COMPUTE}
        self.ccnt = {e: 0 for e in self.COMPUTE}
        self.dpool = {"sp": [nc.alloc_semaphore(name="dsp%d" % i) for i in range(4)],
                      "pool": [nc.alloc_semaphore(name="dpl%d" % i) for i in range(2)]}
        self.duse = {q: [0] * len(v) for q, v in self.dpool.items()}
        self.drr = {q: 0 for q in self.dpool}
        self.waited = {e: {} for e in ("pe", "act", "dve", "pool", "sp")}
        self.nops = 0
        self.psum_names = set()
        self.begin()

    def begin(self):
        self.ops = []
        self.trk = {}
        self.dlast = {}

    def _entries(self, tok, create):
        name, slot = tok
        d = self.trk.setdefault(name, {})
        if slot is None:
            if create and None not in d:
                d[None] = [None, []]
            return list(d.values())
        out = []
        if None in d:
            out.append(d[None])
        if slot not in d and create:
            d[slot] = [None, []]
        if slot in d:
            out.append(d[slot])
        return out

    def add(self, eng, fn, r=(), w=(), dma=False):
        op = Op()
        op.eng, op.fn, op.dma, op.flag, op.sem, op.val = eng, fn, dma, False, None, 0
        idx = len(self.ops)
        deps = set()

        def want(d, kind):
            if d is None:
                return
            o = self.ops[d]
            if not o.dma and not dma and o.eng == eng:
                if eng == "pe" or kind != "raw":
                    return
            deps.add(d)

        for tok in r:
            tok = tok if isinstance(tok, tuple) else (tok, None)
            for e in self._entries(tok, True):
                want(e[0], "raw")
                if tok[0] in self.psum_names:
                    for rd in e[1]:
                        want(rd, "rar")
        for tok in w:
            tok = tok if isinstance(tok, tuple) else (tok, None)
            for e in self._entries(tok, True):
                want(e[0], "waw")
                for rd in e[1]:
                    want(rd, "war")
        for tok in r:
            tok = tok if isinstance(tok, tuple) else (tok, None)
            name, slot = tok
            self._entries(tok, True)
            tgt = self.trk[name][slot]
            if dma:
                tgt[1].append(idx)
            else:
                tgt[1][:] = [x for x in tgt[1] if self.ops[x].dma or self.ops[x].eng != eng]
                tgt[1].append(idx)
        for tok in w:
            tok = tok if isinstance(tok, tuple) else (tok, None)
            name, slot = tok
            d = self.trk[name]
            if slot is None:
                d.clear()
                d[None] = [idx, []]
            else:
                d[slot] = [idx, []]
        if dma:
            q = eng
            j = self.drr[q]
            self.drr[q] = (j + 1) % len(self.dpool[q])
            op.semi = j
            prev = self.dlast.get((q, j))
            if prev is not None:
                deps.add(prev)
            self.dlast[(q, j)] = idx
        op.deps = deps
        self.ops.append(op)
        return idx

    def emit(self):
        nc = self.nc
        ops = self.ops
        for op in ops:
            for d in op.deps:
                ops[d].flag = True
        for op in ops:
            if op.dma:
                q = op.eng
                self.duse[q][op.semi] += 1
                op.sem = self.dpool[q][op.semi]
                op.val = 16 * self.duse[q][op.semi]
            elif op.flag:
                self.ccnt[op.eng] += 1
                op.sem = self.csem[op.eng]
                op.val = self.ccnt[op.eng]
        self.nops += len(ops)
        sched = self

        def run(engname, eng):
            wd = sched.waited[engname]
            for op in ops:
                if op.eng != engname:
                    continue
                need = {}
                for d in op.deps:
                    o = ops[d]
                    key = id(o.sem)
                    if key not in need or need[key][1] < o.val:
                        need[key] = (o.sem, o.val)
                lst = []
                for key, (s, v) in need.items():
                    if wd.get(key, 0) >= v:
                        continue
                    wd[key] = v
                    lst.append((s, v))
                if op.dma or len(lst) > 1:
                    extra = lst if op.dma else lst[:-1]
                    for s, v in extra:
                        eng.wait_ge(s, v)
                    lst = [] if op.dma else lst[-1:]
                ins = op.fn(eng)
                if lst:
                    ins._wait_ge(lst[0][0], lst[0][1])
                if op.dma:
                    ins.then_inc(op.sem, 16)
                elif op.flag:
                    ins.then_inc(op.sem, 1)
            if engname in sched.dpool:
                for j, s in enumerate(sched.dpool[engname]):
                    v = 16 * sched.duse[engname][j]
                    if v and wd.get(id(s), 0) < v:
                        wd[id(s)] = v
                        eng.wait_ge(s, v)

        used = set(op.eng for op in ops)
        with nc.Block() as block:
            if "pe" in used:
                @block.tensor
                def _(e):
                    run("pe", e)
            if "act" in used:
                @block.scalar
                def _(e):
                    run("act", e)
            if "dve" in used:
                @block.vector
                def _(e):
                    run("dve", e)
            if "pool" in used:
                @block.gpsimd
                def _(e):
                    run("pool", e)
            if "sp" in used:
                @block.sync
                def _(e):
                    run("sp", e)
        self.begin()


class Buf:
    def __init__(self, name, h):
        self.name, self.h = name, h

    def t(self, slot=None):
        return (self.name, slot)

    def __getitem__(self, k):
        return self.h[k]


class Sl:
    def __init__(self, buf, n):
        self.b, self.n, self.name = buf, n, buf.name

    def t(self, slot=None):
        return self.b.t(slot)

    def __getitem__(self, k):
        if not isinstance(k, tuple):
            k = (k,)
        nd = len(self.b.h.shape)
        k = tuple(k) + (slice(None),) * (nd - len(k))
        last = k[-1]
        if isinstance(last, slice) and last == slice(None):
            k = k[:-1] + (slice(0, self.n),)
        return self.b.h[k]


def rap(buf, row, p0, npart, off, dims):
    return bass.AP(buf.h, p0 * row + off, [[row, npart]] + [list(d) for d in dims])


import contextlib

NA_NOPOOL = True
PENG = "dve"
TOKBLKS = [(0, 256, 1)] + [(256 + 512 * i, 512, 0) for i in range(8)]


class KB:
    def __init__(self, dbg=()):
        self.dbg = set(dbg)
        nc = self.nc = bass.Bass("TRN2", target_bir_lowering=False)
        self.S = Sched(nc)
        self.ges = contextlib.ExitStack()
        self.es = None
        self.din = {}
        self._uid = 0

    def inp(self, name, shape, dt=F32):
        t = self.nc.dram_tensor(name, list(shape), dt, kind="ExternalInput")
        self.din[name] = t
        return t

    def dram(self, name, shape, dt, out=False):
        kind = "ExternalOutput" if (out or name in self.dbg) else "Internal"
        return Buf(name, self.nc.dram_tensor(name, list(shape), dt, kind=kind))

    def sb(self, name, shape, dt, glob=False, es=None):
        es = es if es is not None else (self.ges if glob else self.es)
        self._uid += 1
        name = "%s_u%d" % (name, self._uid)
        return Buf(name, es.enter_context(self.nc.sbuf_tensor(name, list(shape), dt)))

    def ps(self, name, shape, dt=F32):
        self._uid += 1
        name = "%s_u%d" % (name, self._uid)
        self.S.psum_names.add(name)
        return Buf(name, self.es.enter_context(self.nc.psum_tensor(name, list(shape), dt)))

    @contextlib.contextmanager
    def stage(self):
        self.es = contextlib.ExitStack()
        with self.es:
            yield
            with self.nc.allow_non_contiguous_dma(reason="small strided parameter loads"):
                self.S.emit()
        self.es = None

    def dma(self, q, out, in_, r, w):
        self.S.add(q, lambda e: e.dma_start(out=out, in_=in_), r=r, w=w, dma=True)

    def mm(self, out, lhsT, rhs, start, stop, r, w):
        self.S.add("pe", lambda e: e.matmul(out, lhsT, rhs, start=start, stop=stop), r=r, w=w)

    def tr(self, out, in_, ident, r, w):
        self.S.add("pe", lambda e: e.transpose(out, in_, ident), r=r, w=w)

    def act(self, out, in_, func, r, w, bias=0.0, scale=1.0, accum_out=None):
        if accum_out is None:
            self.S.add("act", lambda e: e.activation(out=out, in_=in_, func=func, bias=bias, scale=scale), r=r, w=w)
        else:
            self.S.add("act", lambda e: e.activation(out=out, in_=in_, func=func, bias=bias, scale=scale,
                                                     accum_out=accum_out), r=r, w=w)

    def tt(self, eng, out, in0, in1, op, r, w):
        self.S.add(eng, lambda e: e.tensor_tensor(out=out, in0=in0, in1=in1, op=op), r=r, w=w)

    def ts(self, eng, out, in0, s1, s2, op0, op1, r, w):
        if s2 is None:
            self.S.add(eng, lambda e: e.tensor_scalar(out=out, in0=in0, scalar1=s1, scalar2=None, op0=op0), r=r, w=w)
        else:
            self.S.add(eng, lambda e: e.tensor_scalar(out=out, in0=in0, scalar1=s1, scalar2=s2, op0=op0, op1=op1),
                       r=r, w=w)

    def stt(self, out, in0, scalar, in1, op0, op1, r, w):
        self.S.add("dve", lambda e: e.scalar_tensor_tensor(out=out, in0=in0, scalar=scalar, in1=in1, op0=op0, op1=op1),
                   r=r, w=w)

    def cp(self, eng, out, in_, r, w):
        if eng == "act":
            self.S.add("act", lambda e: e.copy(out=out, in_=in_), r=r, w=w)
        else:
            self.S.add(eng, lambda e: e.tensor_copy(out=out, in_=in_), r=r, w=w)

    def memset(self, eng, ap, val, w):
        self.S.add(eng, lambda e: e.memset(ap, val), r=(), w=w)

    def declare_io(self):
        L = DEPTH
        self.x = self.inp("x", [SEQ, D])
        self.c = self.inp("c", [D])
        self.ctx = self.inp("ctx", [LC, D])
        self.c_ctx = self.inp("c_ctx", [D])
        self.w_mod = self.inp("w_mod", [L, D, 6 * D])
        self.b_mod = self.inp("b_mod", [L, 6 * D])
        self.norm1_w = self.inp("norm1_w", [L, D])
        self.norm2_w = self.inp("norm2_w", [L, D])
        self.w_in = self.inp("w_in", [L, D, DIN])
        self.na_bias_g = self.inp("na_bias_g", [L, 16, 128, 512])
        self.na_mask = self.inp("na_mask", [16, 128, 512])
        self.rope = self.inp("rope", [SEQ, 2, 32])
        self.ssm_conv_w = self.inp("ssm_conv_w", [L, 3, 1280])
        self.ssm_conv_b = self.inp("ssm_conv_b", [L, 1280])
        self.ssm_a_log = self.inp("ssm_a_log", [L, 2, 16])
        self.ssm_dt_bias = self.inp("ssm_dt_bias", [L, 2, 16])
        self.ssm_d = self.inp("ssm_d", [L, 16])
        self.ssm_norm_w = self.inp("ssm_norm_w", [L, 1024])
        self.q_norm_w = self.inp("q_norm_w", [L, 64])
        self.k_norm_w = self.inp("k_norm_w", [L, 64])
        self.w_out_na = self.inp("w_out_na", [L, 512, D])
        self.w_out_ssm = self.inp("w_out_ssm", [L, 1024, D])
        self.w_out_gqa = self.inp("w_out_gqa", [L, 512, D])
        self.w_o = self.inp("w_o", [L, D, D])
        self.ffn_w_up = self.inp("ffn_w_up", [L, D, 2 * FH])
        self.ffn_conv_w = self.inp("ffn_conv_w", [L, 3, 2 * FH])
        self.ffn_conv_b = self.inp("ffn_conv_b", [L, 2 * FH])
        self.ffn_w_down = self.inp("ffn_w_down", [L, FH, D])
        self.final_norm_w = self.inp("final_norm_w", [D])
        self.out = self.dram("out", [SEQ, D], F32, out=True)
        self.hT = self.dram("hT", [D, NT], F32)
        self.hT2 = self.dram("hT2", [D, NT], F32)
        self.hcur = self.hT
        self.naqT = self.dram("naqT", [512, NT], BF16)
        self.nakT = self.dram("nakT", [512, NT], BF16)
        self.nav = self.dram("nav", [NT, 512], BF16)
        self.zs = self.dram("zs", [NT, 1024], BF16)
        self.xbcT = self.dram("xbcT", [1280, NT], BF16)
        self.dtr = self.dram("dtr", [NT, 32], F32)
        self.gqT = self.dram("gqT", [512, NT], BF16)
        self.gkT2 = self.dram("gkT2", [2, 128, NT], BF16)
        self.gv = self.dram("gv", [NT, 128], BF16)
        self.gatesT = self.dram("gatesT", [3072, NT], BF16)
        self.onaT = self.dram("onaT", [8, 64, NT], BF16)
        self.ogT = self.dram("ogT", [8, 64, NT], BF16)
        self.ossmT = self.dram("ossmT", [1024, NT], BF16)
        self.yf = self.dram("yf", [NT, 1024], F32)
        self.wup_b = self.dram("wup_b", [D, 2 * FH], BF16)
        self.wdn_b = self.dram("wdn_b", [FH, D], BF16)
        self.wna_b = self.dram("wna_b", [512, D], BF16)
        self.wgq_b = self.dram("wgq_b", [512, D], BF16)
        self.wss_b = self.dram("wss_b", [1024, D], BF16)
        self.wo_b = self.dram("wo_b", [D, D], BF16)

    def consts(self):
        nc = self.nc
        self.ident_f = self.sb("ident_f", [128, 128], F32, glob=True)
        self.ident_b = self.sb("ident_b", [128, 128], BF16, glob=True)
        self.ones_b = self.sb("ones_b", [128, 128], BF16, glob=True)
        self.ones_f = self.sb("ones_f", [128, 128], F32, glob=True)
        self.modv = [self.sb("modv%d" % l, [128, 48, 2], F32, glob=True) for l in range(DEPTH)]
        self.A1 = [self.sb("A1_%d" % l, [128, 8, 2], F32, glob=True) for l in range(DEPTH)]
        self.A2 = [self.sb("A2_%d" % l, [128, 8, 2], F32, glob=True) for l in range(DEPTH)]
        self.eps_t = self.sb("eps_t", [128, 1], F32, glob=True)
        self.zero_t = self.sb("zero_t", [128, 1], F32, glob=True)
        with self.stage():
            idf, idb, ob, of = self.ident_f, self.ident_b, self.ones_b, self.ones_f
            self.memset("dve", self.eps_t[:], EPS, w=[self.eps_t.t()])
            self.memset("dve", self.zero_t[:], 0.0, w=[self.zero_t.t()])
            self.memset("pool", idf[:], 0.0, w=[idf.t()])
            self.S.add("pool", lambda e: e.affine_select(out=idf[:], in_=idf[:], pattern=[[-1, 128]],
                                                         compare_op=ALU.not_equal, fill=1.0, base=0,
                                                         channel_multiplier=1), r=[idf.t()], w=[idf.t()])
            self.cp("dve", idb[:], idf[:], r=[idf.t()], w=[idb.t()])
            self.memset("dve", ob[:], 1.0, w=[ob.t()])
            self.memset("dve", of[:], 1.0, w=[of.t()])

    def prologue(self):
        with self.stage():
            xin = [self.sb("xin%d" % i, [128, D], F32) for i in range(3)]
            pt = [self.ps("ptr%d" % i, [128, 1024], F32) for i in range(2)]
            stg = [self.sb("stg%d" % i, [128, 8, 512], F32) for i in range(2)]
            hTv = self.hT.h.ap().rearrange("(k p) t -> p k t", p=128)
            groups = [(0, 2)] + [(2 + 4 * g, 4) for g in range(8)]
            craw = self.sb("craw", [128, 8, 2], F32)
            csil = self.sb("csil", [128, 8, 2], F32)
            self.dma("sp", craw[:, :, 0], self.c.ap().rearrange("(k p) -> p k", p=128), r=[], w=[craw.t(0)])
            self.dma("sp", craw[:, :, 1], self.c_ctx.ap().rearrange("(k p) -> p k", p=128), r=[], w=[craw.t(1)])
            self.act(csil[:], craw[:], AF.Silu, r=[craw.t()], w=[csil.t()])
            wm = [self.sb("wm%d" % i, [128, 8, 512], F32) for i in range(2)]
            bm = self.sb("bm", [128, 48], F32)
            nw = self.sb("nw", [128, 8], F32)
            pms = [self.ps("pmod%d" % l, [128, 512], F32) for l in range(DEPTH)]
            cnt = {"n": 0}

            def mod_group(l, og):
                pm = pms[l]
                wv = self.w_mod.ap()[l].rearrange("(k p) n -> p k n", p=128)
                w = wm[cnt["n"] % 2]
                cnt["n"] += 1
                self.dma("sp", w[:], wv[:, :, og * 512:(og + 1) * 512], r=[], w=[w.t()])
                for oc in range(4):
                    o = og * 4 + oc
                    for k in range(8):
                        self.mm(pm[:, 2 * o:2 * o + 2], w[:, k, oc * 128:(oc + 1) * 128], csil[:, k, :],
                                k == 0, k == 7, r=[w.t(), csil.t()], w=[pm.t()])

            def mod_finish(l):
                pm = pms[l]
                self.dma("sp", bm[:], self.b_mod.ap()[l].rearrange("(o p) -> p o", p=128), r=[], w=[bm.t()])
                mv = self.modv[l]
                self.tt("dve", mv[:], pm[:, 0:96].rearrange("p (o j) -> p o j", j=2),
                        bm[:].unsqueeze(2).broadcast_to([128, 48, 2]), ALU.add, r=[pm.t(), bm.t()], w=[mv.t()])
                for (A, nwin, sc0) in ((self.A1[l], self.norm1_w, 8), (self.A2[l], self.norm2_w, 32)):
                    self.dma("sp", nw[:], nwin.ap()[l].rearrange("(k p) -> p k", p=128), r=[], w=[nw.t()])
                    self.ts("dve", A[:], mv[:, sc0:sc0 + 8, :], 1.0, None, ALU.add, None, r=[mv.t()], w=[A.t()])
                    self.tt("dve", A[:], A[:], nw[:].unsqueeze(2).broadcast_to([128, 8, 2]), ALU.mult,
                            r=[A.t(), nw.t()], w=[A.t()])

            todo = [(l, og) for l in range(DEPTH) for og in range(12)]
            n = 0
            for gi, (ti0, cntg) in enumerate(groups):
                sg = stg[gi % 2]
                for j in range(cntg):
                    ti = ti0 + j
                    xb = xin[n % 3]
                    p = pt[n % 2]
                    n += 1
                    src = self.ctx.ap()[ti * 128:(ti + 1) * 128, :] if ti < 2 else \
                        self.x.ap()[(ti - 2) * 128:(ti - 1) * 128, :]
                    self.dma("sp", xb[:], src, r=[], w=[xb.t()])
                    for k in range(8):
                        self.tr(p[:, k * 128:(k + 1) * 128], xb[:, k * 128:(k + 1) * 128], self.ident_f[:],
                                r=[xb.t()], w=[p.t()])
                    dst = sg[:, :, j * 128:(j + 1) * 128]
                    src_ps = p[:].rearrange("p (k t) -> p k t", k=8)
                    self.cp("act" if n % 2 else "dve", dst, src_ps, r=[p.t()], w=[sg.t(j)])
                t0 = ti0 * 128
                self.dma("sp", hTv[:, :, t0:t0 + cntg * 128], sg[:, :, 0:cntg * 128], r=[sg.t()], w=[self.hT.t()])
                for _ in range(3):
                    if todo:
                        l_, og_ = todo.pop(0)
                        mod_group(l_, og_)
                        if og_ == 11:
                            mod_finish(l_)
            while todo:
                l_, og_ = todo.pop(0)
                mod_group(l_, og_)
                if og_ == 11:
                    mod_finish(l_)

    def modstage(self):
        pass

    def norm_block(self, hb, N, A, mv, b0, j, out_fn, out_w, sq, ssp, rs, tmp):
        self.act(sq[:, :, 0:N], hb[:, :, 0:N], AF.Square, r=[hb.t()], w=[sq.t()])
        for k in range(8):
            self.mm(ssp[:, 0:N], self.ones_b[:], sq[:, k, 0:N], k == 0, k == 7, r=[sq.t()], w=[ssp.t()])
        self.act(rs[:, 0:N], ssp[:, 0:N], AF.Sqrt, bias=self.eps_t[:, 0:1], scale=1.0 / D, r=[ssp.t()], w=[rs.t()])
        self.S.add("dve", lambda e: e.reciprocal(out=rs[:, 0:N], in_=rs[:, 0:N]), r=[rs.t()], w=[rs.t()])
        if isinstance(tmp, list):
            for k in range(8):
                tk = tmp[k % len(tmp)]
                self.tt("dve", tk[:, 0:N], hb[:, k, 0:N], rs[:, 0:N], ALU.mult, r=[hb.t(), rs.t()], w=[tk.t()])
                self.act(out_fn(k), tk[:, 0:N], AF.Identity, scale=A[:, k, j:j + 1], bias=mv[:, b0 + k, j:j + 1],
                         r=[tk.t()], w=out_w)
            return
        self.tt("dve", tmp[:, :, 0:N], hb[:, :, 0:N], rs[:, 0:N].unsqueeze(1).broadcast_to([128, 8, N]), ALU.mult,
                r=[hb.t(), rs.t()], w=[tmp.t()])
        for k in range(8):
            self.act(out_fn(k), tmp[:, k, 0:N], AF.Identity, scale=A[:, k, j:j + 1], bias=mv[:, b0 + k, j:j + 1],
                     r=[tmp.t()], w=out_w)

    def instage(self, l, outer):
        xnT = self.sb("xnT", [128, 8, NT], BF16, es=outer)
        hTv = self.hcur.h.ap().rearrange("(k p) t -> p k t", p=128)
        with self.stage():
            hbs = [self.sb("hb%d" % i, [128, 8, 512], F32) for i in range(3)]
            sqs = [self.sb("sq%d" % i, [128, 8, 512], BF16) for i in range(2)]
            tmps = [self.sb("ntmp%d" % i, [128, 8, 512], F32) for i in range(2)]
            rss = [self.sb("nrs%d" % i, [128, 512], F32) for i in range(2)]
            ssps = [self.ps("nssp%d" % i, [128, 512], F32) for i in range(2)]
            A, mv = self.A1[l], self.modv[l]

            def X(bi):
                t0, N, isctx = TOKBLKS[bi]
                hb, sq, rs, ssp = hbs[bi % 3], sqs[bi % 2], rss[bi % 2], ssps[bi % 2]
                self.dma("sp", hb[:, :, 0:N], hTv[:, :, t0:t0 + N], r=[self.hcur.t()], w=[hb.t()])
                self.act(sq[:, :, 0:N], hb[:, :, 0:N], AF.Square, r=[hb.t()], w=[sq.t()])
                for k in range(8):
                    self.mm(ssp[:, 0:N], self.ones_b[:], sq[:, k, 0:N], k == 0, k == 7, r=[sq.t()], w=[ssp.t()])
                self.act(rs[:, 0:N], ssp[:, 0:N], AF.Sqrt, bias=self.eps_t[:, 0:1], scale=1.0 / D, r=[ssp.t()], w=[rs.t()])
                self.S.add("dve", lambda e: e.reciprocal(out=rs[:, 0:N], in_=rs[:, 0:N]), r=[rs.t()], w=[rs.t()])

            def Y(bi):
                t0, N, j = TOKBLKS[bi]
                hb, rs, tmp = hbs[bi % 3], rss[bi % 2], tmps[bi % 2]
                self.tt("dve", tmp[:, :, 0:N], hb[:, :, 0:N], rs[:, 0:N].unsqueeze(1).broadcast_to([128, 8, N]), ALU.mult,
                        r=[hb.t(), rs.t()], w=[tmp.t()])
                for k in range(8):
                    self.act(xnT[:, k, t0:t0 + N], tmp[:, k, 0:N], AF.Identity, scale=A[:, k, j:j + 1], bias=mv[:, k, j:j + 1],
                             r=[tmp.t()], w=[xnT.t(bi)])

            X(0)
            for bi in range(len(TOKBLKS)):
                if bi + 1 < len(TOKBLKS):
                    X(bi + 1)
                Y(bi)
        wv = self.w_in.ap()[l].rearrange("(k p) n -> p k n", p=128)
        with self.stage():
            Ws = [self.sb("Wc%d" % i, [128, 8, 512], BF16) for i in range(2)]
            pss = [self.ps("pin%d" % i, [128, 512], F32) for i in range(4)]
            stg = [self.sb("stgi%d" % i, [128, 4, 512], BF16) for i in range(3)]
            stgf = [self.sb("stgf%d" % i, [128, 4, 32], F32) for i in range(2)]
            cnt = {"w": 0, "p": 0, "s": 0, "e": 0}

            def nxt(lst, key):
                b = lst[cnt[key] % len(lst)]
                cnt[key] += 1
                return b

            def evac(func, out, in_, r, w):
                if func == "copy":
                    cnt["e"] += 1
                    self.cp("act" if cnt["e"] % 2 else "dve", out, in_, r=r, w=w)
                elif func == "silu":
                    self.act(out, in_, AF.Silu, r=r, w=w)
                elif func == "sigmoid":
                    self.act(out, in_, AF.Sigmoid, r=r, w=w)
                elif func == "q8":
                    self.ts("dve", out, in_, 0.125, None, ALU.mult, None, r=r, w=w)

            def blk_of_tile(ti):
                return 0 if ti < 2 else 1 + (ti - 2) // 4

            def fm_group(c0, n, func, dstT, drow0):
                W = nxt(Ws, "w")
                self.dma("pool", W[:, :, 0:n], wv[:, :, c0:c0 + n], r=[], w=[W.t()])
                for bi, (t0, N, isctx) in enumerate(TOKBLKS):
                    sg = nxt(stg, "s")
                    for oc in range(n // 128):
                        ps = nxt(pss, "p")
                        for k in range(8):
                            self.mm(ps[:, 0:N], W[:, k, oc * 128:(oc + 1) * 128], xnT[:, k, t0:t0 + N], k == 0, k == 7,
                                    r=[W.t(), xnT.t(bi)], w=[ps.t()])
                        evac(func, sg[:, oc, 0:N], ps[:, 0:N], r=[ps.t()], w=[sg.t(oc)])
                    dst = dstT.h.ap()[drow0:drow0 + n, t0:t0 + N].rearrange("(o p) t -> p o t", p=128)
                    self.dma("sp", dst, sg[:, 0:n // 128, 0:N], r=[sg.t()], w=[dstT.t()])

            def tm_group(c0, n, func, dst, dcol0, fp32=False):
                W = nxt(Ws, "w")
                self.dma("pool", W[:, :, 0:n], wv[:, :, c0:c0 + n], r=[], w=[W.t()])
                for (ti0, tc) in [(0, 2)] + [(2 + 4 * g, 4) for g in range(8)]:
                    sg = nxt(stgf, "s") if fp32 else nxt(stg, "s")
                    for j in range(tc):
                        ti = ti0 + j
                        ps = nxt(pss, "p")
                        for k in range(8):
                            self.mm(ps[:, 0:n], xnT[:, k, ti * 128:(ti + 1) * 128], W[:, k, 0:n], k == 0, k == 7,
                                    r=[W.t(), xnT.t(blk_of_tile(ti))], w=[ps.t()])
                        evac(func, sg[:, j, 0:n], ps[:, 0:n], r=[ps.t()], w=[sg.t(j)])
                    d = dst.h.ap()[ti0 * 128:(ti0 + tc) * 128, dcol0:dcol0 + n].rearrange("(j p) n -> p j n", p=128)
                    self.dma("sp", d, sg[:, 0:tc, 0:n], r=[sg.t()], w=[dst.t()])

            fm_group(C_NAQ, 512, "q8", self.naqT, 0)
            fm_group(C_NAK, 512, "copy", self.nakT, 0)
            tm_group(C_NAV, 512, "copy", self.nav, 0)
            tm_group(C_Z, 512, "silu", self.zs, 0)
            tm_group(C_Z + 512, 512, "silu", self.zs, 512)
            fm_group(C_XBC, 512, "copy", self.xbcT, 0)
            fm_group(C_XBC + 512, 512, "copy", self.xbcT, 512)
            fm_group(C_XBC + 1024, 256, "copy", self.xbcT, 1024)
            tm_group(C_DT, 32, "copy", self.dtr, 0, fp32=True)
            tm_group(C_GV, 128, "copy", self.gv, 0)
            for g in range(6):
                fm_group(C_GATE + 512 * g, 512, "sigmoid", self.gatesT, 512 * g)
            self.gqa_group(l, xnT, wv, blk_of_tile)

    def gqa_group(self, l, xnT, wv, blk_of_tile):
        wg = self.sb("wg", [128, 8, 640], BF16)
        self.dma("pool", wg[:], wv[:, :, C_GQ:C_GQ + 640], r=[], w=[wg.t()])
        ropeT = self.sb("ropeT", [128, 32, 64], F32)
        self.dma("sp", ropeT[:], self.rope.ap().rearrange("(i p) a b -> p i (a b)", p=128), r=[], w=[ropeT.t()])
        wqk = self.sb("wqk", [128, 640], F32)
        self.dma("sp", wqk[:, 0:64], self.q_norm_w.ap()[l:l + 1, :].broadcast_to([128, 64]), r=[], w=[wqk.t()])
        self.dma("sp", wqk[:, 512:576], self.k_norm_w.ap()[l:l + 1, :].broadcast_to([128, 64]), r=[], w=[wqk.t()])
        self.ts("dve", wqk[:, 0:64], wqk[:, 0:64], 0.125, None, ALU.mult, None, r=[wqk.t()], w=[wqk.t()])
        for h in range(1, 8):
            self.cp("dve", wqk[:, h * 64:(h + 1) * 64], wqk[:, 0:64], r=[wqk.t()], w=[wqk.t()])
        self.cp("dve", wqk[:, 576:640], wqk[:, 512:576], r=[wqk.t()], w=[wqk.t()])
        ps2 = self.ps("pg2", [128, 1024], F32)
        pT = self.ps("pgT", [128, 6, 128], BF16)
        sqv = self.sb("gsqv", [128, 640], F32)
        ss10 = self.sb("gss", [128, 10], F32)
        xn = self.sb("gxn", [128, 640], F32)
        t1 = self.sb("gt1", [128, 320], F32)
        t2 = self.sb("gt2", [128, 320], F32)
        t3 = self.sb("gt3", [128, 320], F32)
        t4 = self.sb("gt4", [128, 320], F32)
        qkb = self.sb("gqkb", [128, 640], BF16)
        kd = self.sb("gkd", [128, 256], BF16)
        stq = [self.sb("gstq%d" % i, [128, 6, 512], BF16) for i in range(2)]
        v3 = lambda ap, a: ap.rearrange("p (a b) -> p a b", a=a)
        gi = 0
        for (ti0, tc) in [(0, 2)] + [(2 + 4 * g, 4) for g in range(8)]:
            sg = stq[gi % 2]
            gi += 1
            for j in range(tc):
                ti = ti0 + j
                xs = [wg.t(), xnT.t(blk_of_tile(ti))]
                for k in range(8):
                    self.mm(ps2[:, 0:512], xnT[:, k, ti * 128:(ti + 1) * 128], wg[:, k, 0:512], k == 0, k == 7,
                            r=xs, w=[ps2.t(0)])
                for k in range(8):
                    self.mm(ps2[:, 512:640], xnT[:, k, ti * 128:(ti + 1) * 128], wg[:, k, 512:640], k == 0, k == 7,
                            r=xs, w=[ps2.t(1)])
                self.act(sqv[:], ps2[:, 0:640], AF.Square, r=[ps2.t()], w=[sqv.t()])
                self.S.add("dve", lambda e: e.tensor_reduce(out=ss10[:], in_=v3(sqv[:], 10), axis=AX.X, op=ALU.add),
                           r=[sqv.t()], w=[ss10.t()])
                self.act(ss10[:], ss10[:], AF.Sqrt, bias=self.eps_t[:, 0:1], scale=1.0 / 64, r=[ss10.t()], w=[ss10.t()])
                self.S.add("dve", lambda e: e.reciprocal(out=ss10[:], in_=ss10[:]), r=[ss10.t()], w=[ss10.t()])
                self.tt("dve", v3(xn[:], 10), v3(ps2[:, 0:640], 10), ss10[:].unsqueeze(2).broadcast_to([128, 10, 64]),
                        ALU.mult, r=[ps2.t(), ss10.t()], w=[xn.t()])
                self.tt("dve", xn[:], xn[:], wqk[:], ALU.mult, r=[xn.t(), wqk.t()], w=[xn.t()])
                if ti >= 2:
                    x4 = xn[:].rearrange("p (h i two) -> p h i two", h=10, two=2)
                    o4 = qkb[:].rearrange("p (h i two) -> p h i two", h=10, two=2)
                    xe, xo, oe, oo = x4[:, :, :, 0], x4[:, :, :, 1], o4[:, :, :, 0], o4[:, :, :, 1]
                    cosb = ropeT[:, ti - 2, 0:32].unsqueeze(1).broadcast_to([128, 10, 32])
                    sinb = ropeT[:, ti - 2, 32:64].unsqueeze(1).broadcast_to([128, 10, 32])
                    a1, a2, a3, a4 = v3(t1[:], 10), v3(t2[:], 10), v3(t3[:], 10), v3(t4[:], 10)
                    self.tt("dve", a1, xe, cosb, ALU.mult, r=[xn.t(), ropeT.t()], w=[t1.t()])
                    self.tt("dve", a2, xo, sinb, ALU.mult, r=[xn.t(), ropeT.t()], w=[t2.t()])
                    self.tt("dve", oe, a1, a2, ALU.subtract, r=[t1.t(), t2.t()], w=[qkb.t(0)])
                    self.tt(PENG, a3, xe, sinb, ALU.mult, r=[xn.t(), ropeT.t()], w=[t3.t()])
                    self.tt(PENG, a4, xo, cosb, ALU.mult, r=[xn.t(), ropeT.t()], w=[t4.t()])
                    self.tt(PENG, oo, a3, a4, ALU.add, r=[t3.t(), t4.t()], w=[qkb.t(1)])
                else:
                    self.cp("dve", qkb[:], xn[:], r=[xn.t()], w=[qkb.t()])
                kd4 = kd[:].rearrange("p (g c d) -> p g c d", g=2, c=2)
                ksrc = v3(qkb[:, 512:640], 2).unsqueeze(2).broadcast_to([128, 2, 2, 64])
                self.cp("act", kd4, ksrc, r=[qkb.t()], w=[kd.t()])
                for c in range(4):
                    self.tr(pT[:, c, :], qkb[:, c * 128:(c + 1) * 128], self.ident_b[:], r=[qkb.t()], w=[pT.t()])
                for g in range(2):
                    self.tr(pT[:, 4 + g, :], kd[:, g * 128:(g + 1) * 128], self.ident_b[:], r=[kd.t()], w=[pT.t()])
                self.cp("act", sg[:, :, j * 128:(j + 1) * 128], pT[:], r=[pT.t()], w=[sg.t(j)])
            t0, n = ti0 * 128, tc * 128
            self.dma("sp", self.gqT.h.ap()[:, t0:t0 + n].rearrange("(c p) t -> p c t", p=128), sg[:, 0:4, 0:n],
                     r=[sg.t()], w=[self.gqT.t()])
            self.dma("sp", self.gkT2.h.ap()[:, :, t0:t0 + n].rearrange("g p t -> p g t"), sg[:, 4:6, 0:n],
                     r=[sg.t()], w=[self.gkT2.t()])

    def softmax_finish(self, po, N, rrow, pb, pbs, out_ap, out_w):
        self.S.add("dve", lambda e: e.reciprocal(out=rrow[64:65, 0:N], in_=po[64:65, 0:N]), r=[po.t()], w=[rrow.t()])
        self.mm(pb[0:64, 0:N], self.ones_f[64:65, 0:64], rrow[64:65, 0:N], True, True, r=[rrow.t()], w=[pb.t()])
        self.cp("act", pbs[0:64, 0:N], pb[0:64, 0:N], r=[pb.t()], w=[pbs.t()])
        self.tt("dve", out_ap, po[0:64, 0:N], pbs[0:64, 0:N], ALU.mult, r=[po.t(), pbs.t()], w=out_w)

    def nastage(self, l, need_ctx, lim=None):
        with self.stage():
            kT = self.sb("nkT", [128, 4, NT], BF16)
            qT = self.sb("nqT", [128, 4, NT], BF16)
            V = self.sb("nV", [128, NTILE, 8, 65], BF16)
            TB = self.sb("nTB", [128, 16, 512], BF16)
            self.dma("sp", kT[:], self.nakT.h.ap().rearrange("(c p) t -> p c t", p=128), r=[self.nakT.t()], w=[kT.t()])
            self.dma("sp", qT[:], self.naqT.h.ap().rearrange("(c p) t -> p c t", p=128), r=[self.naqT.t()], w=[qT.t()])
            self.memset("dve", V[:, :, :, 64:65], 1.0, w=[V.t("ones")])
            nv = self.nav.h.ap().rearrange("(i p) (h d) -> p i h d", p=128, h=8)
            for i0 in range(NTILE):
                self.dma("sp", V[:, i0, :, 0:64], nv[:, i0], r=[self.nav.t()], w=[V.t("d%d" % i0)])
            bst = [self.sb("nbst%d" % i, [128, 2, 512], F32) for i in range(2)]
            mst = [self.sb("nmst%d" % i, [128, 2, 512], F32) for i in range(2)]
            for ch in range(8):
                b, m = bst[ch % 2], mst[ch % 2]
                self.dma("sp", b[:], self.na_bias_g.ap()[l, 2 * ch:2 * ch + 2].rearrange("t p n -> p t n"), r=[], w=[b.t()])
                self.dma("sp", m[:], self.na_mask.ap()[2 * ch:2 * ch + 2].rearrange("t p n -> p t n"), r=[], w=[m.t()])
                self.act(b[:], b[:], AF.Exp, r=[b.t()], w=[b.t()])
                self.tt("dve", TB[:, 2 * ch:2 * ch + 2, :], b[:], m[:], ALU.mult, r=[b.t(), m.t()], w=[TB.t(ch)])
            TB5 = TB[:].rearrange("p t (c e q) -> p t c e q", c=4, e=2)
            pS = [self.ps("nps%d" % i, [128, 512], F32) for i in range(4)]
            pO = [self.ps("npo%d" % i, [128, 512], F32) for i in range(2)]
            pb = self.ps("npb", [128, 512], F32)
            P = [[[self.sb("nP%d_%d_%d" % (u, a, e), [128, 512], BF16) for e in range(2)] for a in range(4)]
                 for u in range(2)]
            rrow = self.sb("nrrow", [128, 512], F32)
            pbs = self.sb("npbs", [64, 512], F32)
            osb = [self.sb("nosb%d" % i, [64, 8, 256], BF16) for i in range(2)]
            units = []
            if need_ctx:
                for u in range(4):
                    units.append((u * 64, [(0, 0, None), (1, 128, None)]))
            for r_ in range(64):
                rs = min(max(r_ - 4, 0), 56)
                kts = [(0, 0, None), (1, 128, None)]
                if rs % 2 == 0:
                    for j in range(4):
                        a = rs + 2 * j
                        kts.append((2 + a // 2, 256 + a * 64, a - r_ + 7))
                else:
                    for j in range(5):
                        a = rs - 1 + 2 * j
                        dr = a - r_ + 7
                        tb = 14 if j == 0 else (15 if j == 4 else dr)
                        kts.append((2 + a // 2, 256 + a * 64, tb))
                units.append((256 + r_ * 64, kts))
            psn = 0
            mtog = 0
            npend = []
            if lim:
                units = units[:lim]
            def phaseA(ui, hook=None):
                nonlocal psn, mtog
                tq0, kts = units[ui]
                ub = ui % 2
                nk = len(kts)
                npair = (nk + 1) // 2
                for a in range(npair):
                    pair = kts[2 * a:2 * a + 2]
                    banks = [pS[psn % 4], pS[(psn + 1) % 4]]
                    psn += 2
                    for s_, (vt, kc0, tb) in enumerate(pair):
                        for h in range(8):
                            e, c = h % 2, h // 2
                            self.mm(banks[e][:, s_ * 256 + c * 64: s_ * 256 + c * 64 + 64],
                                    kT[e * 64:(e + 1) * 64, c, kc0:kc0 + 128], qT[e * 64:(e + 1) * 64, c, tq0:tq0 + 64],
                                    True, True, r=[kT.t(), qT.t()], w=[banks[e].t()])
                    ncol = 256 * len(pair)
                    for e in range(2):
                        pt = P[ub][a][e]
                        self.act(pt[:, 0:ncol], banks[e][:, 0:ncol], AF.Exp, r=[banks[e].t()], w=[pt.t()])
                        for s_, (vt, kc0, tb) in enumerate(pair):
                            if tb is None:
                                continue
                            mtog += 1
                            sl = pt[:, s_ * 256:(s_ + 1) * 256].rearrange("p (c q) -> p c q", c=4)
                            self.tt("dve" if (mtog % 2 or NA_NOPOOL) else "pool", sl, sl, TB5[:, tb, :, e, :], ALU.mult,
                                    r=[pt.t(), TB.t()], w=[pt.t()])
                    if hook:
                        hook(a, npair)

            def pv_heads(ui, heads):
                tq0, kts = units[ui]
                ub = ui % 2
                nk = len(kts)
                po = pO[ui % 2]
                for h in heads:
                    e, c = h % 2, h // 2
                    for ki, (vt, kc0, tb) in enumerate(kts):
                        pt = P[ub][ki // 2][e]
                        s_ = ki % 2
                        self.mm(po[0:65, h * 64:(h + 1) * 64], V[:, vt, h, :],
                                pt[:, s_ * 256 + c * 64: s_ * 256 + c * 64 + 64], ki == 0, ki == nk - 1,
                                r=[V.t(), pt.t()], w=[po.t()])

            def finish(ui):
                tq0, kts = units[ui]
                po = pO[ui % 2]
                ob = osb[(ui // 4) % 2]
                j = ui % 4
                self.act(rrow[64:65, 0:512], po[64:65, 0:512], AF.Ln, r=[po.t()], w=[rrow.t()])
                self.act(rrow[64:65, 0:512], rrow[64:65, 0:512], AF.Exp, scale=-1.0, r=[rrow.t()], w=[rrow.t()])
                self.mm(pb[0:64, 0:512], self.ones_f[64:65, 0:64], rrow[64:65, 0:512], True, True, r=[rrow.t()], w=[pb.t()])
                self.cp("act", pbs[0:64, 0:512], pb[0:64, 0:512], r=[pb.t()], w=[pbs.t()])
                self.tt("dve", ob[:, :, j * 64:(j + 1) * 64], po[0:64, 0:512], pbs[0:64, 0:512], ALU.mult,
                        r=[po.t(), pbs.t()], w=[ob.t(j)])
                if j == 3:
                    t0 = tq0 - 192
                    self.dma("sp", self.onaT.h.ap()[:, :, t0:t0 + 256].rearrange("h d t -> d h t"), ob[:],
                             r=[ob.t()], w=[self.onaT.t()])

            phaseA(0)
            nu = len(units)
            for ui in range(nu):
                if ui + 1 < nu:
                    def hook(a, npair, ui=ui):
                        lo, hi = (8 * a) // npair, (8 * (a + 1)) // npair
                        pv_heads(ui, range(lo, hi))
                        if a == 0 and ui > 0:
                            finish(ui - 1)
                    phaseA(ui + 1, hook)
                else:
                    pv_heads(ui, range(8))
                    if ui > 0:
                        finish(ui - 1)
            finish(nu - 1)

    def gqastage(self, l, need_ctx, lim=None):
        with self.stage():
            kT2 = self.sb("gkT", [128, 2, NT], BF16)
            qT = self.sb("gqTz", [128, 8, NT], BF16)
            V = self.sb("gV", [128, NTILE, 2, 65], BF16)
            self.dma("sp", kT2[:], self.gkT2.h.ap().rearrange("g p t -> p g t"), r=[self.gkT2.t()], w=[kT2.t()])
            for h in range(8):
                e_, c_ = h % 2, h // 2
                self.memset("dve", qT[(1 - e_) * 64:(2 - e_) * 64, h, :], 0.0, w=[qT.t((h, 0))])
                self.dma("sp", qT[e_ * 64:(e_ + 1) * 64, h, :], self.gqT.h.ap()[c_ * 128 + e_ * 64:c_ * 128 + (e_ + 1) * 64, :],
                         r=[self.gqT.t()], w=[qT.t((h, 1))])
            self.memset("dve", V[:, :, :, 64:65], 1.0, w=[V.t("ones")])
            for g in range(2):
                self.dma("sp", V[:, :, g, 0:64], self.gv.h.ap()[:, g * 64:(g + 1) * 64].rearrange("(i p) d -> p i d", p=128),
                         r=[self.gv.t()], w=[V.t("d%d" % g)])
            pS = [self.ps("gps%d" % i, [128, 1024], F32) for i in range(2)]
            pO = [self.ps("gpo%d" % i, [128, 512], F32) for i in range(2)]
            pb = self.ps("gpb", [128, 512], F32)
            Pb = [self.sb("gP%d" % i, [128, 1024], BF16) for i in range(3)]
            rrow = self.sb("grrow", [128, 512], F32)
            pbs = self.sb("gpbs", [64, 512], F32)
            osb = [self.sb("gosb%d" % i, [64, 512], BF16) for i in range(2)]
            blocks = []
            if need_ctx:
                blocks.append((0, 256, [0, 1]))
            for qb in range(8):
                blocks.append((256 + qb * 512, 512, list(range(NTILE))))
            n = 0
            ui = 0
            pend = []
            if lim:
                blocks = blocks[:lim]
            bg = []
            for (dst, srcw, rows) in ((self.wna_b, self.w_out_na, 512), (self.wss_b, self.w_out_ssm, 1024),
                                      (self.wgq_b, self.w_out_gqa, 512), (self.wo_b, self.w_o, 1024),
                                      (self.wup_b, self.ffn_w_up, 1024), (self.wdn_b, self.ffn_w_down, FH)):
                r0 = 0
                while r0 < rows:
                    rr = min(256, rows - r0)
                    bg.append((dst, srcw, r0, rr))
                    r0 += rr

            bgst = [self.sb("gbgst%d" % i, [128, 2, 2 * FH], BF16) for i in range(2)]
            bgn = {"n": 0}

            def bg_issue(k=1):
                for _ in range(k):
                    if bg:
                        dst, srcw, r0, rr = bg.pop(0)
                        st = bgst[bgn["n"] % 2]
                        bgn["n"] += 1
                        nc_ = dst.h.shape[1]
                        self.dma("pool", st[:, 0:rr // 128, 0:nc_],
                                 srcw.ap()[l, r0:r0 + rr, :].rearrange("(a p) n -> p a n", p=128), r=[], w=[st.t()])
                        self.dma("sp", dst.h.ap()[r0:r0 + rr, :].rearrange("(a p) n -> p a n", p=128),
                                 st[:, 0:rr // 128, 0:nc_], r=[st.t()], w=[dst.t(r0)])
            for h in range(8 if not lim else 2):
                g, e, c = h // 4, h % 2, h // 2
                for (tq0, N, kts) in blocks:
                    po = pO[ui % 2]
                    ob = osb[ui % 2]
                    ui += 1
                    bg_issue(1)
                    nk = len(kts)
                    npair = nk // 2
                    LA = 1
                    bufs = {}

                    def qk(i):
                        nonlocal n
                        ps, pt = pS[n % 2], Pb[n % 3]
                        n += 1
                        for a_ in range(2):
                            kt = kts[2 * i + a_]
                            self.mm(ps[:, a_ * 512:a_ * 512 + N], kT2[:, g, kt * 128:(kt + 1) * 128],
                                    qT[:, h, tq0:tq0 + N], True, True, r=[kT2.t(), qT.t()], w=[ps.t(a_)])
                        v2 = lambda ap: ap.rearrange("p (a n) -> p a n", a=2)[:, :, 0:N]
                        self.act(v2(pt[:]), v2(ps[:]), AF.Exp, r=[ps.t()], w=[pt.t()])
                        bufs[i] = pt

                    for i in range(min(LA, npair)):
                        qk(i)
                    for i in range(npair):
                        if i + LA < npair:
                            qk(i + LA)
                        if i == 3 and pend:
                            pend.pop()()
                        pt = bufs.pop(i)
                        for a_ in range(2):
                            self.mm(po[0:65, 0:N], V[:, kts[2 * i + a_], g, :], pt[:, a_ * 512:a_ * 512 + N],
                                    i == 0 and a_ == 0, i == npair - 1 and a_ == 1, r=[V.t(), pt.t()], w=[po.t()])
                    if pend:
                        pend.pop()()

                    def fin(po=po, N=N, ob=ob, h=h, tq0=tq0):
                        self.softmax_finish(po, N, rrow, pb, pbs, ob[:, 0:N], [ob.t()])
                        self.dma("sp", self.ogT.h.ap()[h, :, tq0:tq0 + N], ob[:, 0:N], r=[ob.t()], w=[self.ogT.t()])
                    pend.append(fin)
            if pend:
                pend.pop()()
            bg_issue(len(bg))

    def ssdstage(self, l, need_ctx):
        with contextlib.ExitStack() as outer:
            sbo = lambda name, shape, dt: self.sb(name, shape, dt, es=outer)
            xT = sbo("sxT", [128, NTILE, 1024], BF16)
            BT = sbo("sBT", [128, NT], BF16)
            CT0 = sbo("sCT0", [128, NT], BF16)
            CT1 = sbo("sCT1", [128, NT], BF16)
            Btm = sbo("sBtm", [128, NTILE, 128], BF16)
            dt = sbo("sdt", [128, NTILE, 32], F32)
            da = sbo("sda", [128, NTILE, 32], F32)
            with self.stage():
                cw = self.sb("scw", [128, 10, 3], F32)
                cbias = self.sb("scb", [128, 10], F32)
                for k in range(3):
                    self.dma("sp", cw[:, :, k], self.ssm_conv_w.ap()[l, k].rearrange("(c p) -> p c", p=128),
                             r=[], w=[cw.t(k)])
                self.dma("sp", cbias[:], self.ssm_conv_b.ap()[l].rearrange("(c p) -> p c", p=128), r=[], w=[cbias.t()])
                raws = [self.sb("sraw%d" % i, [128, NT], BF16) for i in range(2)]
                acc = self.sb("sacc", [128, NT], F32)
                xF = self.sb("sxF", [128, NT], BF16)
                pT = [self.ps("spT%d" % i, [128, 8, 128], BF16) for i in range(2)]
                xv = self.xbcT.h.ap().rearrange("(c p) t -> p c t", p=128)
                ntr = 0
                for c in range(10):
                    raw = raws[c % 2]
                    self.dma("sp", raw[:], xv[:, c, :], r=[self.xbcT.t()], w=[raw.t()])
                    self.ts("dve", acc[:], raw[:], cw[:, c, 1:2], cbias[:, c:c + 1], ALU.mult, ALU.add,
                            r=[raw.t(), cw.t(), cbias.t()], w=[acc.t()])
                    for (a, b) in ((0, LC), (LC, NT)):
                        self.stt(acc[:, a + 1:b], raw[:, a:b - 1], cw[:, c, 0:1], acc[:, a + 1:b], ALU.mult, ALU.add,
                                 r=[raw.t(), acc.t()], w=[acc.t()])
                        self.stt(acc[:, a:b - 1], raw[:, a + 1:b], cw[:, c, 2:3], acc[:, a:b - 1], ALU.mult, ALU.add,
                                 r=[raw.t(), acc.t()], w=[acc.t()])
                    if c < 9:
                        dstF = xF if c < 8 else BT
                        self.act(dstF[:], acc[:], AF.Silu, r=[acc.t()], w=[dstF.t()])
                        for (ti0, tc) in [(0, 2)] + [(2 + 4 * g, 4) for g in range(8)]:
                            p = pT[ntr % 2]
                            ntr += 1
                            for j in range(tc):
                                ti = ti0 + j
                                self.tr(p[:, j, :], dstF[:, ti * 128:(ti + 1) * 128], self.ident_b[:], r=[dstF.t()], w=[p.t()])
                            if c < 8:
                                self.cp("act" if ntr % 2 else "dve", xT[:, ti0:ti0 + tc, c * 128:(c + 1) * 128], p[:, 0:tc, :],
                                        r=[p.t()], w=[xT.t((c, ti0))])
                            else:
                                self.cp("dve", Btm[:, ti0:ti0 + tc, :], p[:, 0:tc, :], r=[p.t()], w=[Btm.t(ti0)])
                    else:
                        self.act(CT0[:], acc[:], AF.Silu, r=[acc.t()], w=[CT0.t()])
                        self.cp("dve", CT1[:], CT0[:], r=[CT0.t()], w=[CT1.t()])
                        self.memset("dve", CT0[64:128, :], 0.0, w=[CT0.t()])
                        self.memset("dve", CT1[0:64, :], 0.0, w=[CT1.t()])
                dtb = self.sb("sdtb", [128, 32], F32)
                ab = self.sb("sab", [128, 32], F32)
                self.dma("sp", dtb[:], self.ssm_dt_bias.ap()[l:l + 1].rearrange("o a b -> o (a b)").broadcast_to([128, 32]),
                         r=[], w=[dtb.t()])
                self.dma("sp", ab[:], self.ssm_a_log.ap()[l:l + 1].rearrange("o a b -> o (a b)").broadcast_to([128, 32]),
                         r=[], w=[ab.t()])
                self.act(ab[:], ab[:], AF.Exp, r=[ab.t()], w=[ab.t()])
                self.ts("dve", ab[:], ab[:], -1.0, None, ALU.mult, None, r=[ab.t()], w=[ab.t()])
                self.dma("sp", dt[:], self.dtr.h.ap().rearrange("(i p) n -> p i n", p=128), r=[self.dtr.t()], w=[dt.t()])
                self.tt("dve", dt[:], dt[:], dtb[:].unsqueeze(1).broadcast_to([128, NTILE, 32]), ALU.add,
                        r=[dt.t(), dtb.t()], w=[dt.t()])
                self.act(dt[:], dt[:], AF.Exp, r=[dt.t()], w=[dt.t()])
                self.act(dt[:], dt[:], AF.Ln, bias=self.ones_f[:, 0:1], r=[dt.t()], w=[dt.t()])
                self.tt("dve", da[:], dt[:], ab[:].unsqueeze(1).broadcast_to([128, NTILE, 32]), ALU.mult,
                        r=[dt.t(), ab.t()], w=[da.t()])
            with self.stage():
                U = self.sb("sU", [128, 128], F32)
                Lo = self.sb("sLo", [128, 128], F32)
                Ls = self.sb("sLs", [128, 128], F32)
                Us = self.sb("sUs", [128, 128], F32)
                for (mtx, op, sg) in ((U, ALU.is_ge, -1), (Lo, ALU.is_ge, 1), (Ls, ALU.is_gt, 1), (Us, ALU.is_gt, -1)):
                    self.memset("pool", mtx[:], 1.0, w=[mtx.t()])
                    self.S.add("pool", lambda e, mtx=mtx, op=op, sg=sg: e.affine_select(
                        out=mtx[:], in_=mtx[:], pattern=[[-sg, 128]], compare_op=op, fill=0.0, base=0,
                        channel_multiplier=sg), r=[mtx.t()], w=[mtx.t()])
                dsk = self.sb("sdsk", [128, 16], F32)
                nwT = self.sb("snwT", [128, 8], F32)
                self.dma("sp", dsk[:], self.ssm_d.ap()[l:l + 1, :].broadcast_to([128, 16]), r=[], w=[dsk.t()])
                self.dma("sp", nwT[:], self.ssm_norm_w.ap()[l].rearrange("(k p) -> p k", p=128), r=[], w=[nwT.t()])
                LA = self.sb("sLA", [128, 16, 128], F32)
                expD = self.sb("sexpD", [128, 16, 128], BF16)
                mcb = self.sb("smcb", [128, 2, 128], BF16)
                Mt = [self.sb("sM%d" % i, [128, 16, 128], BF16) for i in range(2)]
                xs = [self.sb("sxs%d" % i, [128, 16, 64], BF16) for i in range(2)]
                xsw = [self.sb("sxsw%d" % i, [128, 16, 64], BF16) for i in range(2)]
                E = self.sb("sE", [128, 16], F32)
                dec = self.sb("sdec", [128, 8], F32)
                H = self.sb("sH", [128, 512], F32)
                Hb = self.sb("sHb", [128, 512], BF16)
                tmpH = self.sb("stmpH", [128, 512], F32)
                yt = self.sb("syt", [128, 1024], F32)
                ys = [self.sb("sy%d" % i, [128, 1024], F32) for i in range(2)]
                yfb = self.sb("syfb", [128, 1024], F32)
                zt = self.sb("szt", [128, 1024], BF16)
                gg = self.sb("sgg", [128, 1024], F32)
                gjunk = self.sb("sgjunk", [128, 1024], BF16)
                ssq = self.sb("sssq", [128, 1], F32)
                ob = self.sb("sob", [128, 1024], BF16)
                ost = self.sb("sost", [128, 8, 512], BF16)
                pD = self.ps("spD", [128, 2048], F32)
                pm = self.ps("spm", [128, 512], F32)
                pm2 = self.ps("spm2", [128, 512], F32)
                pS = self.ps("spS", [128, 512], F32)
                pTo = self.ps("spTo", [128, 8, 128], BF16)
                oT = self.ossmT.h.ap().rearrange("(k p) t -> p k t", p=128)
                decs = [dec, self.sb("sdec2", [128, 8], F32)]
                steps = []
                for d in range(2):
                    order = [0, 1] + list(range(2, NTILE)) if d == 0 else [1, 0] + list(range(NTILE - 1, 1, -1))
                    for oi, ti in enumerate(order):
                        steps.append((d, ti, oi == 0))

                def par(i):
                    d, ti, first = steps[i]
                    A_, Bm_, mask_, wcol = (Ls, U, U, 127) if d == 0 else (Us, Lo, Lo, 0)
                    return dict(d=d, ti=ti, first=first, A_=A_, Bm_=Bm_, mask_=mask_, wcol=wcol,
                                do_y=(need_ctx or ti >= 2), cols=slice(ti * 128, (ti + 1) * 128),
                                dav=da[:, ti, d * 16:(d + 1) * 16], dtv=dt[:, ti, d * 16:(d + 1) * 16],
                                M=Mt[i % 2], xs_=xs[i % 2], xsw_=xsw[i % 2], y=ys[i % 2], dec=decs[i % 2],
                                x3=xT[:, ti, :].rearrange("p (h q) -> p h q", h=16))

                def stepA(i):
                    p = par(i)
                    A_, Bm_, dav, dtv, xs_, xsw_, dec_ = p["A_"], p["Bm_"], p["dav"], p["dtv"], p["xs_"], p["xsw_"], p["dec"]
                    for h in range(16):
                        self.act(LA[:, h, :], A_[:], AF.Identity, scale=dav[:, h:h + 1], bias=self.zero_t[:, 0:1],
                                 r=[A_.t(), da.t()], w=[LA.t(h)])
                    for h in range(16):
                        self.mm(pD[:, h * 128:(h + 1) * 128], LA[:, h, :], Bm_[:], True, True,
                                r=[LA.t(h), Bm_.t()], w=[pD.t(h // 4)])
                    self.mm(pm2[:, 0:16], Bm_[:], dav, True, True, r=[Bm_.t(), da.t()], w=[pm2.t()])
                    self.mm(pm2[:, 16:32], self.ones_f[:], dav, True, True, r=[da.t()], w=[pm2.t()])
                    for q in range(4):
                        self.act(expD[:, 4 * q:4 * q + 4, :], pD[:, q * 512:(q + 1) * 512].rearrange("p (h t) -> p h t", h=4),
                                 AF.Exp, r=[pD.t(q)], w=[expD.t(q)])
                    self.act(E[:], pm2[:, 0:16], AF.Exp, r=[pm2.t()], w=[E.t()])
                    self.act(dec_[0:64, :], pm2[0:64, 16:24], AF.Exp, r=[pm2.t()], w=[dec_.t(0)])
                    self.act(dec_[64:128, :], pm2[64:128, 24:32], AF.Exp, r=[pm2.t()], w=[dec_.t(1)])

                def stepB(i):
                    p = par(i)
                    if p["first"]:
                        self.memset("dve", H[:], 0.0, w=[H.t()])
                        self.memset("dve", Hb[:], 0.0, w=[Hb.t()])
                    self.tt(PENG, p["xs_"][:], p["x3"], p["dtv"].unsqueeze(2).broadcast_to([128, 16, 64]), ALU.mult,
                            r=[xT.t(), dt.t()], w=[p["xs_"].t()])
                    if p["do_y"]:
                        stepB_y(p)
                    self.tt(PENG, p["xsw_"][:], p["xs_"][:], expD[:, :, p["wcol"]].unsqueeze(2).broadcast_to([128, 16, 64]),
                            ALU.mult, r=[p["xs_"].t(), expD.t()], w=[p["xsw_"].t()])

                def stepB_y(p):
                    cols, mask_, M, xs_, y = p["cols"], p["mask_"], p["M"], p["xs_"], p["y"]
                    for g, CTg in enumerate((CT0, CT1)):
                        self.mm(pm[:, g * 128:(g + 1) * 128], BT[:, cols], CTg[:, cols], True, True,
                                r=[BT.t(), CTg.t()], w=[pm.t()])
                    self.tt("dve", mcb[:], pm[:, 0:256].rearrange("p (g t) -> p g t", g=2),
                            mask_[:].unsqueeze(1).broadcast_to([128, 2, 128]), ALU.mult,
                            r=[pm.t(), mask_.t()], w=[mcb.t()])
                    for g in range(2):
                        self.tt("dve", M[:, 8 * g:8 * g + 8, :], expD[:, 8 * g:8 * g + 8, :],
                                mcb[:, g:g + 1, :].broadcast_to([128, 8, 128]), ALU.mult,
                                r=[expD.t(), mcb.t()], w=[M.t(g)])
                    for h in range(16):
                        self.mm(pD[:, h * 64:(h + 1) * 64], M[:, h, :], xs_[:, h, :], True, True,
                                r=[M.t(), xs_.t()], w=[pD.t(h // 8)])
                    for g, CTg in enumerate((CT0, CT1)):
                        self.mm(pD[:, 1024 + g * 512:1024 + (g + 1) * 512], CTg[:, cols], Hb[:], True, True,
                                r=[CTg.t(), Hb.t()], w=[pD.t(2 + g)])
                    for g in range(2):
                        self.tt("dve", yt[:, g * 512:(g + 1) * 512].rearrange("p (h q) -> p h q", h=8),
                                pD[:, 1024 + g * 512:1024 + (g + 1) * 512].rearrange("p (h q) -> p h q", h=8),
                                E[:, 8 * g:8 * g + 8].unsqueeze(2).broadcast_to([128, 8, 64]), ALU.mult,
                                r=[pD.t(2 + g), E.t()], w=[yt.t(g)])
                    self.tt("dve", y[:], yt[:], pD[:, 0:1024], ALU.add, r=[yt.t(), pD.t(0), pD.t(1)], w=[y.t()])

                def stepC(i):
                    p = par(i)
                    ti, xsw_, dec_ = p["ti"], p["xsw_"], p["dec"]
                    for g in range(2):
                        self.mm(pS[g * 64:(g + 1) * 64, :], Btm[:, ti, g * 64:(g + 1) * 64],
                                xsw_[:, 8 * g:8 * g + 8, :].rearrange("p h q -> p (h q)"), True, True,
                                r=[Btm.t(), xsw_.t()], w=[pS.t(g)])
                    self.tt("dve", tmpH[:].rearrange("p (h q) -> p h q", h=8), H[:].rearrange("p (h q) -> p h q", h=8),
                            dec_[:].unsqueeze(2).broadcast_to([128, 8, 64]), ALU.mult, r=[H.t(), dec_.t()], w=[tmpH.t()])
                    self.tt("dve", H[:], tmpH[:], pS[:], ALU.add, r=[tmpH.t(), pS.t()], w=[H.t()])
                    self.cp("act", Hb[:], H[:], r=[H.t()], w=[Hb.t()])

                def stepD(i):
                    p = par(i)
                    if not p["do_y"]:
                        return
                    d, ti, y, x3 = p["d"], p["ti"], p["y"], p["x3"]
                    rows = self.yf.h.ap()[ti * 128:(ti + 1) * 128, :]
                    if d == 0:
                        self.dma("sp", rows, y[:], r=[y.t()], w=[self.yf.t(ti)])
                        return
                    self.dma("sp", yfb[:], rows, r=[self.yf.t(ti)], w=[yfb.t()])
                    self.dma("sp", zt[:], self.zs.h.ap()[ti * 128:(ti + 1) * 128, :], r=[self.zs.t()], w=[zt.t()])
                    self.tt("dve", y[:], y[:], yfb[:], ALU.add, r=[y.t(), yfb.t()], w=[y.t()])
                    self.tt(PENG, gg[:].rearrange("p (h q) -> p h q", h=16), x3,
                            dsk[:].unsqueeze(2).broadcast_to([128, 16, 64]), ALU.mult, r=[xT.t(), dsk.t()], w=[gg.t()])
                    self.tt("dve", y[:], y[:], gg[:], ALU.add, r=[y.t(), gg.t()], w=[y.t()])
                    self.tt("dve", gg[:], y[:], zt[:], ALU.mult, r=[y.t(), zt.t()], w=[gg.t()])
                    self.act(gjunk[:], gg[:], AF.Square, r=[gg.t()], w=[gjunk.t(), ssq.t()], accum_out=ssq[:])
                    self.act(ssq[:], ssq[:], AF.Sqrt, bias=self.eps_t[:, 0:1], scale=1.0 / 1024, r=[ssq.t()], w=[ssq.t()])
                    self.S.add("dve", lambda e: e.reciprocal(out=ssq[:], in_=ssq[:]), r=[ssq.t()], w=[ssq.t()])
                    self.ts("dve", ob[:], gg[:], ssq[:, 0:1], None, ALU.mult, None, r=[gg.t(), ssq.t()], w=[ob.t()])
                    for k in range(8):
                        self.tr(pTo[:, k, :], ob[:, k * 128:(k + 1) * 128], self.ident_b[:], r=[ob.t()], w=[pTo.t()])
                    j = ti if ti < 2 else (ti - 2) % 4
                    self.tt("dve", ost[:, :, j * 128:(j + 1) * 128], pTo[:], nwT[:].unsqueeze(2).broadcast_to([128, 8, 128]),
                            ALU.mult, r=[pTo.t(), nwT.t()], w=[ost.t(j)])
                    if j == 0:
                        t0, n = (0, 256) if ti < 2 else (ti * 128, 512)
                        self.dma("sp", oT[:, :, t0:t0 + n], ost[:, :, 0:n], r=[ost.t()], w=[self.ossmT.t()])

                stepA(0)
                for i in range(len(steps)):
                    stepB(i)
                    if i + 1 < len(steps):
                        stepA(i + 1)
                    stepC(i)
                    stepD(i)

    def mergestage(self, l, need_ctx):
        with self.stage():
            wna = self.sb("mwna", [64, 8, 1024], BF16)
            wgq = self.sb("mwgq", [64, 8, 1024], BF16)
            wss = self.sb("mwss", [128, 8, 1024], BF16)
            wo = self.sb("mwo", [128, 8, 1024], BF16)
            self.dma("sp", wna[:], self.wna_b.h.ap().rearrange("(h d) n -> d h n", d=64), r=[self.wna_b.t()], w=[wna.t()])
            self.dma("sp", wss[:], self.wss_b.h.ap().rearrange("(k p) n -> p k n", p=128), r=[self.wss_b.t()], w=[wss.t()])
            self.dma("sp", wgq[:], self.wgq_b.h.ap().rearrange("(h d) n -> d h n", d=64), r=[self.wgq_b.t()], w=[wgq.t()])
            self.dma("sp", wo[:], self.wo_b.h.ap().rearrange("(k p) n -> p k n", p=128), r=[self.wo_b.t()], w=[wo.t()])
            NB = 512
            ona = [self.sb("mona%d" % i, [64, 8, NB], BF16) for i in range(2)]
            ogq = [self.sb("mogq%d" % i, [64, 8, NB], BF16) for i in range(2)]
            oss = [self.sb("moss%d" % i, [128, 8, NB], BF16) for i in range(2)]
            gts = [self.sb("mgt%d" % i, [128, 24, NB], BF16) for i in range(2)]
            hbs = [self.sb("mhb%d" % i, [128, 8, NB], F32) for i in range(1)] * 2
            yTs = [self.sb("myT%d" % i, [128, 8, NB], BF16) for i in range(1)] * 2
            t1 = [self.sb("mt1_%d" % i, [128, NB], F32) for i in range(1)] * 2
            t2 = [self.sb("mt2_%d" % i, [128, NB], F32) for i in range(1)] * 2
            t3 = [self.sb("mt3_%d" % i, [128, NB], F32) for i in range(1)] * 2
            pA = [self.ps("mpA%d" % i, [128, 512], F32) for i in range(2)]
            pB = [self.ps("mpB%d" % i, [128, 512], F32) for i in range(2)]
            pC = [self.ps("mpC%d" % i, [128, 512], F32) for i in range(2)]
            pM = [self.ps("mpM%d" % i, [128, 512], F32) for i in range(2)]
            hv = self.hcur.h.ap().rearrange("(k p) t -> p k t", p=128)
            blocks = ([(0, LC, 1)] if need_ctx else []) + [(LC + NB * i, NB, 0) for i in range(SEQ // NB)]
            mv = self.modv[l]
            n = 0
            for bi, (t0, N, j) in enumerate(blocks):
                a, g_, s_, gt, hb, yT = ona[bi % 2], ogq[bi % 2], oss[bi % 2], gts[bi % 2], hbs[bi % 2], yTs[bi % 2]
                a, g_, s_, gt, hb, yT = (Sl(a, N), Sl(g_, N), Sl(s_, N), Sl(gt, N), Sl(hb, N), Sl(yT, N))
                self.dma("sp", a[:], self.onaT.h.ap()[:, :, t0:t0 + N].rearrange("h d t -> d h t"), r=[self.onaT.t()], w=[a.t()])
                self.dma("sp", g_[:], self.ogT.h.ap()[:, :, t0:t0 + N].rearrange("h d t -> d h t"), r=[self.ogT.t()], w=[g_.t()])
                self.dma("sp", s_[:], self.ossmT.h.ap()[:, t0:t0 + N].rearrange("(k p) t -> p k t", p=128),
                         r=[self.ossmT.t()], w=[s_.t()])
                self.dma("sp", gt[:], self.gatesT.h.ap()[:, t0:t0 + N].rearrange("(c p) t -> p c t", p=128),
                         r=[self.gatesT.t()], w=[gt.t()])
                self.dma("sp", hb[:], hv[:, :, t0:t0 + N], r=[self.hcur.t()], w=[hb.t()])
                for oc in range(8):
                    n += 1
                    A_, B_, C_ = pA[n % 2], pB[n % 2], pC[n % 2]
                    u1, u2, u3 = Sl(t1[n % 2], N), Sl(t2[n % 2], N), Sl(t3[n % 2], N)
                    cs = slice(oc * 128, (oc + 1) * 128)
                    for h in range(8):
                        self.mm(A_[:, 0:N], wna[:, h, cs], a[:, h, :], h == 0, h == 7, r=[wna.t(), a.t()], w=[A_.t()])
                    for k in range(8):
                        self.mm(B_[:, 0:N], wss[:, k, cs], s_[:, k, :], k == 0, k == 7, r=[wss.t(), s_.t()], w=[B_.t()])
                    for h in range(8):
                        self.mm(C_[:, 0:N], wgq[:, h, cs], g_[:, h, :], h == 0, h == 7, r=[wgq.t(), g_.t()], w=[C_.t()])
                    self.tt("dve", u1[:], A_[:, 0:N], gt[:, oc, :], ALU.mult, r=[A_.t(), gt.t()], w=[u1.t()])
                    self.tt("dve", u2[:], B_[:, 0:N], gt[:, 8 + oc, :], ALU.mult, r=[B_.t(), gt.t()], w=[u2.t()])
                    self.tt("dve", u3[:], C_[:, 0:N], gt[:, 16 + oc, :], ALU.mult, r=[C_.t(), gt.t()], w=[u3.t()])
                    self.tt("dve", u1[:], u1[:], u2[:], ALU.add, r=[u1.t(), u2.t()], w=[u1.t()])
                    self.tt("dve", yT[:, oc, :], u1[:], u3[:], ALU.add, r=[u1.t(), u3.t()], w=[yT.t(oc)])
                for oc in range(8):
                    n += 1
                    M_ = pM[n % 2]
                    cs = slice(oc * 128, (oc + 1) * 128)
                    for k in range(8):
                        self.mm(M_[:, 0:N], wo[:, k, cs], yT[:, k, :], k == 0, k == 7, r=[wo.t(), yT.t()], w=[M_.t()])
                    self.stt(hb[:, oc, :], M_[:, 0:N], mv[:, 16 + oc, j:j + 1], hb[:, oc, :], ALU.mult, ALU.add,
                             r=[M_.t(), hb.t(oc)], w=[hb.t(oc)])
                self.dma("sp", hv[:, :, t0:t0 + N], hb[:], r=[hb.t()], w=[self.hcur.t()])

    def ffnstage(self, l, need_ctx):
        src, dst = self.hcur, (self.hT2 if self.hcur is self.hT else self.hT)
        with self.stage():
            wup = self.sb("fwup", [128, 8, 2 * FH], BF16)
            wdn = self.sb("fwdn", [128, 22, 1024], BF16)
            uv = self.wup_b.h.ap().rearrange("(k p) n -> p k n", p=128)
            for c in (0, 5, 6, 1, 7, 2, 8, 3, 9, 4, 10):
                self.dma("sp", wup[:, :, c * 512:(c + 1) * 512], uv[:, :, c * 512:(c + 1) * 512], r=[self.wup_b.t()], w=[wup.t(c)])
            dv = self.wdn_b.h.ap().rearrange("(c p) n -> p c n", p=128)
            for c in range(2):
                self.dma("sp", wdn[:, c * 11:(c + 1) * 11, :], dv[:, c * 11:(c + 1) * 11, :], r=[self.wdn_b.t()], w=[wdn.t(c)])
            fcw = self.sb("ffcw", [128, 44, 3], F32)
            fcb = self.sb("ffcb", [128, 44], F32)
            for k in range(3):
                self.dma("sp", fcw[:, :, k], self.ffn_conv_w.ap()[l, k].rearrange("(c p) -> p c", p=128), r=[], w=[fcw.t(k)])
            self.dma("sp", fcb[:], self.ffn_conv_b.ap()[l].rearrange("(c p) -> p c", p=128), r=[], w=[fcb.t()])
            NB = 384
            NH = NB + 2
            hbs = [self.sb("fhb%d" % i, [128, 8, NH], F32) for i in range(2)]
            tmp = [self.sb("ftmp%d" % i, [128, NH], F32) for i in range(2)]
            rs = self.sb("frs", [128, NH], F32)
            xns = [self.sb("fxn%d" % i, [128, 8, NH], BF16) for i in range(2)]
            ua = [self.sb("fua%d" % i, [128, NB], F32) for i in range(2)]
            ub = [self.sb("fub%d" % i, [128, NB], F32) for i in range(2)]
            sa = [self.sb("fsa%d" % i, [128, NB], BF16) for i in range(2)]
            t0s = [self.sb("ft0_%d" % i, [128, NB], F32) for i in range(4)]
            actb = self.sb("factb", [128, 22, NB], BF16)

            class SqView:
                name = actb.name

                def t(self_, slot=None):
                    return actb.t()

                def __getitem__(self_, k):
                    v = actb[:].rearrange("p c n -> p (c n)")[:, 0:8 * NH].rearrange("p (k n) -> p k n", k=8)
                    return v[k]
            sq = SqView()
            ssp = self.ps("fssp", [128, 512], F32)
            pa = [self.ps("fpa%d" % i, [128, 512], F32) for i in range(2)]
            pb_ = [self.ps("fpb%d" % i, [128, 512], F32) for i in range(2)]
            pd = [self.ps("fpd%d" % i, [128, 512], F32) for i in range(2)]
            sv = src.h.ap().rearrange("(k p) t -> p k t", p=128)
            dvw = dst.h.ap().rearrange("(k p) t -> p k t", p=128)
            mv = self.modv[l]
            blocks = [(0, LC, 1, 0, LC)] if need_ctx else []
            t_ = LC
            while t_ < NT:
                nb_ = min(NB, NT - t_)
                blocks.append((t_, nb_, 0, LC, NT))
                t_ += nb_
            n = 0

            def prep(bi):
                t0, nb, j, s0, s1 = blocks[bi]
                nh = nb + 2
                hb, xn = hbs[bi % 2], xns[bi % 2]
                lo, hi = max(t0 - 1, s0), min(t0 + nb + 1, s1)
                a0 = lo - (t0 - 1)
                if lo > t0 - 1:
                    self.memset("dve", hb[:, :, 0:1], 0.0, w=[hb.t()])
                if hi < t0 + nb + 1:
                    self.memset("dve", hb[:, :, nh - 1:nh], 0.0, w=[hb.t()])
                self.dma("sp", hb[:, :, a0:a0 + hi - lo], sv[:, :, lo:hi], r=[src.t()], w=[hb.t()])
                self.norm_block(hb, nh, self.A2[l], mv, 24, j, lambda k, xn=xn, nh=nh: xn[:, k, 0:nh], [xn.t()], xn, ssp, rs, tmp)
                if lo > t0 - 1:
                    self.memset("dve", xn[:, :, 0:1], 0.0, w=[xn.t()])
                if hi < t0 + nb + 1:
                    self.memset("dve", xn[:, :, nh - 1:nh], 0.0, w=[xn.t()])

            prep(0)
            for bi, (t0, nb, j, s0, s1) in enumerate(blocks):
                nh = nb + 2
                hb, xn = hbs[bi % 2], xns[bi % 2]
                for c in range(22):
                    n += 1
                    A_, B_ = pa[n % 2], pb_[n % 2]
                    va, vb, vs = ua[n % 2], ub[n % 2], sa[n % 2]
                    for (P_, cc, v) in ((A_, c, va), (B_, 22 + c, vb)):
                        for k in range(8):
                            self.mm(P_[:, 0:nh], wup[:, k, cc * 128:(cc + 1) * 128], xn[:, k, 0:nh], k == 0, k == 7,
                                    r=[wup.t(cc // 4), xn.t()], w=[P_.t()])
                        self.act(v[:, 0:nb], P_[:, 1:nb + 1], AF.Identity, scale=fcw[:, cc, 1:2], bias=fcb[:, cc:cc + 1],
                                 r=[P_.t(), fcw.t(), fcb.t()], w=[v.t()])
                        t0_ = t0s[(2 * n + (cc >= 22)) % 4]
                        self.act(t0_[:, 0:nb], P_[:, 0:nb], AF.Identity, scale=fcw[:, cc, 0:1], bias=self.zero_t[:, 0:1], r=[P_.t(), fcw.t()], w=[t0_.t()])
                        self.tt("dve", v[:, 0:nb], v[:, 0:nb], t0_[:, 0:nb], ALU.add, r=[v.t(), t0_.t()], w=[v.t()])
                        self.stt(v[:, 0:nb], P_[:, 2:nb + 2], fcw[:, cc, 2:3], v[:, 0:nb], ALU.mult, ALU.add, r=[P_.t(), v.t()], w=[v.t()])
                    self.act(vs[:, 0:nb], va[:, 0:nb], AF.Silu, r=[va.t()], w=[vs.t()])
                    self.tt("dve", actb[:, c, 0:nb], vs[:, 0:nb], vb[:, 0:nb], ALU.mult, r=[vs.t(), vb.t()], w=[actb.t()])
                if bi + 1 < len(blocks):
                    prep(bi + 1)
                for oc in range(8):
                    n += 1
                    D_ = pd[n % 2]
                    for c in range(22):
                        self.mm(D_[:, 0:nb], wdn[:, c, oc * 128:(oc + 1) * 128], actb[:, c, 0:nb], c == 0, c == 21,
                                r=[wdn.t(c // 11), actb.t()], w=[D_.t()])
                    self.stt(hb[:, oc, 1:nb + 1], D_[:, 0:nb], mv[:, 40 + oc, j:j + 1], hb[:, oc, 1:nb + 1], ALU.mult, ALU.add,
                             r=[D_.t(), hb.t()], w=[hb.t()])
                self.dma("sp", dvw[:, :, t0:t0 + nb], hb[:, :, 1:nb + 1], r=[hb.t()], w=[dst.t()])
        self.hcur = dst

    def finalstage(self):
        with self.stage():
            fw = self.sb("zfw", [128, 8], F32)
            self.dma("sp", fw[:], self.final_norm_w.ap().rearrange("(k p) -> p k", p=128), r=[], w=[fw.t()])
            NB = 256
            hbs = [self.sb("zhb%d" % i, [128, 8, NB], F32) for i in range(3)]
            sqs = [self.sb("zsq%d" % i, [128, 8, NB], BF16) for i in range(2)]
            rss = [self.sb("zrs%d" % i, [128, NB], F32) for i in range(2)]
            tmps = [self.sb("ztmp%d" % i, [128, 8, NB], F32) for i in range(2)]
            osb = [self.sb("zosb%d" % i, [128, 1024], F32) for i in range(2)]
            ssps = [self.ps("zssp%d" % i, [128, 512], F32) for i in range(2)]
            pT = [self.ps("zpT%d" % i, [128, 1024], F32) for i in range(2)]
            sv = self.hcur.h.ap().rearrange("(k p) t -> p k t", p=128)
            nblk = SEQ // NB
            cnt = {"n": 0}

            def X(bi):
                t0 = LC + bi * NB
                hb, sq, rs, ssp, tmp = hbs[bi % 3], sqs[bi % 2], rss[bi % 2], ssps[bi % 2], tmps[bi % 2]
                self.dma("sp", hb[:], sv[:, :, t0:t0 + NB], r=[self.hcur.t()], w=[hb.t()])
                self.act(sq[:], hb[:], AF.Square, r=[hb.t()], w=[sq.t()])
                for k in range(8):
                    self.mm(ssp[:, 0:NB], self.ones_b[:], sq[:, k, :], k == 0, k == 7, r=[sq.t()], w=[ssp.t()])
                self.act(rs[:], ssp[:, 0:NB], AF.Sqrt, bias=self.eps_t[:, 0:1], scale=1.0 / D, r=[ssp.t()], w=[rs.t()])
                self.S.add("dve", lambda e: e.reciprocal(out=rs[:], in_=rs[:]), r=[rs.t()], w=[rs.t()])
                self.tt("dve", tmp[:], hb[:], rs[:].unsqueeze(1).broadcast_to([128, 8, NB]), ALU.mult, r=[hb.t(), rs.t()], w=[tmp.t()])
                self.tt("dve", tmp[:], tmp[:], fw[:].unsqueeze(2).broadcast_to([128, 8, NB]), ALU.mult, r=[tmp.t(), fw.t()], w=[tmp.t()])

            def Y(bi):
                tmp = tmps[bi % 2]
                for tt_ in range(NB // 128):
                    cnt["n"] += 1
                    n = cnt["n"]
                    p, ob = pT[n % 2], osb[n % 2]
                    for k in range(8):
                        self.tr(p[:, k * 128:(k + 1) * 128], tmp[:, k, tt_ * 128:(tt_ + 1) * 128], self.ident_f[:], r=[tmp.t()], w=[p.t()])
                    self.cp("act" if n % 2 else "dve", ob[:], p[:], r=[p.t()], w=[ob.t()])
                    r0 = bi * NB + tt_ * 128
                    self.dma("sp", self.out.h.ap()[r0:r0 + 128, :], ob[:], r=[ob.t()], w=[self.out.t()])

            X(0)
            for bi in range(nblk):
                if bi + 1 < nblk:
                    X(bi + 1)
                Y(bi)

    def build_all(self):
        self.declare_io()
        self.consts()
        self.prologue()
        self.modstage()
        for l in range(DEPTH):
            need_ctx = l < DEPTH - 1
            with contextlib.ExitStack() as outer:
                self.instage(l, outer)
            self.nastage(l, need_ctx)
            self.gqastage(l, need_ctx)
            self.ssdstage(l, need_ctx)
            self.mergestage(l, need_ctx)
            self.ffnstage(l, need_ctx)
        self.finalstage()

    def dump(self, name, src_ap, shape, dt, r):
        d = self.dram(name, shape, dt, out=True)
        self.dma("sp", d.h.ap(), src_ap, r=r, w=[d.t()])
        return d


NA_TABLES = [(dr, True, True) for dr in range(14)] + [(2, False, True), (10, True, False)]


def host_tables(na_rel_bias):
    L = na_rel_bias.shape[0]
    p = np.arange(128)
    half, kc = p // 64, p % 64
    qc = np.arange(64)
    dcol = np.clip(kc[:, None] - qc[None, :] + 15, 0, 30)
    win0 = np.clip(qc - 8, 0, 48)
    ok = (kc[:, None] >= win0[None, :]) & (kc[:, None] < win0[None, :] + 16)
    bias_g = np.zeros((L, 16, 128, 8, 64), np.float32)
    mask = np.zeros((16, 128, 8, 64), np.float32)
    for tb, (dr, v0, v1) in enumerate(NA_TABLES):
        drow = dr + half
        g = na_rel_bias[:, :, drow[:, None], dcol]
        bias_g[:, tb] = np.transpose(g, (0, 2, 1, 3))
        rv = np.where(half == 0, v0, v1)
        mask[tb] = (ok & rv[:, None])[:, None, :].astype(np.float32)
    n_freq = 16
    inv_freq = (10000.0 ** (-np.arange(n_freq, dtype=np.float32) / n_freq)).astype(np.float32)
    t = np.arange(SEQ)
    row = (t // GW).astype(np.float32)
    col = (t % GW).astype(np.float32)
    ang = np.concatenate([row[:, None] * inv_freq, col[:, None] * inv_freq], axis=-1).astype(np.float32)
    rope = np.stack([np.cos(ang), np.sin(ang)], axis=1).astype(np.float32)
    return bias_g.reshape(L, 16, 128, 512), mask.reshape(16, 128, 512), rope


def make_in_maps(inputs, n_cores=8):
    f = lambda a: np.ascontiguousarray(np.asarray(a, dtype=np.float32))
    bias_g, mask, rope = host_tables(f(inputs["na_rel_bias"]))
    shared = {k: f(v) for k, v in inputs.items() if k not in ("x", "c", "ctx", "na_rel_bias")}
    shared["na_bias_g"] = bias_g
    shared["na_mask"] = mask
    shared["rope"] = rope
    x, c, ctx = f(inputs["x"]), f(inputs["c"]), f(inputs["ctx"])
    maps = []
    for b in range(n_cores):
        m = dict(shared)
        m["x"], m["c"], m["ctx"] = x[b], c[b], ctx[b]
        maps.append(m)
    return maps


_CACHE = {}


def kernel(**inputs):
    n_cores = 8
    if "kb" not in _CACHE:
        kb = KB()
        kb.build_all()
        _CACHE["kb"] = kb
    kb = _CACHE["kb"]
    maps = make_in_maps(inputs, n_cores)
    used = set(kb.din.keys())
    maps = [{k: v for k, v in m.items() if k in used} for m in maps]
    res = run_bass_kernel_spmd(kb.nc, maps, core_ids=list(range(n_cores)))
    return np.stack([np.asarray(r["out"], dtype=np.float32) for r in res.results], axis=0)
```
